# Optimizing a Trainium2 kernel written in Bass

```python
import jax, jax.numpy as jnp
from jax import lax
import numpy as np

D_MODEL = 1024
BATCH = 8
SEQ = 2048
DEPTH = 2

HEAD_DIM = 64
N_Q_HEADS = D_MODEL // 128
N_KV_HEADS = 2
ATTN_Q_W = N_Q_HEADS * HEAD_DIM
ATTN_KV_W = N_KV_HEADS * HEAD_DIM
WINDOW = 128
ATTN_BLOCK = 128
ROPE_THETA = 10000.0
MLSTM_HEADS = 4
MLSTM_HEAD_DIM = D_MODEL // 16
MLSTM_W = MLSTM_HEADS * MLSTM_HEAD_DIM
MLSTM_CHUNK = 128
N_GATE_TYPES = 4
SGU_GROUPS = 4
SGU_GROUP_DIM = D_MODEL // 16
SGU_W = SGU_GROUPS * SGU_GROUP_DIM
SGU_CHUNK = 128
D_MIX = ATTN_Q_W + MLSTM_W + SGU_W
D_IN = ATTN_Q_W + 2 * ATTN_KV_W + 4 * MLSTM_W + N_GATE_TYPES * MLSTM_HEADS + 2 * SGU_W
D_FF = ((8 * D_MODEL // 3 + 127) // 128) * 128
EPS = 1e-6

kernel_name = "hybrid_mlstm_swa_sgu_macaron_encoder"


def rms_norm(x, g):
    xf = x.astype(jnp.float32)
    y = xf * lax.rsqrt(jnp.mean(xf * xf, axis=-1, keepdims=True) + EPS)
    return (y * g.astype(jnp.float32)).astype(x.dtype)


def swiglu(h, w_gate, w_up, w_down):
    return (jax.nn.silu(h @ w_gate) * (h @ w_up)) @ w_down


def rope(x, pos):
    d = x.shape[-1]
    freqs = ROPE_THETA ** (-jnp.arange(0, d, 2, dtype=jnp.float32) / d)
    ang = pos.astype(jnp.float32)[..., None] * freqs
    cos = jnp.cos(ang)[:, :, None, :]
    sin = jnp.sin(ang)[:, :, None, :]
    xf = x.astype(jnp.float32)
    x1, x2 = xf[..., : d // 2], xf[..., d // 2:]
    out = jnp.concatenate([x1 * cos - x2 * sin, x2 * cos + x1 * sin], axis=-1)
    return out.astype(x.dtype)


def window_attention(q, k, v, sink):
    B, S, HQ, d = q.shape
    HKV = k.shape[2]
    G = HQ // HKV
    NB = S // ATTN_BLOCK
    pad = ((0, 0), (ATTN_BLOCK, ATTN_BLOCK), (0, 0), (0, 0))
    kp = jnp.pad(k, pad).reshape(B, NB + 2, ATTN_BLOCK, HKV, d)
    vp = jnp.pad(v, pad).reshape(B, NB + 2, ATTN_BLOCK, HKV, d)
    kw = jnp.concatenate([kp[:, :-2], kp[:, 1:-1], kp[:, 2:]], axis=2)
    vw = jnp.concatenate([vp[:, :-2], vp[:, 1:-1], vp[:, 2:]], axis=2)
    qb = q.reshape(B, NB, ATTN_BLOCK, HKV, G, d)
    s = jnp.einsum('bnqkgd,bnskd->bnkgqs', qb, kw).astype(jnp.float32) * (d ** -0.5)
    blk = jnp.arange(NB)[:, None, None]
    qpos = blk * ATTN_BLOCK + jnp.arange(ATTN_BLOCK)[None, :, None]
    kpos = (blk - 1) * ATTN_BLOCK + jnp.arange(3 * ATTN_BLOCK)[None, None, :]
    valid = (jnp.abs(kpos - qpos) <= WINDOW) & (kpos >= 0) & (kpos < S)
    s = jnp.where(valid[None, :, None, None], s, -jnp.inf)
    sink_b = sink.astype(jnp.float32).reshape(HKV, G)[None, None, :, :, None]
    m = jnp.maximum(jnp.max(s, axis=-1), sink_b)
    p = jnp.exp(s - m[..., None])
    denom = jnp.sum(p, axis=-1) + jnp.exp(sink_b - m)
    o = jnp.einsum('bnkgqs,bnskd->bnqkgd', p, vw.astype(jnp.float32))
    o = o / jnp.moveaxis(denom, -1, 2)[..., None]
    return o.reshape(B, S, HQ * d).astype(q.dtype)


def mlstm_direction(q, k, v, log_i, log_f):
    B, H, S, d = q.shape
    L = MLSTM_CHUNK
    NC = S // L
    qc = q.reshape(B, H, NC, L, d)
    kc = k.reshape(B, H, NC, L, d)
    vc = v.reshape(B, H, NC, L, d)
    li = log_i.reshape(B, H, NC, L)
    lf = log_f.reshape(B, H, NC, L)
    b = jnp.cumsum(lf, axis=-1)
    b_last = b[..., -1]
    lower = jnp.tril(jnp.ones((L, L), dtype=bool))
    dmat = jnp.where(lower, b[..., :, None] - b[..., None, :] + li[..., None, :], -jnp.inf)
    a = b_last[..., None] - b + li
    m_loc = jnp.max(a, axis=-1)
    w = jnp.exp(a - m_loc[..., None])
    c_loc = jnp.einsum('bhcs,bhcsd,bhcse->bhcde', w, vc, kc)
    n_loc = jnp.einsum('bhcs,bhcse->bhce', w, kc)

    def step(carry, xs):
        c, n, m = carry
        cl, nl, ml, bl = xs
        m_new = jnp.maximum(bl + m, ml)
        sa = jnp.exp(bl + m - m_new)
        sb = jnp.exp(ml - m_new)
        c_new = sa[..., None, None] * c + sb[..., None, None] * cl
        n_new = sa[..., None] * n + sb[..., None] * nl
        return (c_new, n_new, m_new), (c, n, m)

    init = (jnp.zeros((B, H, d, d), jnp.float32), jnp.zeros((B, H, d), jnp.float32),
            jnp.zeros((B, H), jnp.float32))
    xs = (jnp.moveaxis(c_loc, 2, 0), jnp.moveaxis(n_loc, 2, 0), jnp.moveaxis(m_loc, 2, 0),
          jnp.moveaxis(b_last, 2, 0))
    _, (c_prev, n_prev, m_prev) = lax.scan(step, init, xs)
    c_prev = jnp.moveaxis(c_prev, 0, 2)
    n_prev = jnp.moveaxis(n_prev, 0, 2)
    m_prev = jnp.moveaxis(m_prev, 0, 2)
    inter = b + m_prev[..., None]
    m_t = jnp.maximum(inter, jnp.max(dmat, axis=-1))
    p = jnp.exp(dmat - m_t[..., None]) * jnp.einsum('bhcld,bhcsd->bhcls', qc, kc)
    sc = jnp.exp(inter - m_t)
    num = sc[..., None] * jnp.einsum('bhcde,bhcle->bhcld', c_prev, qc) + jnp.einsum('bhcls,bhcsd->bhcld', p, vc)
    den = sc * jnp.einsum('bhce,bhcle->bhcl', n_prev, qc) + jnp.sum(p, axis=-1)
    h = num / jnp.maximum(jnp.abs(den), jnp.exp(-m_t))[..., None]
    return h.reshape(B, H, S, d)


def mixer(h, pos, w_in, q_norm_g, k_norm_g, attn_sink, mlstm_gate_b, mlstm_head_g,
          sgu_norm_g, sgu_w_s, sgu_b, w_out):
    B, S, _ = h.shape
    sizes = [ATTN_Q_W, ATTN_KV_W, ATTN_KV_W, MLSTM_W, MLSTM_W, MLSTM_W, MLSTM_W,
             N_GATE_TYPES * MLSTM_HEADS, SGU_W, SGU_W]
    cuts = [int(c) for c in np.cumsum(sizes)[:-1]]
    z = h @ w_in
    aq, ak, av, mq, mk, mv, mo, mg, su, sv = jnp.split(z, cuts, axis=-1)

    aq = rope(rms_norm(aq.reshape(B, S, N_Q_HEADS, HEAD_DIM), q_norm_g), pos)
    ak = rope(rms_norm(ak.reshape(B, S, N_KV_HEADS, HEAD_DIM), k_norm_g), pos)
    av = av.reshape(B, S, N_KV_HEADS, HEAD_DIM)
    y_attn = window_attention(aq, ak, av, attn_sink)

    def heads(t):
        return jnp.transpose(t.reshape(B, S, MLSTM_HEADS, MLSTM_HEAD_DIM), (0, 2, 1, 3)).astype(jnp.float32)
    q_m, k_m, v_m = heads(mq), heads(mk) * (MLSTM_HEAD_DIM ** -0.5), heads(mv)
    g = mg.astype(jnp.float32).reshape(B, S, N_GATE_TYPES, MLSTM_HEADS) + mlstm_gate_b.astype(jnp.float32)
    g = jnp.transpose(g, (2, 0, 3, 1))
    li_f, lf_f = g[0], jax.nn.log_sigmoid(g[1])
    li_b, lf_b = g[2], jax.nn.log_sigmoid(g[3])
    h_fwd = mlstm_direction(q_m, k_m, v_m, li_f, lf_f)
    flip = lambda t: jnp.flip(t, axis=2)
    h_bwd = flip(mlstm_direction(flip(q_m), flip(k_m), flip(v_m), flip(li_b), flip(lf_b)))
    hm = jnp.transpose(h_fwd + h_bwd, (0, 2, 1, 3))
    hm = rms_norm(hm, mlstm_head_g.reshape(MLSTM_HEADS, MLSTM_HEAD_DIM)).reshape(B, S, MLSTM_W)
    y_mlstm = (jax.nn.sigmoid(mo.astype(jnp.float32)) * hm).astype(h.dtype)

    u = jax.nn.gelu(su, approximate=False)
    vg = jax.nn.gelu(sv, approximate=False).reshape(B, S, SGU_GROUPS, SGU_GROUP_DIM)
    vg = rms_norm(vg, sgu_norm_g.reshape(SGU_GROUPS, SGU_GROUP_DIM))
    vg = vg.reshape(B, S // SGU_CHUNK, SGU_CHUNK, SGU_GROUPS, SGU_GROUP_DIM)
    vg = jnp.einsum('gts,bcsgd->bctgd', sgu_w_s, vg) + jnp.transpose(sgu_b)[None, None, :, :, None]
    y_sgu = u * vg.reshape(B, S, SGU_W)

    y = jnp.concatenate([y_attn, y_mlstm, y_sgu.astype(h.dtype)], axis=-1)
    return y @ w_out


def setup_inputs(seed: int = 0) -> dict:
    key = jax.random.key(seed)
    ks = jax.random.split(key, 24)
    f32 = jnp.float32

    def nrm(k, shape, scale):
        return jax.random.normal(k, shape, f32) * scale

    def gain(k, shape):
        return 1.0 + 0.02 * jax.random.normal(k, shape, f32)

    x = jax.random.normal(ks[0], (BATCH, SEQ, D_MODEL), f32)
    offsets = jax.random.randint(ks[1], (BATCH, 1), 0, 4096, dtype=jnp.int32)
    positions = (jnp.arange(SEQ, dtype=jnp.int32)[None, :] + offsets).astype(jnp.int32)
    i_bias = 0.1 * jax.random.normal(ks[2], (DEPTH, 1, MLSTM_HEADS), f32)
    f_bias = jnp.linspace(3.0, 6.0, MLSTM_HEADS, dtype=f32)[None, None, :] + 0.1 * jax.random.normal(ks[3], (DEPTH, 1, MLSTM_HEADS), f32)
    i_bias_b = 0.1 * jax.random.normal(ks[4], (DEPTH, 1, MLSTM_HEADS), f32)
    f_bias_b = jnp.linspace(3.0, 6.0, MLSTM_HEADS, dtype=f32)[None, None, :] + 0.1 * jax.random.normal(ks[5], (DEPTH, 1, MLSTM_HEADS), f32)
    mlstm_gate_b = jnp.concatenate([i_bias, f_bias, i_bias_b, f_bias_b], axis=1)
    return {
        "x": x,
        "positions": positions,
        "norm_ffn1_g": gain(ks[6], (DEPTH, D_MODEL)),
        "ffn1_w_gate": nrm(ks[7], (DEPTH, D_MODEL, D_FF), D_MODEL ** -0.5),
        "ffn1_w_up": nrm(ks[8], (DEPTH, D_MODEL, D_FF), D_MODEL ** -0.5),
        "ffn1_w_down": nrm(ks[9], (DEPTH, D_FF, D_MODEL), D_FF ** -0.5),
        "norm_mix_g": gain(ks[10], (DEPTH, D_MODEL)),
        "w_in": nrm(ks[11], (DEPTH, D_MODEL, D_IN), D_MODEL ** -0.5),
        "q_norm_g": gain(ks[12], (DEPTH, HEAD_DIM)),
        "k_norm_g": gain(ks[13], (DEPTH, HEAD_DIM)),
        "attn_sink": nrm(ks[14], (DEPTH, N_Q_HEADS), 0.5),
        "mlstm_gate_b": mlstm_gate_b,
        "mlstm_head_g": gain(ks[15], (DEPTH, MLSTM_W)),
        "sgu_norm_g": gain(ks[16], (DEPTH, SGU_W)),
        "sgu_w_s": nrm(ks[17], (DEPTH, SGU_GROUPS, SGU_CHUNK, SGU_CHUNK), SGU_CHUNK ** -0.5),
        "sgu_b": 1.0 + 0.1 * jax.random.normal(ks[18], (DEPTH, SGU_GROUPS, SGU_CHUNK), f32),
        "w_out": nrm(ks[19], (DEPTH, D_MIX, D_MODEL), D_MIX ** -0.5),
        "norm_ffn2_g": gain(ks[20], (DEPTH, D_MODEL)),
        "ffn2_w_gate": nrm(ks[21], (DEPTH, D_MODEL, D_FF), D_MODEL ** -0.5),
        "ffn2_w_up": nrm(ks[22], (DEPTH, D_MODEL, D_FF), D_MODEL ** -0.5),
        "ffn2_w_down": nrm(ks[23], (DEPTH, D_FF, D_MODEL), D_FF ** -0.5),
        "norm_out_g": gain(jax.random.fold_in(key, 99), (DEPTH, D_MODEL)),
    }


def reference(x, positions, norm_ffn1_g, ffn1_w_gate, ffn1_w_up, ffn1_w_down, norm_mix_g, w_in,
              q_norm_g, k_norm_g, attn_sink, mlstm_gate_b, mlstm_head_g, sgu_norm_g, sgu_w_s, sgu_b,
              w_out, norm_ffn2_g, ffn2_w_gate, ffn2_w_up, ffn2_w_down, norm_out_g):
    for l in range(DEPTH):
        x = x + 0.5 * swiglu(rms_norm(x, norm_ffn1_g[l]), ffn1_w_gate[l], ffn1_w_up[l], ffn1_w_down[l])
        x = x + mixer(rms_norm(x, norm_mix_g[l]), positions, w_in[l], q_norm_g[l], k_norm_g[l],
                      attn_sink[l], mlstm_gate_b[l], mlstm_head_g[l], sgu_norm_g[l], sgu_w_s[l],
                      sgu_b[l], w_out[l])
        x = x + 0.5 * swiglu(rms_norm(x, norm_ffn2_g[l]), ffn2_w_gate[l], ffn2_w_up[l], ffn2_w_down[l])
        x = rms_norm(x, norm_out_g[l])
    return x
```

```python
import numpy as np
import concourse.bass as bass
import concourse.mybir as mybir
from concourse.bass_utils import run_bass_kernel_spmd

F32 = mybir.dt.float32
BF16 = mybir.dt.bfloat16
I32 = mybir.dt.int32
AF = mybir.ActivationFunctionType
ALU = mybir.AluOpType
AX = mybir.AxisListType

D = 1024
S = 2048
DFF = 2816
NFC = DFF // 128
DIN = 2320
EPS = 1e-6
NCORES = 8
PI = 3.14159265358979
TWO_PI = 2.0 * PI

PT_GN = 0
PT_GQ = 32
PT_GK = 33
PT_SINK = 34
PT_GB = 38
PT_HG = 54
PT_SG = 310
PT_SB = 566
NPT = 822

C_ID = 0
C_TRIF = 128
C_TRIB = 256
C_ONES = 384
C_BD = 512
C_PERM = 640
C_FREQ = 768
C_SIGN = 769
NCT = 770


class Op:
    __slots__ = ("eng", "fn", "deps", "signal", "sigval", "dsem", "dval", "idx")


class _Rec:
    def __getattr__(self, name):
        def f(*a, **k):
            self.call = (name, a, k)
            return self
        return f


class Sched:
    def __init__(self, nc):
        self.nc = nc
        self.E = {"pe": nc.tensor, "act": nc.scalar, "dve": nc.vector, "pool": nc.gpsimd, "sp": nc.sync}
        self.ops = []
        self.lastw = {}
        self.readers = {}
        self.slots = {}
        self.final_waits = []

    def _new(self, eng, fn, eager=True):
        o = Op()
        o.eng = eng
        if eager:
            rec = _Rec()
            fn(rec)
            name, a, k = rec.call
            o.fn = lambda e, name=name, a=a, k=k: getattr(e, name)(*a, **k)
        else:
            o.fn = fn
        o.deps = []
        o.signal = False
        o.sigval = 0
        o.dsem = None
        o.dval = 0
        o.idx = len(self.ops)
        self.ops.append(o)
        return o

    def _track(self, o, r, w):
        cand = {}
        for k in r:
            lw = self.lastw.get(k)
            if lw is not None:
                cand[lw.idx] = (lw, True)
            if isinstance(k, tuple) and k[0] == "ps":
                for rk, rd in self.readers.get(k, {}).items():
                    if rd.eng != o.eng and rd.idx not in cand:
                        cand[rd.idx] = (rd, False)
        for k in w:
            lw = self.lastw.get(k)
            if lw is not None and lw.idx not in cand:
                cand[lw.idx] = (lw, False)
            for rd in self.readers.get(k, {}).values():
                if rd.idx not in cand:
                    cand[rd.idx] = (rd, False)
        for idx in sorted(cand):
            d, raw = cand[idx]
            if d is o:
                continue
            if d.dsem is None and o.dsem is None and d.eng == o.eng:
                if o.eng == "pe":
                    continue
            if d.dsem is None:
                d.signal = True
            o.deps.append(d)
        for k in r:
            rk = o.eng if o.dsem is None else ("dma", o.idx)
            self.readers.setdefault(k, {})[rk] = o
        for k in w:
            self.lastw[k] = o
            self.readers[k] = {}

    def op(self, eng, fn, r=(), w=()):
        o = self._new(eng, fn)
        self._track(o, r, w)
        return o

    def dma(self, eng, out, in_, r, w, slot, final=False):
        o = self._new(eng, lambda e: e.dma_start(out=out, in_=in_), eager=False)
        if slot not in self.slots:
            self.slots[slot] = [self.nc.alloc_semaphore("dq_" + str(len(self.slots))), 0]
        sl = self.slots[slot]
        sl[1] += 16
        o.dsem = sl[0]
        o.dval = sl[1]
        self._track(o, r, w)
        if final:
            self.final_waits.append(o)
        return o

    def adopt(self, new_keys, old_keys):
        ops = {}
        for k in old_keys:
            lw = self.lastw.get(k)
            if lw is not None:
                ops[("w", lw.idx)] = lw
            for rk, rd in self.readers.get(k, {}).items():
                ops[("r", rd.idx)] = rd
        for k in new_keys:
            d = self.readers.setdefault(k, {})
            for o in ops.values():
                key = o.eng if o.dsem is None else ("dma", o.idx)
                if key in d and d[key].idx >= o.idx:
                    continue
                d[key] = o

    def finalize(self):
        nc = self.nc
        sem = {e: nc.alloc_semaphore("eng_" + e) for e in ("pe", "act", "dve", "pool")}
        cnt = {e: 0 for e in sem}
        waited = {}
        for o in self.ops:
            e = self.E[o.eng]
            for d in o.deps:
                if d.dsem is not None:
                    s, v = d.dsem, d.dval
                else:
                    s, v = sem[d.eng], d.sigval
                    assert v > 0
                key = (o.eng, s.num)
                if waited.get(key, 0) >= v:
                    continue
                e.wait_ge(s, v)
                waited[key] = v
            inst = o.fn(e)
            if o.dsem is not None:
                inst.then_inc(o.dsem, 16)
            elif o.signal:
                cnt[o.eng] += 1
                o.sigval = cnt[o.eng]
                inst.then_inc(sem[o.eng], 1)
        for o in self.final_waits:
            key = ("sp", o.dsem.num)
            if waited.get(key, 0) >= o.dval:
                continue
            nc.sync.wait_ge(o.dsem, o.dval)
            waited[key] = o.dval


def fap(base, dims):
    return bass.AP(base.tensor, base.offset, [list(base.ap[0])] + [list(d) for d in dims])


def build_program(layers, n_layers_total, stop_after=None, dumps=None):
    nc = bass.Bass("TRN2", target_bir_lowering=False)
    L = n_layers_total
    SC = Sched(nc)

    def dram(name, shape, dt, out=False):
        return nc.dram_tensor(name, list(shape), dt, kind="ExternalOutput" if out else "ExternalInput").ap()

    d_xT = dram("xT", [D, S], F32)
    d_pos = dram("pos", [1, S], I32)
    d_ct = dram("ctab", [128, NCT], F32)
    d_pt = dram("ptab", [128, L * NPT], F32)
    d_wst = dram("wst", [L, 128, 512], F32)
    d_wg = [dram("wg%d" % k, [L, 11, 128, 2048], F32) for k in range(2)]
    d_wu = [dram("wu%d" % k, [L, 11, 128, 2048], F32) for k in range(2)]
    d_wd = [dram("wd%d" % k, [L, 16, 128, 1408], F32) for k in range(2)]
    d_winF = dram("winF", [L, 11, 128, 1024], F32)
    d_winT0 = dram("winT0", [L, 128, 8 * 144], F32)
    d_winT = dram("winT", [L, 4, 128, 2048], F32)
    d_woA = dram("woA", [L, 4, 128, 1536], F32)
    d_woB = dram("woB", [L, 128, 2048], F32)
    d_out = dram("outT", [D, S], F32, out=True)

    xT = nc.alloc_sbuf_tensor("xT_sb", [128, 8, S], F32)
    ring = nc.alloc_sbuf_tensor("ring", [128, 6, 2048], BF16)
    ropeC = nc.alloc_sbuf_tensor("ropeC", [128, S], BF16)
    ropeS = nc.alloc_sbuf_tensor("ropeS", [128, S], BF16)
    ctab = nc.alloc_sbuf_tensor("ctab_sb", [128, NCT], F32)
    cbf = nc.alloc_sbuf_tensor("cbf", [128, 768], BF16)
    ptab = nc.alloc_sbuf_tensor("ptab_sb", [128, L * NPT], F32)
    wst = nc.alloc_sbuf_tensor("wst_sb", [128, L, 512], BF16)
    NTMP = 8
    tmpt = nc.alloc_sbuf_tensor("tmp", [128, NTMP, 512], F32)
    small = nc.alloc_sbuf_tensor("small", [128, 256], F32)
    AR_BYTES = 81920
    arena = nc.alloc_sbuf_tensor("arena", [128, AR_BYTES // 2], BF16)
    psb = [nc.alloc_psum_tensor("ps%d" % i, [128, 512], F32) for i in range(8)]

    def ar(off_bytes, shape, dt):
        n = int(np.prod(shape[1:]))
        if dt == BF16:
            base = arena[:, off_bytes // 2: off_bytes // 2 + n]
        else:
            base = arena[:, off_bytes // 2: off_bytes // 2 + 2 * n].bitcast(F32)
        dims = []
        st = 1
        for s_ in reversed(shape[1:]):
            dims.insert(0, [st, s_])
            st *= s_
        return fap(base, dims)

    hT = ar(0, [128, 8, S], BF16)
    act = ar(32768, [128, NFC, 1024], BF16)
    uT = ar(32768, [128, 2, S], BF16)
    vgn = ar(40960, [128, 16, 256], BF16)
    ysgT = ar(49152, [128, 2, S], BF16)
    yatT = ar(32768, [128, 4, S], BF16)
    qTa = ar(57344, [128, 4, S], BF16)
    kTa = ar(73728, [128, S], BF16)
    va = ar(77824, [128, 16, 128], BF16)
    qTm = ar(32768, [128, 2, S], BF16)
    kTm = ar(40960, [128, 2, S], BF16)
    km = ar(49152, [128, 16, 256], BF16)
    vaug = ar(57344, [128, 16, 4, 65], BF16)
    GO = ar(65664, [128, 16, 256], BF16)
    SPt = ar(74880, [128, 128], F32)
    Ut = ar(75392, [128, 128], F32)
    Et = ar(75904, [128, 128], F32)
    FLt = ar(76416, [128, 128], F32)
    Atab = ar(76928, [128, 128], F32)
    Sst = ar(77440, [128, 2, 260], F32)
    Sbf = ar(79520, [128, 2, 260], BF16)
    hm = ar(0, [128, 16, 256], F32)
    ymT = ar(16384, [128, 2, S], BF16)
    rowt = ar(24576, [128, 512], F32)
    Qblk = ar(26624, [128, 2, 512], BF16)

    kX = lambda c, bt: ("x", c, bt)
    kH = lambda c, bt: ("h", c, bt)
    ALLH = [kH(c, bt) for c in range(8) for bt in range(4)]
    kPS = lambda b: ("ps", b)

    st = {"ps": 0, "tmp": 0, "ring": 0}

    def ps_next():
        b = st["ps"]
        st["ps"] = (b + 1) % 8
        return b

    def tmp_next():
        i = st["tmp"]
        st["tmp"] = (i + 1) % NTMP
        return tmpt[:, i, :], ("tmp", i)

    def tmp_bf(tile_):
        return tile_.bitcast(BF16)

    def wload(src2d, n):
        i = st["ring"]
        st["ring"] = (i + 1) % 6
        dst = ring[:, i, 0:n]
        SC.dma("pool", dst, src2d, r=[], w=[("ring", i)], slot=("ring", i))
        return ring[:, i, :], ("ring", i)

    SC.dma("sp", ctab[:, :], d_ct, r=[], w=["ctab"], slot="ct")
    SC.dma("sp", ptab[:, :], d_pt, r=[], w=["ptab"], slot="pt")
    SC.dma("pool", cbf[:, :], d_ct[:, 0:768], r=[], w=["cbf"], slot="cbf")
    for l in range(L):
        SC.dma("pool", wst[:, l, :], d_wst[l], r=[], w=["wst"], slot="wst%d" % l)
    for c in range(8):
        for hf in range(2):
            SC.dma("sp", xT[:, c, hf * 1024:(hf + 1) * 1024], d_xT[c * 128:(c + 1) * 128, hf * 1024:(hf + 1) * 1024],
                   r=[], w=[kX(c, 2 * hf), kX(c, 2 * hf + 1)], slot="x%d_%d" % (c, hf))

    ident_f = ctab[:, C_ID:C_ID + 128]
    triF_f = ctab[:, C_TRIF:C_TRIF + 128]
    triB_f = ctab[:, C_TRIB:C_TRIB + 128]
    ones_f = ctab[:, C_ONES:C_ONES + 128]
    ident_b = cbf[:, C_ID:C_ID + 128]
    triF_b = cbf[:, C_TRIF:C_TRIF + 128]
    triB_b = cbf[:, C_TRIB:C_TRIB + 128]
    ones_b = cbf[:, C_ONES:C_ONES + 128]
    bd_b = cbf[:, C_BD:C_BD + 128]
    perm_b = cbf[:, C_PERM:C_PERM + 128]

    def pcol(l, col, n=1):
        return ptab[:, l * NPT + col: l * NPT + col + n]

    def build_rope():
        for pc in range(4):
            sl = slice(pc * 512, (pc + 1) * 512)
            ti, ki = tmp_next()
            posi = ti.bitcast(I32)
            SC.dma("sp", posi, bass.AP(d_pos.tensor, pc * 512, [[0, 128], [1, 512]]),
                   r=[], w=[ki], slot="pos%d" % pc)
            ta, ka = tmp_next()
            SC.op("dve", lambda e, o=ta, i=posi: e.tensor_copy(o, i), r=[ki], w=[ka])
            SC.op("dve", lambda e, o=ta: e.tensor_scalar(o, o, ctab[:, C_FREQ:C_FREQ + 1], None, ALU.mult),
                  r=[ka, "ctab"], w=[ka])
            ty, ky = tmp_next()
            SC.op("dve", lambda e, o=ty, i=ta: e.tensor_scalar(o, i, 1.0 / TWO_PI, None, ALU.mult), r=[ka], w=[ky])
            tk, kk = tmp_next()
            tki = tk.bitcast(I32)
            SC.op("dve", lambda e, o=tki, i=ty: e.tensor_copy(o, i), r=[ky], w=[kk])
            SC.op("dve", lambda e, o=ty, i=tki: e.tensor_copy(o, i), r=[kk], w=[ky])
            C1 = 6.28125
            C2 = TWO_PI - C1
            SC.op("dve", lambda e, o=ta, k_=ty: e.scalar_tensor_tensor(o, k_, -C1, o, ALU.mult, ALU.add),
                  r=[ka, ky], w=[ka])
            SC.op("dve", lambda e, o=ta, k_=ty: e.scalar_tensor_tensor(o, k_, -C2, o, ALU.mult, ALU.add),
                  r=[ka, ky], w=[ka])

            def wrap(t_, kt, m_, km_):
                SC.op("dve", lambda e: e.tensor_single_scalar(m_, t_, PI, ALU.is_gt), r=[kt], w=[km_])
                SC.op("dve", lambda e: e.scalar_tensor_tensor(t_, m_, -TWO_PI, t_, ALU.mult, ALU.add),
                      r=[kt, km_], w=[kt])
                SC.op("dve", lambda e: e.tensor_single_scalar(m_, t_, -PI, ALU.is_lt), r=[kt], w=[km_])
                SC.op("dve", lambda e: e.scalar_tensor_tensor(t_, m_, TWO_PI, t_, ALU.mult, ALU.add),
                      r=[kt, km_], w=[kt])
                SC.op("dve", lambda e: e.tensor_scalar(t_, t_, -3.1415925, 3.1415925, ALU.max, ALU.min),
                      r=[kt], w=[kt])

            wrap(ta, ka, tk, kk)
            SC.op("act", lambda e, i=ta: e.activation(ropeS[:, sl], i, AF.Sin, scale=ctab[:, C_SIGN:C_SIGN + 1]),
                  r=[ka, "ctab"], w=[("ropeS", pc)])
            SC.op("dve", lambda e, o=ty, i=ta: e.tensor_scalar(o, i, PI / 2, None, ALU.add), r=[ka], w=[ky])
            wrap(ty, ky, tk, kk)
            SC.op("act", lambda e, i=ty: e.activation(ropeC[:, sl], i, AF.Sin), r=[ky], w=[("ropeC", pc)])

    def rms_norm(l, which, dst, bts=(0, 1, 2, 3)):
        for bt in bts:
            tsl = slice(bt * 512, (bt + 1) * 512)
            b = ps_next()
            for c in range(8):
                tq, kq = tmp_next()
                sq = tmp_bf(tq)[:, 0:512]
                SC.op("act", lambda e, o=sq, c=c: e.activation(o, xT[:, c, tsl], AF.Square),
                      r=[kX(c, bt)], w=[kq])
                SC.op("pe", lambda e, b=b, i=sq, c=c: e.matmul(psb[b][:, :], ones_b, i, start=(c == 0), stop=(c == 7)),
                      r=[kq, "cbf"], w=[kPS(b)])
            tl, kl = tmp_next()
            SC.op("act", lambda e, o=tl, b=b: e.activation(o, psb[b][:, :], AF.Ln, bias=EPS, scale=1.0 / D),
                  r=[kPS(b)], w=[kl])
            SC.op("act", lambda e, o=tl: e.activation(o, o, AF.Exp, scale=-0.5), r=[kl], w=[kl])
            for c in range(8):
                gcol = pcol(l, PT_GN + which * 8 + c)
                if dst == "h":
                    SC.op("dve", lambda e, c=c, g=gcol, rs=tl: e.scalar_tensor_tensor(
                        hT[:, c, tsl], xT[:, c, tsl], g, rs, ALU.mult, ALU.mult),
                        r=[kX(c, bt), kl, "ptab"], w=[kH(c, bt)])
                else:
                    SC.op("dve", lambda e, c=c, g=gcol, rs=tl: e.scalar_tensor_tensor(
                        xT[:, c, tsl], xT[:, c, tsl], g, rs, ALU.mult, ALU.mult),
                        r=[kX(c, bt), kl, "ptab"], w=[kX(c, bt)])

    def ffn(l, k, hooks=(None, None)):
        kA = lambda f, s_: ("act", f, s_)
        for hf in range(2):
            for fb in range(11):
                if fb == 2 and hooks[hf] is not None:
                    hooks[hf]()
                wg, kwg = wload(d_wg[k][l, fb], 2048)
                wu, kwu = wload(d_wu[k][l, fb], 2048)
                for fc in range(2):
                    f = fb * 2 + fc
                    bg = [ps_next(), ps_next()]
                    bu = [ps_next(), ps_next()]
                    for (wt, kw, bb) in ((wg, kwg, bg), (wu, kwu, bu)):
                        for c in range(8):
                            lhsT = wt[:, c * 256 + fc * 128: c * 256 + fc * 128 + 128]
                            for s_ in range(2):
                                bt = hf * 2 + s_
                                SC.op("pe", lambda e, o=psb[bb[s_]], a=lhsT, c=c, bt=bt: e.matmul(
                                    o[:, :], a, hT[:, c, bt * 512:(bt + 1) * 512], start=(c == 0), stop=(c == 7)),
                                    r=[kw, kH(c, bt)], w=[kPS(bb[s_])])
                    for s_ in range(2):
                        tg, kg = tmp_next()
                        SC.op("act", lambda e, o=tg, b=bg[s_]: e.activation(o, psb[b][:, :], AF.Silu),
                              r=[kPS(bg[s_])], w=[kg])
                        SC.op("dve", lambda e, i=tg, b=bu[s_], f=f, s_=s_: e.tensor_tensor(
                            act[:, f, s_ * 512:(s_ + 1) * 512], i, psb[b][:, :], ALU.mult),
                            r=[kg, kPS(bu[s_])], w=[kA(f, s_)])
            for dc in range(8):
                wd0, kw0 = wload(d_wd[k][l, dc * 2], 1408)
                wd1, kw1 = wload(d_wd[k][l, dc * 2 + 1], 1408)
                bo = [ps_next(), ps_next()]
                for f in range(NFC):
                    wt, kw = (wd0, kw0) if f < 11 else (wd1, kw1)
                    lhsT = wt[:, (f % 11) * 128:(f % 11) * 128 + 128]
                    for s_ in range(2):
                        SC.op("pe", lambda e, o=psb[bo[s_]], a=lhsT, f=f, s_=s_: e.matmul(
                            o[:, :], a, act[:, f, s_ * 512:(s_ + 1) * 512], start=(f == 0), stop=(f == NFC - 1)),
                            r=[kw, kA(f, s_)], w=[kPS(bo[s_])])
                for s_ in range(2):
                    bt = hf * 2 + s_
                    SC.op("dve", lambda e, b=bo[s_], dc=dc, bt=bt: e.scalar_tensor_tensor(
                        xT[:, dc, bt * 512:(bt + 1) * 512], psb[b][:, :], 0.5, xT[:, dc, bt * 512:(bt + 1) * 512],
                        ALU.mult, ALU.add),
                        r=[kPS(bo[s_]), kX(dc, bt)], w=[kX(dc, bt)])

    ACTKEYS = [("act", f, s_) for f in range(NFC) for s_ in range(2)]

    PEND = []

    def flush_pending():
        while PEND:
            PEND.pop(0)()

    def fproj(l, chunk, epilogue):
        wt, kw = wload(d_winF[l, chunk], 1024)
        for bt in range(4):
            b = ps_next()
            for c in range(8):
                SC.op("pe", lambda e, b=b, c=c, bt=bt: e.matmul(
                    psb[b][:, :], wt[:, c * 128:(c + 1) * 128], hT[:, c, bt * 512:(bt + 1) * 512],
                    start=(c == 0), stop=(c == 7)), r=[kw, kH(c, bt)], w=[kPS(b)])
            flush_pending()
            cont = epilogue(bt, b)
            if cont is not None:
                PEND.append(cont)

    def tproj(l, src, ncols, epilogue):
        wt, kw = wload(src, 8 * ncols)
        for tt in range(16):
            b = ps_next()
            for c in range(8):
                SC.op("pe", lambda e, b=b, c=c, tt=tt: e.matmul(
                    psb[b][:, 0:ncols], hT[:, c, tt * 128:(tt + 1) * 128], wt[:, c * ncols:(c + 1) * ncols],
                    start=(c == 0), stop=(c == 7)), r=[kw, kH(c, tt // 4)], w=[kPS(b)])
            flush_pending()
            epilogue(tt, b)

    def wout_pass(l, chunks, ysrc, src_blocks, ndc_per_blk, bt_hook=None):
        nmc = len(chunks)
        for blk in range(8 // ndc_per_blk):
            wt, kw = wload(src_blocks(blk), ndc_per_blk * nmc * 128)
            order = ([(dci, bt) for bt in range(4) for dci in range(ndc_per_blk)] if bt_hook is not None
                     else [(dci, bt) for dci in range(ndc_per_blk) for bt in range(4)])
            for oi, (dci, bt) in enumerate(order):
                dc = blk * ndc_per_blk + dci
                if True:
                    b = ps_next()
                    for i, mc in enumerate(chunks):
                        rhs, kr = ysrc(mc, bt)
                        o0 = (dci * nmc + i) * 128
                        SC.op("pe", lambda e, b=b, o0=o0, rhs=rhs, i=i: e.matmul(
                            psb[b][:, :], wt[:, o0:o0 + 128], rhs, start=(i == 0), stop=(i == nmc - 1)),
                            r=[kw, kr], w=[kPS(b)])
                    SC.op("dve", lambda e, b=b, dc=dc, bt=bt: e.tensor_tensor(
                        xT[:, dc, bt * 512:(bt + 1) * 512], psb[b][:, :], xT[:, dc, bt * 512:(bt + 1) * 512], ALU.add),
                        r=[kPS(b), kX(dc, bt)], w=[kX(dc, bt)])
                    if bt_hook is not None and dci == ndc_per_blk - 1:
                        bt_hook(bt)

    def mixer(l, bt_hook=None):
        kU = lambda gp, bt: ("uT", gp, bt)
        kVG = lambda tt: ("vgn", tt)
        kYS = lambda gp, bt: ("ysg", gp, bt)
        SC.adopt([kU(g_, b_) for g_ in range(2) for b_ in range(4)] + [kVG(t_) for t_ in range(16)]
                 + [kYS(g_, b_) for g_ in range(2) for b_ in range(4)], ACTKEYS)
        kQ = lambda c, bt: ("qTa", c, bt)
        kK = lambda bt: ("kTa", bt)
        kV = lambda tt: ("va", tt)
        SC.adopt([kQ(c, b_) for c in range(4) for b_ in range(4)] + [kK(b_) for b_ in range(4)]
                 + [kV(t_) for t_ in range(16)], ACTKEYS)

        for gp in range(2):
            def ep_u(bt, b, gp=gp):
                SC.op("act", lambda e: e.activation(uT[:, gp, bt * 512:(bt + 1) * 512], psb[b][:, :], AF.Gelu),
                      r=[kPS(b)], w=[kU(gp, bt)])
            fproj(l, 9 + gp, ep_u)

        sgug = pcol(l, PT_SG, 256)

        def ep_sv(tt, b):
            tv, kv_ = tmp_next()
            vg = tv[:, 0:256]
            vsq = tv[:, 256:512]
            SC.op("act", lambda e: e.activation(vg, psb[b][:, 0:256], AF.Gelu), r=[kPS(b)], w=[kv_])
            SC.op("dve", lambda e: e.tensor_tensor(vsq, vg, vg, ALU.mult), r=[kv_], w=[kv_])
            ks_ = ("small", "sgu")
            ms = small[:, 0:4]
            SC.op("dve", lambda e: e.tensor_reduce(ms, fap(vsq, [[64, 4], [1, 64]]), AX.X, ALU.add), r=[kv_], w=[ks_])
            SC.op("act", lambda e: e.activation(ms, ms, AF.Ln, bias=EPS, scale=1.0 / 64), r=[ks_], w=[ks_])
            SC.op("act", lambda e: e.activation(ms, ms, AF.Exp, scale=-0.5), r=[ks_], w=[ks_])
            SC.op("dve", lambda e: e.tensor_tensor(fap(vg, [[64, 4], [1, 64]]), fap(vg, [[64, 4], [1, 64]]),
                                                   fap(ms, [[1, 4], [0, 64]]), ALU.mult), r=[kv_, ks_], w=[kv_])
            SC.op("dve", lambda e: e.tensor_tensor(vgn[:, tt, :], vg, sgug, ALU.mult), r=[kv_, "ptab"], w=[kVG(tt)])
        tproj(l, d_winT[l, 0], 256, ep_sv)

        for gp in range(2):
            for bt in range(4):
                b = ps_next()
                for ci in range(4):
                    tt = bt * 4 + ci
                    for gi in range(2):
                        g = gp * 2 + gi
                        SC.op("pe", lambda e, b=b, ci=ci, tt=tt, gi=gi, g=g: e.matmul(
                            psb[b][gi * 64:(gi + 1) * 64, ci * 128:(ci + 1) * 128],
                            vgn[:, tt, g * 64:(g + 1) * 64], wst[:, l, g * 128:(g + 1) * 128],
                            start=True, stop=True), r=[kVG(tt), "wst"], w=[kPS(b)])
                tb, kb = tmp_next()
                biasT = pcol(l, PT_SB + gp * 128, 128)
                SC.op("dve", lambda e, b=b, tb=tb, biasT=biasT: e.tensor_tensor(
                    fap(tb, [[128, 4], [1, 128]]), fap(psb[b][:, :], [[128, 4], [1, 128]]),
                    fap(biasT, [[0, 4], [1, 128]]), ALU.add), r=[kPS(b), "ptab"], w=[kb])
                SC.op("dve", lambda e, tb=tb, gp=gp, bt=bt: e.tensor_tensor(
                    ysgT[:, gp, bt * 512:(bt + 1) * 512], tb, uT[:, gp, bt * 512:(bt + 1) * 512], ALU.mult),
                    r=[kb, kU(gp, bt)], w=[kYS(gp, bt)])
        if stop_after == "sgu":
            return

        kYA = lambda c, bt: ("yat", c, bt)
        SC.adopt([kYA(c, b_) for c in range(4) for b_ in range(4)],
                 [kU(g_, b_) for g_ in range(2) for b_ in range(4)] + [kVG(t_) for t_ in range(16)])

        def qk_epilogue(dst_fn, kdst_fn, gcol):
            def ep(bt, b):
                tsl = slice(bt * 512, (bt + 1) * 512)
                tq, kq = tmp_next()
                ts, ks = tmp_next()
                sqb = tmp_bf(ts)[:, 0:512]
                qgb = tmp_bf(ts)[:, 512:1024]
                SC.op("act", lambda e: e.activation(sqb, psb[b][:, :], AF.Square), r=[kPS(b)], w=[ks])
                SC.op("act", lambda e: e.activation(tq, psb[b][:, :], AF.Copy, scale=gcol), r=[kPS(b), "ptab"], w=[kq])
                SC.op("dve", lambda e: e.tensor_copy(qgb, tq), r=[kq], w=[ks])

                def part2():
                    bm = ps_next()
                    SC.op("pe", lambda e: e.matmul(psb[bm][:, :], bd_b, sqb, start=True, stop=True),
                          r=[ks, "cbf"], w=[kPS(bm)])
                    bp = ps_next()
                    SC.op("pe", lambda e: e.matmul(psb[bp][:, :], perm_b, qgb, start=True, stop=True),
                          r=[ks, "cbf"], w=[kPS(bp)])
                    tr, kr = tmp_next()
                    SC.op("act", lambda e: e.activation(tr, psb[bm][:, :], AF.Ln, bias=EPS, scale=1.0 / 64),
                          r=[kPS(bm)], w=[kr])
                    SC.op("act", lambda e: e.activation(tr, tr, AF.Exp, scale=-0.5), r=[kr], w=[kr])
                    t2, k2 = tmp_next()
                    SC.op("dve", lambda e: e.tensor_tensor(t2, psb[bp][:, :], ropeS[:, tsl], ALU.mult),
                          r=[kPS(bp), ("ropeS", bt)], w=[k2])
                    SC.op("dve", lambda e: e.tensor_tensor(tq, tq, ropeC[:, tsl], ALU.mult), r=[kq, ("ropeC", bt)], w=[kq])
                    SC.op("dve", lambda e: e.tensor_tensor(tq, tq, t2, ALU.add), r=[kq, k2], w=[kq])
                    SC.op("dve", lambda e: e.tensor_tensor(dst_fn(bt), tq, tr, ALU.mult), r=[kq, kr], w=[kdst_fn(bt)])
                return part2
            return ep

        for c in range(4):
            fproj(l, c, qk_epilogue(lambda bt, c=c: qTa[:, c, bt * 512:(bt + 1) * 512], lambda bt, c=c: kQ(c, bt),
                                    pcol(l, PT_GQ)))
        fproj(l, 4, qk_epilogue(lambda bt: kTa[:, bt * 512:(bt + 1) * 512], lambda bt: kK(bt), pcol(l, PT_GK)))

        kG = lambda tt: ("G", tt)
        gbias = pcol(l, PT_GB, 16)

        def ep_t0(tt, b):
            SC.op("act", lambda e: e.activation(va[:, tt, :], psb[b][:, 0:128], AF.Copy), r=[kPS(b)], w=[kV(tt)])
            SC.op("dve", lambda e: e.tensor_tensor(Gs[:, tt, :], psb[b][:, 128:144], gbias, ALU.add),
                  r=[kPS(b), "ptab"], w=[kG(tt)])
        tproj(l, d_winT0[l], 144, ep_t0)

        flush_pending()
        esink = small[:, 8:12]
        SC.op("act", lambda e: e.activation(esink, pcol(l, PT_SINK, 4), AF.Exp), r=["ptab"], w=[("small", "esink")])

        aunits = [(n, hp) for n in range(16) for hp in range(2)]
        AC = [dict() for _ in aunits]
        OD = {}

        def stS(ui):
            n, hp = aunits[ui]
            prs = slice(hp * 64, (hp + 1) * 64)
            js = [j for j in (n - 1, n, n + 1) if 0 <= j < 16]
            ptiles = []
            for j in js:
                bs = ps_next()
                rhs = fap(qTa[prs, 0, n * 128:(n + 1) * 128], [[S, 4], [1, 128]])
                SC.op("pe", lambda e: e.matmul(
                    psb[bs][:, :], kTa[prs, j * 128:(j + 1) * 128], rhs, start=True, stop=True),
                    r=[kK(j // 4)] + [kQ(c, n // 4) for c in range(4)], w=[kPS(bs)])
                tp, kp = tmp_next()
                pt = tmp_bf(tp)[:, 0:512]
                SC.op("act", lambda e: e.activation(pt, psb[bs][:, :], AF.Exp, scale=0.125),
                      r=[kPS(bs)], w=[kp])
                if j != n:
                    msk = triB_b if j == n - 1 else triF_b
                    SC.op("dve", lambda e: e.tensor_tensor(
                        fap(pt, [[128, 4], [1, 128]]), fap(pt, [[128, 4], [1, 128]]),
                        fap(msk, [[0, 4], [1, 128]]), ALU.mult), r=[kp, "cbf"], w=[kp])
                ptiles.append((j, pt, kp))
            AC[ui]["pt"] = ptiles

        def stP(ui):
            n, hp = aunits[ui]
            prs = slice(hp * 64, (hp + 1) * 64)
            if hp == 0:
                OD[n] = (ps_next(), ps_next())
            bO, bD = OD[n]
            ptiles = AC[ui]["pt"]
            for i, (j, pt, kp) in enumerate(ptiles):
                SC.op("pe", lambda e: e.matmul(
                    psb[bO][prs, :], va[:, j, hp * 64:(hp + 1) * 64], pt,
                    start=(i == 0), stop=(i == len(ptiles) - 1)), r=[kV(j), kp], w=[kPS(bO)])
                SC.op("pe", lambda e: e.matmul(
                    psb[bD][prs, :], ones_b[:, 0:64], pt,
                    start=(i == 0), stop=(i == len(ptiles) - 1)), r=["cbf", kp], w=[kPS(bD)])
            if hp == 1:
                td, kd = tmp_next()
                SC.op("dve", lambda e: e.tensor_tensor(
                    fap(td, [[128, 4], [1, 128]]), fap(psb[bD][:, :], [[128, 4], [1, 128]]),
                    fap(esink, [[1, 4], [0, 128]]), ALU.add), r=[kPS(bD), ("small", "esink")], w=[kd])
                SC.op("dve", lambda e: e.reciprocal(td, td), r=[kd], w=[kd])
                SC.op("dve", lambda e: e.tensor_tensor(
                    fap(yatT[:, 0, n * 128:(n + 1) * 128], [[S, 4], [1, 128]]),
                    fap(psb[bO][:, :], [[128, 4], [1, 128]]), fap(td, [[128, 4], [1, 128]]), ALU.mult),
                    r=[kPS(bO), kd], w=[kYA(c, n // 4) for c in range(4)])

        for step in range(len(aunits) + 1):
            if step < len(aunits):
                stS(step)
            if step >= 1:
                stP(step - 1)
        if stop_after == "attn":
            return

        def ysrc1(mc, bt):
            if mc < 4:
                return yatT[:, mc, bt * 512:(bt + 1) * 512], kYA(mc, bt)
            return ysgT[:, mc - 6, bt * 512:(bt + 1) * 512], kYS(mc - 6, bt)
        wout_pass(l, [0, 1, 2, 3, 6, 7], ysrc1, lambda blk: d_woA[l, blk], 2)
        if stop_after == "wout1":
            return

        old = ([kYA(c, b_) for c in range(4) for b_ in range(4)] + [kYS(g_, b_) for g_ in range(2) for b_ in range(4)]
               + [kQ(c, b_) for c in range(4) for b_ in range(4)] + [kK(b_) for b_ in range(4)]
               + [kV(t_) for t_ in range(16)])
        kQM = lambda ch, bt: ("qTm", ch, bt)
        kKM = lambda ch, bt: ("kTm", ch, bt)
        kKT = lambda tt: ("km", tt)
        kVA = lambda tt: ("vaug", tt)
        kGO = lambda tt: ("GO", tt)
        newk = ([kQM(ch, b_) for ch in range(2) for b_ in range(4)] + [kKM(ch, b_) for ch in range(2) for b_ in range(4)]
                + [kKT(t_) for t_ in range(16)] + [kVA(t_) for t_ in range(16)] + [kGO(t_) for t_ in range(16)]
                + ["mlmisc", ("Sst", 0), ("Sst", 1), ("Sbf", 0), ("Sbf", 1)])
        SC.adopt(newk, old)

        for ch in range(2):
            def ep_q(bt, b, ch=ch):
                SC.op("act", lambda e: e.activation(qTm[:, ch, bt * 512:(bt + 1) * 512], psb[b][:, :], AF.Copy),
                      r=[kPS(b)], w=[kQM(ch, bt)])
            fproj(l, 5 + ch, ep_q)
        for ch in range(2):
            def ep_k(bt, b, ch=ch):
                SC.op("act", lambda e: e.activation(kTm[:, ch, bt * 512:(bt + 1) * 512], psb[b][:, :], AF.Copy,
                                                    scale=0.125), r=[kPS(b)], w=[kKM(ch, bt)])
            fproj(l, 7 + ch, ep_k)

        def ep_mk(tt, b):
            SC.op("act", lambda e: e.activation(km[:, tt, :], psb[b][:, 0:256], AF.Copy, scale=0.125),
                  r=[kPS(b)], w=[kKT(tt)])
        tproj(l, d_winT[l, 1], 256, ep_mk)

        def ep_mv(tt, b):
            SC.op("dve", lambda e: e.tensor_copy(fap(vaug[:, tt, 0, 0:64], [[65, 4], [1, 64]]),
                                                 fap(psb[b][:, 0:256], [[64, 4], [1, 64]])), r=[kPS(b)], w=[kVA(tt)])
            SC.op("dve", lambda e: e.memset(fap(vaug[:, tt, 0, 64:65], [[65, 4], [1, 1]]), 1.0), r=[], w=[kVA(tt)])
        tproj(l, d_winT[l, 2], 256, ep_mv)

        hg = pcol(l, PT_HG, 256)

        def ep_mo(tt, b):
            tm_, km_ = tmp_next()
            t = tm_[:, 0:256]
            SC.op("act", lambda e: e.activation(t, psb[b][:, 0:256], AF.Exp, scale=-1.0), r=[kPS(b)], w=[km_])
            SC.op("dve", lambda e: e.tensor_scalar(t, t, 1.0, None, ALU.add), r=[km_], w=[km_])
            SC.op("dve", lambda e: e.reciprocal(t, t), r=[km_], w=[km_])
            SC.op("dve", lambda e: e.tensor_tensor(GO[:, tt, :], t, hg, ALU.mult), r=[km_, "ptab"], w=[kGO(tt)])
        tproj(l, d_winT[l, 3], 256, ep_mo)
        kHM = lambda tt: ("hm", tt)
        kYM = lambda ch, bt: ("ymT", ch, bt)
        SC.adopt([kHM(t_) for t_ in range(16)] + [kYM(ch, b_) for ch in range(2) for b_ in range(4)] + ["rowt", ("qblk", 0), ("qblk", 1)], ALLH)

        KM = "mlmisc"
        def gview(types):
            t0, t1 = types
            return fap(Gs[:, 0, t0 * 4:t0 * 4 + 4], [[(t1 - t0) * 4, 2], [16, 16], [1, 4]])
        SPv = fap(SPt, [[64, 2], [4, 16], [1, 4]])
        Uv = fap(Ut, [[64, 2], [4, 16], [1, 4]])
        GK = [kG(t_) for t_ in range(16)]
        SC.op("act", lambda e: e.activation(SPv, gview((1, 3)), AF.Exp, scale=-1.0), r=GK, w=[KM])
        SC.op("act", lambda e: e.activation(SPt, SPt, AF.Ln, bias=1.0), r=[KM], w=[KM])
        bB = ps_next()
        SC.op("pe", lambda e: e.matmul(psb[bB][:, 0:64], triF_f, SPt[:, 0:64], start=True, stop=True),
              r=[KM, "ctab"], w=[kPS(bB)])
        SC.op("pe", lambda e: e.matmul(psb[bB][:, 64:128], triB_f, SPt[:, 64:128], start=True, stop=True),
              r=[KM, "ctab"], w=[kPS(bB)])
        SC.op("pe", lambda e: e.matmul(psb[bB][0:1, 128:256], ones_f[:, 0:1], SPt, start=True, stop=True),
              r=[KM, "ctab"], w=[kPS(bB)])
        SC.op("dve", lambda e: e.tensor_tensor(Uv, gview((0, 2)), fap(psb[bB][:, 0:128], [[64, 2], [4, 16], [1, 4]]),
                                               ALU.add), r=GK + [kPS(bB)], w=[KM])
        SC.op("act", lambda e: e.activation(FLt, psb[bB][:, 0:128], AF.Copy), r=[kPS(bB)], w=[KM])
        rows = small[0:1, 16:16 + 128]
        SC.op("act", lambda e: e.activation(rows, psb[bB][0:1, 128:256], AF.Copy), r=[kPS(bB)], w=[("small", "rows")])
        bT = ps_next()
        SC.op("pe", lambda e: e.transpose(psb[bT][:, 0:128], Ut, ident_f), r=[KM, "ctab"], w=[kPS(bT)])
        ucol = small[:, 12:13]
        SC.op("dve", lambda e: e.tensor_reduce(ucol, psb[bT][:, 0:128], AX.X, ALU.max), r=[kPS(bT)], w=[("small", "ucol")])
        SC.op("pe", lambda e: e.matmul(psb[bT][0:1, 128:256], ucol, ident_f, start=True, stop=True),
              r=[("small", "ucol"), "ctab"], w=[kPS(bT)])
        umax = rowt[0:1, 0:128]
        mprev = rowt[0:1, 128:256]
        mcr = rowt[0:1, 256:384]
        alr = rowt[0:1, 384:512]
        KR = "rowt"
        SC.op("act", lambda e: e.activation(umax, psb[bT][0:1, 128:256], AF.Copy), r=[kPS(bT)], w=[KR])
        SC.op("dve", lambda e: e.memset(mprev, 0.0), r=[], w=[KR])
        for dr in range(2):
            for h in range(4):
                if dr == 0:
                    sel = lambda t_: fap(t_[:, h:h + 1], [[4, 16]])
                else:
                    sel = lambda t_: fap(t_[:, 64 + 60 + h:64 + 60 + h + 1], [[-4, 16]])
                SC.op("dve", lambda e: e.tensor_tensor_scan(sel(mcr), sel(umax), sel(rows), 0.0, ALU.max, ALU.subtract),
                      r=[KR, ("small", "rows")], w=[KR])
        SC.op("dve", lambda e: e.tensor_copy(mprev[:, 4:64], mcr[:, 0:60]), r=[KR], w=[KR])
        SC.op("dve", lambda e: e.tensor_copy(mprev[:, 64:124], mcr[:, 68:128]), r=[KR], w=[KR])
        SC.op("dve", lambda e: e.tensor_tensor(mcr, mcr, rows, ALU.add), r=[KR, ("small", "rows")], w=[KR])
        SC.op("dve", lambda e: e.tensor_tensor(alr, mprev, mcr, ALU.subtract), r=[KR], w=[KR])
        SC.op("act", lambda e: e.activation(alr, alr, AF.Exp), r=[KR], w=[KR])
        bC = ps_next()
        SC.op("pe", lambda e: e.matmul(psb[bC][:, 0:256], ones_f[0:1, :], rowt[0:1, 256:512], start=True, stop=True),
              r=[KR, "ctab"], w=[kPS(bC)])
        SC.op("dve", lambda e: e.tensor_tensor(Et, Ut, psb[bC][:, 0:128], ALU.subtract), r=[KM, kPS(bC)], w=[KM])
        SC.op("act", lambda e: e.activation(Et, Et, AF.Exp), r=[KM], w=[KM])
        SC.op("dve", lambda e: e.tensor_tensor(FLt, FLt, psb[bC][:, 0:128], ALU.subtract), r=[KM, kPS(bC)], w=[KM])
        SC.op("act", lambda e: e.activation(FLt, FLt, AF.Exp), r=[KM], w=[KM])
        SC.op("dve", lambda e: e.memset(Atab, 0.0), r=[], w=[KM])
        for hp in range(2):
            prs = slice(hp * 64, (hp + 1) * 64)
            SC.op("act", lambda e, prs=prs, hp=hp: e.activation(
                fap(Atab[prs, hp:hp + 1], [[64, 2], [4, 16], [2, 2]]),
                fap(psb[bC][prs, 128 + hp:129 + hp], [[64, 2], [4, 16], [2, 2]]), AF.Copy), r=[kPS(bC)], w=[KM])
        SC.op("dve", lambda e: e.memset(Sst, 0.0), r=[], w=[("Sst", 0), ("Sst", 1)])

        SC.op("dve", lambda e: e.memset(Qblk, 0.0), r=[], w=[("qblk", 0), ("qblk", 1)])
        units = []
        for i in range(16):
            units += [(0, i), (1, 15 - i)]
        UC = [dict() for _ in units]

        def stA(ui):
            dr, c = units[ui]
            cx = UC[ui]
            qb = ui % 2
            kqb = ("qblk", qb)
            for hp in range(2):
                prs = slice(hp * 64, (hp + 1) * 64)
                SC.op("act", lambda e: e.activation(
                    fap(Qblk[prs, qb, hp * 128:hp * 128 + 1], [[256, 2], [1, 128]]),
                    fap(qTm[prs, 0, c * 128:c * 128 + 1], [[S, 2], [1, 128]]), AF.Copy),
                    r=[kQM(0, c // 4), kQM(1, c // 4)], w=[kqb])
            bq = ps_next()
            cx["bq"] = bq
            for ch in range(2):
                SC.op("pe", lambda e: e.matmul(
                    psb[bq][:, ch * 256:(ch + 1) * 256], kTm[:, ch, c * 128:(c + 1) * 128],
                    Qblk[:, qb, ch * 256:(ch + 1) * 256], start=True, stop=True),
                    r=[kKM(ch, c // 4), kqb], w=[kPS(bq)])

        def stB(ui):
            dr, c = units[ui]
            cx = UC[ui]
            col = dr * 64 + c * 4
            msk = triF_b if dr == 0 else triB_b
            tp, kp = tmp_next()
            pT = tmp_bf(tp)[:, 0:512]
            bq = cx["bq"]
            SC.op("dve", lambda e: e.tensor_tensor(
                fap(pT, [[128, 4], [1, 128]]), fap(psb[bq][:, :], [[128, 4], [1, 128]]),
                fap(msk, [[0, 4], [1, 128]]), ALU.mult), r=[kPS(bq), "cbf"], w=[kp])
            tv, kv_ = tmp_next()
            vwf = tmp_bf(tv)[:, 0:260]
            SC.op("dve", lambda e: e.tensor_tensor(fap(vwf, [[65, 4], [1, 65]]), vaug[:, c, :, :],
                                                   fap(Et[:, col:col + 4], [[1, 4], [0, 65]]),
                                                   ALU.mult), r=[kVA(c), KM], w=[kv_])
            cx.update(pT=pT, kp=kp, vwf=vwf, kv=kv_)

        def stC(ui):
            dr, c = units[ui]
            cx = UC[ui]
            col = dr * 64 + c * 4
            pT, kp, vwf, kv_ = cx["pT"], cx["kp"], cx["vwf"], cx["kv"]
            Sd = Sst[:, dr, :]
            Sb = Sbf[:, dr, :]
            SC.op("dve", lambda e: e.tensor_tensor(fap(Sd, [[65, 4], [1, 65]]), fap(Sd, [[65, 4], [1, 65]]),
                                                   fap(Atab[:, col:col + 4], [[1, 4], [0, 65]]),
                                                   ALU.mult), r=[("Sst", dr), KM], w=[("Sst", dr)])
            SC.op("act", lambda e: e.activation(Sb, Sd, AF.Copy), r=[("Sst", dr)], w=[("Sbf", dr)])
            bn = ps_next()
            for ch in range(2):
                SC.op("pe", lambda e: e.matmul(
                    psb[bn][:, ch * 130:(ch + 1) * 130], qTm[:, ch, c * 128:(c + 1) * 128], Sbf[:, dr, ch * 130:(ch + 1) * 130],
                    start=True, stop=False), r=[kQM(ch, c // 4), ("Sbf", dr)], w=[kPS(bn)])
                for hp in range(2):
                    h = 2 * ch + hp
                    SC.op("pe", lambda e: e.matmul(
                        psb[bn][:, h * 65:(h + 1) * 65], pT[:, h * 128:(h + 1) * 128], vwf[:, h * 65:(h + 1) * 65],
                        start=False, stop=(hp == 1)), r=[kp, kv_], w=[kPS(bn)])
            bs = ps_next()
            for ch in range(2):
                SC.op("pe", lambda e: e.matmul(
                    psb[bs][:, ch * 130:(ch + 1) * 130], km[:, c, ch * 128:(ch + 1) * 128], vwf[:, ch * 130:(ch + 1) * 130],
                    start=True, stop=True), r=[kKT(c), kv_], w=[kPS(bs)])
            cx.update(bn=bn, bs=bs)

        def stD(ui):
            dr, c = units[ui]
            cx = UC[ui]
            col = dr * 64 + c * 4
            bn, bs = cx["bn"], cx["bs"]
            Sd = Sst[:, dr, :]
            SC.op("dve", lambda e: e.tensor_tensor(Sd, Sd, psb[bs][:, 0:260], ALU.add),
                  r=[("Sst", dr), kPS(bs)], w=[("Sst", dr)])
            kd_ = ("small", "den", dr)
            den = small[:, 144 + dr * 4: 148 + dr * 4]
            SC.op("act", lambda e: e.activation(den, fap(psb[bn][:, 64:65], [[65, 4]]), AF.Abs), r=[kPS(bn)], w=[kd_])
            SC.op("dve", lambda e: e.tensor_tensor(den, den, FLt[:, col:col + 4], ALU.max), r=[kd_, KM], w=[kd_])
            SC.op("dve", lambda e: e.reciprocal(den, den), r=[kd_], w=[kd_])
            first = (dr == 0 and c < 8) or (dr == 1 and c >= 8)
            hmv = fap(hm[:, c, :], [[64, 4], [1, 64]])
            nv = fap(psb[bn][:, 0:64], [[65, 4], [1, 64]])
            rb = fap(den, [[1, 4], [0, 64]])
            if first:
                SC.op("dve", lambda e: e.tensor_tensor(hmv, nv, rb, ALU.mult), r=[kPS(bn), kd_], w=[kHM(c)])
            else:
                t3, k3 = tmp_next()
                t3v = fap(t3[:, 0:256], [[64, 4], [1, 64]])
                SC.op("dve", lambda e: e.tensor_tensor(t3v, nv, rb, ALU.mult), r=[kPS(bn), kd_], w=[k3])
                SC.op("dve", lambda e: e.tensor_tensor(hm[:, c, :], hm[:, c, :], t3[:, 0:256], ALU.add),
                      r=[k3, kHM(c)], w=[kHM(c)])
                finish(c)

        def finish(c):
            t1, k1 = tmp_next()
            sq = t1[:, 0:256]
            SC.op("act", lambda e: e.activation(sq, hm[:, c, :], AF.Square), r=[kHM(c)], w=[k1])
            kf_ = ("small", "fin")
            ms = small[:, 152:156]
            SC.op("dve", lambda e: e.tensor_reduce(ms, fap(sq, [[64, 4], [1, 64]]), AX.X, ALU.add), r=[k1], w=[kf_])
            SC.op("act", lambda e: e.activation(ms, ms, AF.Ln, bias=EPS, scale=1.0 / 64), r=[kf_], w=[kf_])
            SC.op("act", lambda e: e.activation(ms, ms, AF.Exp, scale=-0.5), r=[kf_], w=[kf_])
            yv = t1[:, 256:512]
            SC.op("dve", lambda e: e.tensor_tensor(fap(yv, [[64, 4], [1, 64]]), fap(hm[:, c, :], [[64, 4], [1, 64]]),
                                                   fap(ms, [[1, 4], [0, 64]]), ALU.mult), r=[kHM(c), kf_, k1], w=[k1])
            t2, k2 = tmp_next()
            yb = tmp_bf(t2)[:, 0:256]
            SC.op("dve", lambda e: e.tensor_tensor(yb, yv, GO[:, c, :], ALU.mult), r=[k1, kGO(c)], w=[k2])
            bt_ = ps_next()
            pbf = psb[bt_][:, :].bitcast(BF16)
            for ch in range(2):
                SC.op("pe", lambda e, ch=ch: e.transpose(pbf[:, ch * 128:(ch + 1) * 128], yb[:, ch * 128:(ch + 1) * 128],
                                                         ident_b), r=[k2, "cbf"], w=[kPS(bt_)])
            SC.op("act", lambda e: e.activation(fap(ymT[:, 0, c * 128:(c + 1) * 128], [[S, 2], [1, 128]]),
                                                fap(pbf[:, 0:256], [[128, 2], [1, 128]]), AF.Copy),
                  r=[kPS(bt_)], w=[kYM(0, c // 4), kYM(1, c // 4)])

        nU = len(units)
        for step in range(nU + 3):
            if step < nU:
                stA(step)
            if 0 <= step - 1 < nU:
                stB(step - 1)
            if 0 <= step - 2 < nU:
                stC(step - 2)
            if 0 <= step - 3 < nU:
                stD(step - 3)
        if stop_after == "mlstm":
            return
        wout_pass(l, [4, 5], lambda mc, bt: (ymT[:, mc - 4, bt * 512:(bt + 1) * 512], kYM(mc - 4, bt)),
                  lambda blk: d_woB[l], 8, bt_hook=bt_hook)
        SC.adopt(ALLH, [kHM(t_) for t_ in range(16)] + [kYM(ch, b_) for ch in range(2) for b_ in range(4)] + ["rowt", ("qblk", 0), ("qblk", 1)])
        SC.adopt(ACTKEYS, newk)

    Gs = nc.alloc_sbuf_tensor("Gs", [128, 16, 16], F32)

    build_rope()
    if stop_after is not None:
        for l in layers:
            rms_norm(l, 0, "h")
            ffn(l, 0)
            if stop_after == "ffn1":
                break
            rms_norm(l, 1, "h")
            mixer(l)
            if stop_after != "ffn2":
                break
            rms_norm(l, 2, "h")
            ffn(l, 1)
            break
    else:
        rms_norm(layers[0], 0, "h", (0, 1))
        carry = [lambda l0=layers[0]: rms_norm(l0, 0, "h", (2, 3))]
        for li, l in enumerate(layers):
            last = li == len(layers) - 1
            hookA = carry[0]
            ffn(l, 0, hooks=(hookA, lambda l=l: rms_norm(l, 1, "h", (0, 1))))
            rms_norm(l, 1, "h", (2, 3))
            mixer(l)
            rms_norm(l, 2, "h", (0, 1))

            def hookB(l=l, last=last, li=li):
                rms_norm(l, 3, "x", (0, 1))
                if not last:
                    rms_norm(layers[li + 1], 0, "h", (0, 1))
            ffn(l, 1, hooks=(lambda l=l: rms_norm(l, 2, "h", (2, 3)), hookB))
            if last:
                rms_norm(l, 3, "x", (2, 3))
            else:
                def nxt(l=l, li=li):
                    rms_norm(l, 3, "x", (2, 3))
                    rms_norm(layers[li + 1], 0, "h", (2, 3))
                carry[0] = nxt

    dump_aps = {}
    if dumps:
        avail = {"yatT": (yatT, [128, 4, S], BF16), "ysgT": (ysgT, [128, 2, S], BF16), "ymT": (ymT, [128, 2, S], BF16),
                 "hm": (hm, [128, 16, 256], F32), "qTa": (qTa, [128, 4, S], BF16), "kTa": (kTa, [128, S], BF16),
                 "hT": (hT, [128, 8, S], BF16), "Et": (Et, [128, 128], F32), "FLt": (FLt, [128, 128], F32),
                 "Ut": (Ut, [128, 128], F32), "Atab": (Atab, [128, 128], F32), "ropeC": (ropeC[:, :], [128, S], BF16),
                 "ropeS": (ropeS[:, :], [128, S], BF16), "va": (va, [128, 16, 128], BF16), "Gs": (Gs[:, :, :], [128, 16, 16], F32),
                 "uT": (uT, [128, 2, S], BF16), "vgn": (vgn, [128, 16, 256], BF16)}
        for nm in dumps:
            apv, shp, dt = avail[nm]
            dd = dram("dump_" + nm, shp, dt, out=True)
            allkeys = list(SC.lastw.keys())
            SC.dma("sp", dd, apv, r=allkeys, w=[], slot="dump_" + nm, final=True)

    for c in range(8):
        for hf in range(2):
            SC.dma("sp", d_out[c * 128:(c + 1) * 128, hf * 1024:(hf + 1) * 1024], xT[:, c, hf * 1024:(hf + 1) * 1024],
                   r=[kX(c, 2 * hf), kX(c, 2 * hf + 1)], w=[], slot="o%d_%d" % (c, hf), final=True)
    SC.finalize()
    return nc


def _consts():
    ct = np.zeros((128, NCT), np.float32)
    i = np.arange(128)
    ct[:, C_ID:C_ID + 128] = np.eye(128, dtype=np.float32)
    ct[:, C_TRIF:C_TRIF + 128] = (i[:, None] <= i[None, :]).astype(np.float32)
    ct[:, C_TRIB:C_TRIB + 128] = (i[:, None] >= i[None, :]).astype(np.float32)
    ct[:, C_ONES:C_ONES + 128] = 1.0
    ct[:, C_BD:C_BD + 128] = ((i[:, None] // 64) == (i[None, :] // 64)).astype(np.float32)
    partner = np.where((i % 64) < 32, i + 32, i - 32)
    perm = np.zeros((128, 128), np.float32)
    perm[partner, i] = 1.0
    ct[:, C_PERM:C_PERM + 128] = perm
    j = (i % 64) % 32
    ct[:, C_FREQ] = (10000.0 ** (-(2.0 * j.astype(np.float32)) / np.float32(64.0))).astype(np.float32)
    ct[:, C_SIGN] = np.where((i % 64) < 32, -1.0, 1.0)
    return ct


def _freqs_like_reference():
    d = 64
    ar = np.arange(0, d, 2, dtype=np.float32)
    return (np.float32(10000.0) ** (-ar / np.float32(d))).astype(np.float32)


def _prep_shared(inp):
    L = inp["w_in"].shape[0]
    f32 = np.float32
    ct = _consts()
    fr = _freqs_like_reference()
    i = np.arange(128)
    ct[:, C_FREQ] = fr[(i % 64) % 32]
    pt = np.zeros((128, L * NPT), f32)
    names = ["norm_ffn1_g", "norm_mix_g", "norm_ffn2_g", "norm_out_g"]
    for l in range(L):
        o = l * NPT
        for k, nm in enumerate(names):
            pt[:, o + PT_GN + k * 8: o + PT_GN + (k + 1) * 8] = inp[nm][l].reshape(8, 128).T
        pt[:, o + PT_GQ] = inp["q_norm_g"][l][i % 64]
        pt[:, o + PT_GK] = inp["k_norm_g"][l][i % 64]
        sk = inp["attn_sink"][l]
        for c in range(4):
            pt[:64, o + PT_SINK + c] = sk[c]
            pt[64:, o + PT_SINK + c] = sk[4 + c]
        pt[:, o + PT_GB:o + PT_GB + 16] = inp["mlstm_gate_b"][l].reshape(1, 16)
        pt[:, o + PT_HG:o + PT_HG + 256] = inp["mlstm_head_g"][l][None, :]
        pt[:, o + PT_SG:o + PT_SG + 256] = inp["sgu_norm_g"][l][None, :]
        sb = inp["sgu_b"][l]
        for gp in range(2):
            pt[:64, o + PT_SB + gp * 128: o + PT_SB + (gp + 1) * 128] = sb[2 * gp][None, :]
            pt[64:, o + PT_SB + gp * 128: o + PT_SB + (gp + 1) * 128] = sb[2 * gp + 1][None, :]
    wst = np.ascontiguousarray(np.transpose(inp["sgu_w_s"], (0, 3, 1, 2)).reshape(L, 128, 512))

    def kblocks(W, nb):
        Lk, K, N = W.shape
        return np.ascontiguousarray(
            W.reshape(Lk, K // 128, 128, N // nb, nb).transpose(0, 3, 2, 1, 4).reshape(Lk, N // nb, 128, (K // 128) * nb))

    out = {"ctab": ct, "ptab": pt, "wst": wst}
    for k, pre in enumerate(("ffn1", "ffn2")):
        out["wg%d" % k] = kblocks(inp[pre + "_w_gate"], 256)
        out["wu%d" % k] = kblocks(inp[pre + "_w_up"], 256)
        wd = inp[pre + "_w_down"]
        wdr = wd.reshape(L, 2, 11, 128, 8, 128).transpose(0, 4, 1, 3, 2, 5)
        out["wd%d" % k] = np.ascontiguousarray(wdr.reshape(L, 16, 128, 1408))
    w_in = inp["w_in"]
    colsF = []
    for c in range(4):
        colsF.append(np.concatenate([np.arange(c * 64, c * 64 + 64), np.arange((4 + c) * 64, (4 + c) * 64 + 64)]))
    colsF.append(np.arange(512, 640))
    for ch in range(2):
        colsF.append(np.arange(768 + ch * 128, 768 + ch * 128 + 128))
    for ch in range(2):
        colsF.append(np.arange(1024 + ch * 128, 1024 + ch * 128 + 128))
    for gp in range(2):
        colsF.append(np.arange(1808 + gp * 128, 1808 + gp * 128 + 128))
    winF = np.stack([w_in[:, :, cols] for cols in colsF], axis=1)
    out["winF"] = np.ascontiguousarray(
        winF.reshape(L, 11, 8, 128, 128).transpose(0, 1, 3, 2, 4).reshape(L, 11, 128, 1024))
    cols0 = np.concatenate([np.arange(640, 768), np.arange(1792, 1808)])
    w0 = w_in[:, :, cols0]
    out["winT0"] = np.ascontiguousarray(w0.reshape(L, 8, 128, 144).transpose(0, 2, 1, 3).reshape(L, 128, 8 * 144))
    tb = [np.arange(2064, 2320), np.arange(1024, 1280), np.arange(1280, 1536), np.arange(1536, 1792)]
    wT = np.stack([w_in[:, :, cols] for cols in tb], axis=1)
    out["winT"] = np.ascontiguousarray(wT.reshape(L, 4, 8, 128, 256).transpose(0, 1, 3, 2, 4).reshape(L, 4, 128, 2048))
    w_out = inp["w_out"]
    p = np.arange(128)
    rows = []
    for c in range(4):
        rows.append(np.where(p < 64, c * 64 + p, (4 + c) * 64 + (p - 64)))
    for ch in range(2):
        rows.append(512 + ch * 128 + p)
    for gp in range(2):
        rows.append(768 + gp * 128 + p)
    woP = np.stack([w_out[:, r, :] for r in rows], axis=1)
    woP = woP.reshape(L, 8, 128, 8, 128)
    a = woP[:, [0, 1, 2, 3, 6, 7]]
    a = a.transpose(0, 3, 2, 1, 4)
    out["woA"] = np.ascontiguousarray(a.reshape(L, 4, 2, 128, 6, 128).transpose(0, 1, 3, 2, 4, 5).reshape(L, 4, 128, 1536))
    b = woP[:, [4, 5]].transpose(0, 2, 3, 1, 4)
    out["woB"] = np.ascontiguousarray(b.reshape(L, 128, 2048))
    return {k: np.ascontiguousarray(v, dtype=np.float32) for k, v in out.items()}


_CACHE = {}


def _get_program(key, layers, L, **kw):
    if key not in _CACHE:
        _CACHE[key] = build_program(layers, L, **kw)
    return _CACHE[key]


FUSED = True


def kernel(**inputs):
    inp = {k: np.asarray(v) for k, v in inputs.items()}
    L = inp["w_in"].shape[0]
    shared = _prep_shared(inp)
    x = inp["x"].astype(np.float32, copy=False)
    pos = inp["positions"].astype(np.int32, copy=False)
    xT = [np.ascontiguousarray(x[b].T) for b in range(NCORES)]
    if FUSED:
        nc = _get_program("fused", list(range(L)), L)
        in_maps = []
        for b in range(NCORES):
            m = dict(shared)
            m["xT"] = xT[b]
            m["pos"] = np.ascontiguousarray(pos[b:b + 1])
            in_maps.append(m)
        res = run_bass_kernel_spmd(nc, in_maps, core_ids=list(range(NCORES)))
        outs = [res.results[b]["outT"] for b in range(NCORES)]
    else:
        cur = xT
        for l in range(L):
            nc = _get_program("layer%d" % l, [l], L)
            in_maps = []
            for b in range(NCORES):
                m = dict(shared)
                m["xT"] = cur[b]
                m["pos"] = np.ascontiguousarray(pos[b:b + 1])
                in_maps.append(m)
            res = run_bass_kernel_spmd(nc, in_maps, core_ids=list(range(NCORES)))
            cur = [np.ascontiguousarray(res.results[b]["outT"]) for b in range(NCORES)]
        outs = cur
    return np.stack([np.ascontiguousarray(o.T) for o in outs], axis=0).astype(np.float32)
```

```python
import numpy as np
import concourse.bass as bass
import concourse.mybir as mybir
from concourse.bass_utils import run_bass_kernel_spmd

F32 = mybir.dt.float32
BF16 = mybir.dt.bfloat16
I32 = mybir.dt.int32
AF = mybir.ActivationFunctionType
ALU = mybir.AluOpType
AX = mybir.AxisListType

D = 1024
S = 2048
DFF = 2816
NFC = DFF // 128
DIN = 2320
EPS = 1e-6
NCORES = 8
PI = 3.14159265358979
TWO_PI = 2.0 * PI

PT_GN = 0
PT_GQ = 32
PT_GK = 33
PT_SINK = 34
PT_GB = 38
PT_HG = 54
PT_SG = 310
PT_SB = 566
NPT = 822

C_ID = 0
C_TRIF = 128
C_TRIB = 256
C_ONES = 384
C_BD = 512
C_PERM = 640
C_FREQ = 768
C_SIGN = 769
NCT = 770


class Op:
    __slots__ = ("eng", "fn", "deps", "signal", "sigval", "dsem", "dval", "idx")


class _Rec:
    def __getattr__(self, name):
        def f(*a, **k):
            self.call = (name, a, k)
            return self
        return f


class Sched:
    def __init__(self, nc):
        self.nc = nc
        self.E = {"pe": nc.tensor, "act": nc.scalar, "dve": nc.vector, "pool": nc.gpsimd, "sp": nc.sync}
        self.ops = []
        self.lastw = {}
        self.readers = {}
        self.slots = {}
        self.final_waits = []

    def _new(self, eng, fn, eager=True):
        o = Op()
        o.eng = eng
        if eager:
            rec = _Rec()
            fn(rec)
            name, a, k = rec.call
            o.fn = lambda e, name=name, a=a, k=k: getattr(e, name)(*a, **k)
        else:
            o.fn = fn
        o.deps = []
        o.signal = False
        o.sigval = 0
        o.dsem = None
        o.dval = 0
        o.idx = len(self.ops)
        self.ops.append(o)
        return o

    def _track(self, o, r, w):
        cand = {}
        for k in r:
            lw = self.lastw.get(k)
            if lw is not None:
                cand[lw.idx] = (lw, True)
            if isinstance(k, tuple) and k[0] == "ps":
                for rk, rd in self.readers.get(k, {}).items():
                    if rd.eng != o.eng and rd.idx not in cand:
                        cand[rd.idx] = (rd, False)
        for k in w:
            lw = self.lastw.get(k)
            if lw is not None and lw.idx not in cand:
                cand[lw.idx] = (lw, False)
            for rd in self.readers.get(k, {}).values():
                if rd.idx not in cand:
                    cand[rd.idx] = (rd, False)
        for idx in sorted(cand):
            d, raw = cand[idx]
            if d is o:
                continue
            if d.dsem is None and o.dsem is None and d.eng == o.eng:
                if o.eng == "pe":
                    continue
            if d.dsem is None:
                d.signal = True
            o.deps.append(d)
        for k in r:
            rk = o.eng if o.dsem is None else ("dma", o.idx)
            self.readers.setdefault(k, {})[rk] = o
        for k in w:
            self.lastw[k] = o
            self.readers[k] = {}

    def op(self, eng, fn, r=(), w=()):
        o = self._new(eng, fn)
        self._track(o, r, w)
        return o

    def dma(self, eng, out, in_, r, w, slot, final=False):
        o = self._new(eng, lambda e: e.dma_start(out=out, in_=in_), eager=False)
        if slot not in self.slots:
            self.slots[slot] = [self.nc.alloc_semaphore("dq_" + str(len(self.slots))), 0]
        sl = self.slots[slot]
        sl[1] += 16
        o.dsem = sl[0]
        o.dval = sl[1]
        self._track(o, r, w)
        if final:
            self.final_waits.append(o)
        return o

    def adopt(self, new_keys, old_keys):
        ops = {}
        for k in old_keys:
            lw = self.lastw.get(k)
            if lw is not None:
                ops[("w", lw.idx)] = lw
            for rk, rd in self.readers.get(k, {}).items():
                ops[("r", rd.idx)] = rd
        for k in new_keys:
            d = self.readers.setdefault(k, {})
            for o in ops.values():
                key = o.eng if o.dsem is None else ("dma", o.idx)
                if key in d and d[key].idx >= o.idx:
                    continue
                d[key] = o

    def finalize(self):
        nc = self.nc
        sem = {e: nc.alloc_semaphore("eng_" + e) for e in ("pe", "act", "dve", "pool")}
        cnt = {e: 0 for e in sem}
        waited = {}
        for o in self.ops:
            e = self.E[o.eng]
            for d in o.deps:
                if d.dsem is not None:
                    s, v = d.dsem, d.dval
                else:
                    s, v = sem[d.eng], d.sigval
                    assert v > 0
                key = (o.eng, s.num)
                if waited.get(key, 0) >= v:
                    continue
                e.wait_ge(s, v)
                waited[key] = v
            inst = o.fn(e)
            if o.dsem is not None:
                inst.then_inc(o.dsem, 16)
            elif o.signal:
                cnt[o.eng] += 1
                o.sigval = cnt[o.eng]
                inst.then_inc(sem[o.eng], 1)
        for o in self.final_waits:
            key = ("sp", o.dsem.num)
            if waited.get(key, 0) >= o.dval:
                continue
            nc.sync.wait_ge(o.dsem, o.dval)
            waited[key] = o.dval


def fap(base, dims):
    return bass.AP(base.tensor, base.offset, [list(base.ap[0])] + [list(d) for d in dims])


def build_program(layers, n_layers_total, stop_after=None, dumps=None):
    nc = bass.Bass("TRN2", target_bir_lowering=False)
    L = n_layers_total
    SC = Sched(nc)

    def dram(name, shape, dt, out=False):
        return nc.dram_tensor(name, list(shape), dt, kind="ExternalOutput" if out else "ExternalInput").ap()

    d_xT = dram("xT", [D, S], F32)
    d_pos = dram("pos", [1, S], I32)
    d_ct = dram("ctab", [128, NCT], F32)
    d_pt = dram("ptab", [128, L * NPT], F32)
    d_wst = dram("wst", [L, 128, 512], F32)
    d_wg = [dram("wg%d" % k, [L, 11, 128, 2048], F32) for k in range(2)]
    d_wu = [dram("wu%d" % k, [L, 11, 128, 2048], F32) for k in range(2)]
    d_wd = [dram("wd%d" % k, [L, 16, 128, 1408], F32) for k in range(2)]
    d_winF = dram("winF", [L, 11, 128, 1024], F32)
    d_winT0 = dram("winT0", [L, 128, 8 * 144], F32)
    d_winT = dram("winT", [L, 4, 128, 2048], F32)
    d_woA = dram("woA", [L, 4, 128, 1536], F32)
    d_woB = dram("woB", [L, 128, 2048], F32)
    d_out = dram("outT", [D, S], F32, out=True)

    xT = nc.alloc_sbuf_tensor("xT_sb", [128, 8, S], F32)
    ring = nc.alloc_sbuf_tensor("ring", [128, 6, 2048], BF16)
    ropeC = nc.alloc_sbuf_tensor("ropeC", [128, S], BF16)
    ropeS = nc.alloc_sbuf_tensor("ropeS", [128, S], BF16)
    ctab = nc.alloc_sbuf_tensor("ctab_sb", [128, NCT], F32)
    cbf = nc.alloc_sbuf_tensor("cbf", [128, 768], BF16)
    ptab = nc.alloc_sbuf_tensor("ptab_sb", [128, L * NPT], F32)
    wst = nc.alloc_sbuf_tensor("wst_sb", [128, L, 512], BF16)
    NTMP = 8
    tmpt = nc.alloc_sbuf_tensor("tmp", [128, NTMP, 512], F32)
    small = nc.alloc_sbuf_tensor("small", [128, 256], F32)
    AR_BYTES = 81920
    arena = nc.alloc_sbuf_tensor("arena", [128, AR_BYTES // 2], BF16)
    psb = [nc.alloc_psum_tensor("ps%d" % i, [128, 512], F32) for i in range(8)]

    def ar(off_bytes, shape, dt):
        n = int(np.prod(shape[1:]))
        if dt == BF16:
            base = arena[:, off_bytes // 2: off_bytes // 2 + n]
        else:
            base = arena[:, off_bytes // 2: off_bytes // 2 + 2 * n].bitcast(F32)
        dims = []
        st = 1
        for s_ in reversed(shape[1:]):
            dims.insert(0, [st, s_])
            st *= s_
        return fap(base, dims)

    hT = ar(0, [128, 8, S], BF16)
    act = ar(32768, [128, NFC, 1024], BF16)
    uT = ar(32768, [128, 2, S], BF16)
    vgn = ar(40960, [128, 16, 256], BF16)
    ysgT = ar(49152, [128, 2, S], BF16)
    yatT = ar(32768, [128, 4, S], BF16)
    qTa = ar(57344, [128, 4, S], BF16)
    kTa = ar(73728, [128, S], BF16)
    va = ar(77824, [128, 16, 128], BF16)
    qTm = ar(32768, [128, 2, S], BF16)
    kTm = ar(40960, [128, 2, S], BF16)
    km = ar(49152, [128, 16, 256], BF16)
    vaug = ar(57344, [128, 16, 4, 65], BF16)
    GO = ar(65664, [128, 16, 256], BF16)
    SPt = ar(74880, [128, 128], F32)
    Ut = ar(75392, [128, 128], F32)
    Et = ar(75904, [128, 128], F32)
    FLt = ar(76416, [128, 128], F32)
    Atab = ar(76928, [128, 128], F32)
    Sst = ar(77440, [128, 2, 260], F32)
    Sbf = ar(79520, [128, 2, 260], BF16)
    hm = ar(0, [128, 16, 256], F32)
    ymT = ar(16384, [128, 2, S], BF16)
    rowt = ar(24576, [128, 512], F32)
    Qblk = ar(26624, [128, 2, 512], BF16)

    kX = lambda c, bt: ("x", c, bt)
    kH = lambda c, bt: ("h", c, bt)
    ALLH = [kH(c, bt) for c in range(8) for bt in range(4)]
    kPS = lambda b: ("ps", b)

    st = {"ps": 0, "tmp": 0, "ring": 0}

    def ps_next():
        b = st["ps"]
        st["ps"] = (b + 1) % 8
        return b

    def tmp_next():
        i = st["tmp"]
        st["tmp"] = (i + 1) % NTMP
        return tmpt[:, i, :], ("tmp", i)

    def tmp_bf(tile_):
        return tile_.bitcast(BF16)

    PRE = {}

    def _wissue(src2d, n):
        i = st["ring"]
        st["ring"] = (i + 1) % 6
        dst = ring[:, i, 0:n]
        SC.dma("pool", dst, src2d, r=[], w=[("ring", i)], slot=("ring", i))
        return ring[:, i, :], ("ring", i)

    def prefetch(tag, src2d, n):
        PRE[tag] = _wissue(src2d, n)

    def wload(src2d, n, tag=None):
        if tag is not None and tag in PRE:
            return PRE.pop(tag)
        return _wissue(src2d, n)

    SC.dma("sp", ctab[:, :], d_ct, r=[], w=["ctab"], slot="ct")
    SC.dma("sp", ptab[:, :], d_pt, r=[], w=["ptab"], slot="pt")
    SC.dma("pool", cbf[:, :], d_ct[:, 0:768], r=[], w=["cbf"], slot="cbf")
    for l in range(L):
        SC.dma("pool", wst[:, l, :], d_wst[l], r=[], w=["wst"], slot="wst%d" % l)
    for c in range(8):
        for hf in range(2):
            SC.dma("sp", xT[:, c, hf * 1024:(hf + 1) * 1024], d_xT[c * 128:(c + 1) * 128, hf * 1024:(hf + 1) * 1024],
                   r=[], w=[kX(c, 2 * hf), kX(c, 2 * hf + 1)], slot="x%d_%d" % (c, hf))

    ident_f = ctab[:, C_ID:C_ID + 128]
    triF_f = ctab[:, C_TRIF:C_TRIF + 128]
    triB_f = ctab[:, C_TRIB:C_TRIB + 128]
    ones_f = ctab[:, C_ONES:C_ONES + 128]
    ident_b = cbf[:, C_ID:C_ID + 128]
    triF_b = cbf[:, C_TRIF:C_TRIF + 128]
    triB_b = cbf[:, C_TRIB:C_TRIB + 128]
    ones_b = cbf[:, C_ONES:C_ONES + 128]
    bd_b = cbf[:, C_BD:C_BD + 128]
    perm_b = cbf[:, C_PERM:C_PERM + 128]

    def pcol(l, col, n=1):
        return ptab[:, l * NPT + col: l * NPT + col + n]

    def build_rope():
        for pc in range(4):
            sl = slice(pc * 512, (pc + 1) * 512)
            ti, ki = tmp_next()
            posi = ti.bitcast(I32)
            SC.dma("sp", posi, bass.AP(d_pos.tensor, pc * 512, [[0, 128], [1, 512]]),
                   r=[], w=[ki], slot="pos%d" % pc)
            ta, ka = tmp_next()
            SC.op("dve", lambda e, o=ta, i=posi: e.tensor_copy(o, i), r=[ki], w=[ka])
            SC.op("dve", lambda e, o=ta: e.tensor_scalar(o, o, ctab[:, C_FREQ:C_FREQ + 1], None, ALU.mult),
                  r=[ka, "ctab"], w=[ka])
            ty, ky = tmp_next()
            SC.op("dve", lambda e, o=ty, i=ta: e.tensor_scalar(o, i, 1.0 / TWO_PI, None, ALU.mult), r=[ka], w=[ky])
            tk, kk = tmp_next()
            tki = tk.bitcast(I32)
            SC.op("dve", lambda e, o=tki, i=ty: e.tensor_copy(o, i), r=[ky], w=[kk])
            SC.op("dve", lambda e, o=ty, i=tki: e.tensor_copy(o, i), r=[kk], w=[ky])
            C1 = 6.28125
            C2 = TWO_PI - C1
            SC.op("dve", lambda e, o=ta, k_=ty: e.scalar_tensor_tensor(o, k_, -C1, o, ALU.mult, ALU.add),
                  r=[ka, ky], w=[ka])
            SC.op("dve", lambda e, o=ta, k_=ty: e.scalar_tensor_tensor(o, k_, -C2, o, ALU.mult, ALU.add),
                  r=[ka, ky], w=[ka])

            def wrap(t_, kt, m_, km_):
                SC.op("dve", lambda e: e.tensor_single_scalar(m_, t_, PI, ALU.is_gt), r=[kt], w=[km_])
                SC.op("dve", lambda e: e.scalar_tensor_tensor(t_, m_, -TWO_PI, t_, ALU.mult, ALU.add),
                      r=[kt, km_], w=[kt])
                SC.op("dve", lambda e: e.tensor_single_scalar(m_, t_, -PI, ALU.is_lt), r=[kt], w=[km_])
                SC.op("dve", lambda e: e.scalar_tensor_tensor(t_, m_, TWO_PI, t_, ALU.mult, ALU.add),
                      r=[kt, km_], w=[kt])
                SC.op("dve", lambda e: e.tensor_scalar(t_, t_, -3.1415925, 3.1415925, ALU.max, ALU.min),
                      r=[kt], w=[kt])

            wrap(ta, ka, tk, kk)
            SC.op("act", lambda e, i=ta: e.activation(ropeS[:, sl], i, AF.Sin, scale=ctab[:, C_SIGN:C_SIGN + 1]),
                  r=[ka, "ctab"], w=[("ropeS", pc)])
            SC.op("dve", lambda e, o=ty, i=ta: e.tensor_scalar(o, i, PI / 2, None, ALU.add), r=[ka], w=[ky])
            wrap(ty, ky, tk, kk)
            SC.op("act", lambda e, i=ty: e.activation(ropeC[:, sl], i, AF.Sin), r=[ky], w=[("ropeC", pc)])

    def rms_norm(l, which, dst, bts=(0, 1, 2, 3)):
        for bt in bts:
            tsl = slice(bt * 512, (bt + 1) * 512)
            b = ps_next()
            for c in range(8):
                tq, kq = tmp_next()
                sq = tmp_bf(tq)[:, 0:512]
                SC.op("act", lambda e, o=sq, c=c: e.activation(o, xT[:, c, tsl], AF.Square),
                      r=[kX(c, bt)], w=[kq])
                SC.op("pe", lambda e, b=b, i=sq, c=c: e.matmul(psb[b][:, :], ones_b, i, start=(c == 0), stop=(c == 7)),
                      r=[kq, "cbf"], w=[kPS(b)])
            tl, kl = tmp_next()
            SC.op("act", lambda e, o=tl, b=b: e.activation(o, psb[b][:, :], AF.Ln, bias=EPS, scale=1.0 / D),
                  r=[kPS(b)], w=[kl])
            SC.op("act", lambda e, o=tl: e.activation(o, o, AF.Exp, scale=-0.5), r=[kl], w=[kl])
            for c in range(8):
                gcol = pcol(l, PT_GN + which * 8 + c)
                if dst == "h":
                    SC.op("dve", lambda e, c=c, g=gcol, rs=tl: e.scalar_tensor_tensor(
                        hT[:, c, tsl], xT[:, c, tsl], g, rs, ALU.mult, ALU.mult),
                        r=[kX(c, bt), kl, "ptab"], w=[kH(c, bt)])
                else:
                    SC.op("dve", lambda e, c=c, g=gcol, rs=tl: e.scalar_tensor_tensor(
                        xT[:, c, tsl], xT[:, c, tsl], g, rs, ALU.mult, ALU.mult),
                        r=[kX(c, bt), kl, "ptab"], w=[kX(c, bt)])

    def ffn(l, k, hooks=(None, None)):
        kA = lambda f, s_: ("act", f, s_)
        for hf in range(2):
            for fb in range(11):
                if fb == 2 and hooks[hf] is not None:
                    hooks[hf]()
                wg, kwg = wload(d_wg[k][l, fb], 2048, tag=("wg", k, l, fb, hf))
                wu, kwu = wload(d_wu[k][l, fb], 2048, tag=("wu", k, l, fb, hf))
                for fc in range(2):
                    f = fb * 2 + fc
                    bg = [ps_next(), ps_next()]
                    bu = [ps_next(), ps_next()]
                    for (wt, kw, bb) in ((wg, kwg, bg), (wu, kwu, bu)):
                        for c in range(8):
                            lhsT = wt[:, c * 256 + fc * 128: c * 256 + fc * 128 + 128]
                            for s_ in range(2):
                                bt = hf * 2 + s_
                                SC.op("pe", lambda e, o=psb[bb[s_]], a=lhsT, c=c, bt=bt: e.matmul(
                                    o[:, :], a, hT[:, c, bt * 512:(bt + 1) * 512], start=(c == 0), stop=(c == 7)),
                                    r=[kw, kH(c, bt)], w=[kPS(bb[s_])])
                    for s_ in range(2):
                        tg, kg = tmp_next()
                        SC.op("act", lambda e, o=tg, b=bg[s_]: e.activation(o, psb[b][:, :], AF.Silu),
                              r=[kPS(bg[s_])], w=[kg])
                        SC.op("dve", lambda e, i=tg, b=bu[s_], f=f, s_=s_: e.tensor_tensor(
                            act[:, f, s_ * 512:(s_ + 1) * 512], i, psb[b][:, :], ALU.mult),
                            r=[kg, kPS(bu[s_])], w=[kA(f, s_)])
            for dc in range(8):
                wd0, kw0 = wload(d_wd[k][l, dc * 2], 1408)
                wd1, kw1 = wload(d_wd[k][l, dc * 2 + 1], 1408)
                bo = [ps_next(), ps_next()]
                for f in range(NFC):
                    wt, kw = (wd0, kw0) if f < 11 else (wd1, kw1)
                    lhsT = wt[:, (f % 11) * 128:(f % 11) * 128 + 128]
                    for s_ in range(2):
                        SC.op("pe", lambda e, o=psb[bo[s_]], a=lhsT, f=f, s_=s_: e.matmul(
                            o[:, :], a, act[:, f, s_ * 512:(s_ + 1) * 512], start=(f == 0), stop=(f == NFC - 1)),
                            r=[kw, kA(f, s_)], w=[kPS(bo[s_])])
                for s_ in range(2):
                    bt = hf * 2 + s_
                    SC.op("dve", lambda e, b=bo[s_], dc=dc, bt=bt: e.scalar_tensor_tensor(
                        xT[:, dc, bt * 512:(bt + 1) * 512], psb[b][:, :], 0.5, xT[:, dc, bt * 512:(bt + 1) * 512],
                        ALU.mult, ALU.add),
                        r=[kPS(bo[s_]), kX(dc, bt)], w=[kX(dc, bt)])

    ACTKEYS = [("act", f, s_) for f in range(NFC) for s_ in range(2)]

    PEND = []

    def flush_pending():
        while PEND:
            PEND.pop(0)()

    def fproj(l, chunk, epilogue):
        wt, kw = wload(d_winF[l, chunk], 1024)
        for bt in range(4):
            b = ps_next()
            for c in range(8):
                SC.op("pe", lambda e, b=b, c=c, bt=bt: e.matmul(
                    psb[b][:, :], wt[:, c * 128:(c + 1) * 128], hT[:, c, bt * 512:(bt + 1) * 512],
                    start=(c == 0), stop=(c == 7)), r=[kw, kH(c, bt)], w=[kPS(b)])
            flush_pending()
            cont = epilogue(bt, b)
            if cont is not None:
                PEND.append(cont)

    def tproj(l, src, ncols, epilogue):
        wt, kw = wload(src, 8 * ncols)
        for tt in range(16):
            b = ps_next()
            for c in range(8):
                SC.op("pe", lambda e, b=b, c=c, tt=tt: e.matmul(
                    psb[b][:, 0:ncols], hT[:, c, tt * 128:(tt + 1) * 128], wt[:, c * ncols:(c + 1) * ncols],
                    start=(c == 0), stop=(c == 7)), r=[kw, kH(c, tt // 4)], w=[kPS(b)])
            flush_pending()
            epilogue(tt, b)

    def wout_pass(l, chunks, ysrc, src_blocks, ndc_per_blk, bt_hook=None):
        nmc = len(chunks)
        for blk in range(8 // ndc_per_blk):
            wt, kw = wload(src_blocks(blk), ndc_per_blk * nmc * 128, tag=("wo", l, nmc, blk))
            order = ([(dci, bt) for bt in range(4) for dci in range(ndc_per_blk)] if bt_hook is not None
                     else [(dci, bt) for dci in range(ndc_per_blk) for bt in range(4)])
            for oi, (dci, bt) in enumerate(order):
                dc = blk * ndc_per_blk + dci
                if True:
                    b = ps_next()
                    for i, mc in enumerate(chunks):
                        rhs, kr = ysrc(mc, bt)
                        o0 = (dci * nmc + i) * 128
                        SC.op("pe", lambda e, b=b, o0=o0, rhs=rhs, i=i: e.matmul(
                            psb[b][:, :], wt[:, o0:o0 + 128], rhs, start=(i == 0), stop=(i == nmc - 1)),
                            r=[kw, kr], w=[kPS(b)])
                    SC.op("dve", lambda e, b=b, dc=dc, bt=bt: e.tensor_tensor(
                        xT[:, dc, bt * 512:(bt + 1) * 512], psb[b][:, :], xT[:, dc, bt * 512:(bt + 1) * 512], ALU.add),
                        r=[kPS(b), kX(dc, bt)], w=[kX(dc, bt)])
                    if bt_hook is not None and dci == ndc_per_blk - 1:
                        bt_hook(bt)

    def mixer(l, bt_hook=None):
        kU = lambda gp, bt: ("uT", gp, bt)
        kVG = lambda tt: ("vgn", tt)
        kYS = lambda gp, bt: ("ysg", gp, bt)
        SC.adopt([kU(g_, b_) for g_ in range(2) for b_ in range(4)] + [kVG(t_) for t_ in range(16)]
                 + [kYS(g_, b_) for g_ in range(2) for b_ in range(4)], ACTKEYS)
        kQ = lambda c, bt: ("qTa", c, bt)
        kK = lambda bt: ("kTa", bt)
        kV = lambda tt: ("va", tt)
        SC.adopt([kQ(c, b_) for c in range(4) for b_ in range(4)] + [kK(b_) for b_ in range(4)]
                 + [kV(t_) for t_ in range(16)], ACTKEYS)

        for gp in range(2):
            def ep_u(bt, b, gp=gp):
                SC.op("act", lambda e: e.activation(uT[:, gp, bt * 512:(bt + 1) * 512], psb[b][:, :], AF.Gelu),
                      r=[kPS(b)], w=[kU(gp, bt)])
            fproj(l, 9 + gp, ep_u)

        sgug = pcol(l, PT_SG, 256)

        def ep_sv(tt, b):
            tv, kv_ = tmp_next()
            vg = tv[:, 0:256]
            vsq = tv[:, 256:512]
            SC.op("act", lambda e: e.activation(vg, psb[b][:, 0:256], AF.Gelu), r=[kPS(b)], w=[kv_])
            SC.op("dve", lambda e: e.tensor_tensor(vsq, vg, vg, ALU.mult), r=[kv_], w=[kv_])
            ks_ = ("small", "sgu")
            ms = small[:, 0:4]
            SC.op("dve", lambda e: e.tensor_reduce(ms, fap(vsq, [[64, 4], [1, 64]]), AX.X, ALU.add), r=[kv_], w=[ks_])
            SC.op("act", lambda e: e.activation(ms, ms, AF.Ln, bias=EPS, scale=1.0 / 64), r=[ks_], w=[ks_])
            SC.op("act", lambda e: e.activation(ms, ms, AF.Exp, scale=-0.5), r=[ks_], w=[ks_])
            SC.op("dve", lambda e: e.tensor_tensor(fap(vg, [[64, 4], [1, 64]]), fap(vg, [[64, 4], [1, 64]]),
                                                   fap(ms, [[1, 4], [0, 64]]), ALU.mult), r=[kv_, ks_], w=[kv_])
            SC.op("dve", lambda e: e.tensor_tensor(vgn[:, tt, :], vg, sgug, ALU.mult), r=[kv_, "ptab"], w=[kVG(tt)])
        tproj(l, d_winT[l, 0], 256, ep_sv)

        for gp in range(2):
            for bt in range(4):
                b = ps_next()
                for ci in range(4):
                    tt = bt * 4 + ci
                    for gi in range(2):
                        g = gp * 2 + gi
                        SC.op("pe", lambda e, b=b, ci=ci, tt=tt, gi=gi, g=g: e.matmul(
                            psb[b][gi * 64:(gi + 1) * 64, ci * 128:(ci + 1) * 128],
                            vgn[:, tt, g * 64:(g + 1) * 64], wst[:, l, g * 128:(g + 1) * 128],
                            start=True, stop=True), r=[kVG(tt), "wst"], w=[kPS(b)])
                tb, kb = tmp_next()
                biasT = pcol(l, PT_SB + gp * 128, 128)
                SC.op("dve", lambda e, b=b, tb=tb, biasT=biasT: e.tensor_tensor(
                    fap(tb, [[128, 4], [1, 128]]), fap(psb[b][:, :], [[128, 4], [1, 128]]),
                    fap(biasT, [[0, 4], [1, 128]]), ALU.add), r=[kPS(b), "ptab"], w=[kb])
                SC.op("dve", lambda e, tb=tb, gp=gp, bt=bt: e.tensor_tensor(
                    ysgT[:, gp, bt * 512:(bt + 1) * 512], tb, uT[:, gp, bt * 512:(bt + 1) * 512], ALU.mult),
                    r=[kb, kU(gp, bt)], w=[kYS(gp, bt)])
        if stop_after == "sgu":
            return

        kYA = lambda c, bt: ("yat", c, bt)
        SC.adopt([kYA(c, b_) for c in range(4) for b_ in range(4)],
                 [kU(g_, b_) for g_ in range(2) for b_ in range(4)] + [kVG(t_) for t_ in range(16)])

        def qk_epilogue(dst_fn, kdst_fn, gcol):
            def ep(bt, b):
                tsl = slice(bt * 512, (bt + 1) * 512)
                tq, kq = tmp_next()
                ts, ks = tmp_next()
                sqb = tmp_bf(ts)[:, 0:512]
                qgb = tmp_bf(ts)[:, 512:1024]
                SC.op("act", lambda e: e.activation(sqb, psb[b][:, :], AF.Square), r=[kPS(b)], w=[ks])
                SC.op("act", lambda e: e.activation(tq, psb[b][:, :], AF.Copy, scale=gcol), r=[kPS(b), "ptab"], w=[kq])
                SC.op("dve", lambda e: e.tensor_copy(qgb, tq), r=[kq], w=[ks])

                def part2():
                    bm = ps_next()
                    SC.op("pe", lambda e: e.matmul(psb[bm][:, :], bd_b, sqb, start=True, stop=True),
                          r=[ks, "cbf"], w=[kPS(bm)])
                    bp = ps_next()
                    SC.op("pe", lambda e: e.matmul(psb[bp][:, :], perm_b, qgb, start=True, stop=True),
                          r=[ks, "cbf"], w=[kPS(bp)])
                    tr, kr = tmp_next()
                    SC.op("act", lambda e: e.activation(tr, psb[bm][:, :], AF.Ln, bias=EPS, scale=1.0 / 64),
                          r=[kPS(bm)], w=[kr])
                    SC.op("act", lambda e: e.activation(tr, tr, AF.Exp, scale=-0.5), r=[kr], w=[kr])
                    t2, k2 = tmp_next()
                    SC.op("dve", lambda e: e.tensor_tensor(t2, psb[bp][:, :], ropeS[:, tsl], ALU.mult),
                          r=[kPS(bp), ("ropeS", bt)], w=[k2])
                    SC.op("dve", lambda e: e.tensor_tensor(tq, tq, ropeC[:, tsl], ALU.mult), r=[kq, ("ropeC", bt)], w=[kq])
                    SC.op("dve", lambda e: e.tensor_tensor(tq, tq, t2, ALU.add), r=[kq, k2], w=[kq])
                    SC.op("dve", lambda e: e.tensor_tensor(dst_fn(bt), tq, tr, ALU.mult), r=[kq, kr], w=[kdst_fn(bt)])
                return part2
            return ep

        for c in range(4):
            fproj(l, c, qk_epilogue(lambda bt, c=c: qTa[:, c, bt * 512:(bt + 1) * 512], lambda bt, c=c: kQ(c, bt),
                                    pcol(l, PT_GQ)))
        fproj(l, 4, qk_epilogue(lambda bt: kTa[:, bt * 512:(bt + 1) * 512], lambda bt: kK(bt), pcol(l, PT_GK)))

        kG = lambda tt: ("G", tt)
        gbias = pcol(l, PT_GB, 16)

        def ep_t0(tt, b):
            SC.op("act", lambda e: e.activation(va[:, tt, :], psb[b][:, 0:128], AF.Copy), r=[kPS(b)], w=[kV(tt)])
            SC.op("dve", lambda e: e.tensor_tensor(Gs[:, tt, :], psb[b][:, 128:144], gbias, ALU.add),
                  r=[kPS(b), "ptab"], w=[kG(tt)])
        tproj(l, d_winT0[l], 144, ep_t0)

        flush_pending()
        for blk in range(4):
            prefetch(("wo", l, 6, blk), d_woA[l, blk], 1536)
        esink = small[:, 8:12]
        SC.op("act", lambda e: e.activation(esink, pcol(l, PT_SINK, 4), AF.Exp), r=["ptab"], w=[("small", "esink")])

        aunits = [(n, hp) for n in range(16) for hp in range(2)]
        AC = [dict() for _ in aunits]
        OD = {}

        def stS(ui):
            n, hp = aunits[ui]
            prs = slice(hp * 64, (hp + 1) * 64)
            js = [j for j in (n - 1, n, n + 1) if 0 <= j < 16]
            ptiles = []
            for j in js:
                bs = ps_next()
                rhs = fap(qTa[prs, 0, n * 128:(n + 1) * 128], [[S, 4], [1, 128]])
                SC.op("pe", lambda e: e.matmul(
                    psb[bs][:, :], kTa[prs, j * 128:(j + 1) * 128], rhs, start=True, stop=True),
                    r=[kK(j // 4)] + [kQ(c, n // 4) for c in range(4)], w=[kPS(bs)])
                tp, kp = tmp_next()
                pt = tmp_bf(tp)[:, 0:512]
                SC.op("act", lambda e: e.activation(pt, psb[bs][:, :], AF.Exp, scale=0.125),
                      r=[kPS(bs)], w=[kp])
                if j != n:
                    msk = triB_b if j == n - 1 else triF_b
                    SC.op("pool", lambda e: e.tensor_tensor(
                        fap(pt, [[128, 4], [1, 128]]), fap(pt, [[128, 4], [1, 128]]),
                        fap(msk, [[0, 4], [1, 128]]), ALU.mult), r=[kp, "cbf"], w=[kp])
                ptiles.append((j, pt, kp))
            AC[ui]["pt"] = ptiles

        def stP(ui):
            n, hp = aunits[ui]
            prs = slice(hp * 64, (hp + 1) * 64)
            if hp == 0:
                OD[n] = (ps_next(), ps_next())
            bO, bD = OD[n]
            ptiles = AC[ui]["pt"]
            for i, (j, pt, kp) in enumerate(ptiles):
                SC.op("pe", lambda e: e.matmul(
                    psb[bO][prs, :], va[:, j, hp * 64:(hp + 1) * 64], pt,
                    start=(i == 0), stop=(i == len(ptiles) - 1)), r=[kV(j), kp], w=[kPS(bO)])
                SC.op("pe", lambda e: e.matmul(
                    psb[bD][prs, :], ones_b[:, 0:64], pt,
                    start=(i == 0), stop=(i == len(ptiles) - 1)), r=["cbf", kp], w=[kPS(bD)])
            if hp == 1:
                td, kd = tmp_next()
                SC.op("dve", lambda e: e.tensor_tensor(
                    fap(td, [[128, 4], [1, 128]]), fap(psb[bD][:, :], [[128, 4], [1, 128]]),
                    fap(esink, [[1, 4], [0, 128]]), ALU.add), r=[kPS(bD), ("small", "esink")], w=[kd])
                SC.op("dve", lambda e: e.reciprocal(td, td), r=[kd], w=[kd])
                SC.op("dve", lambda e: e.tensor_tensor(
                    fap(yatT[:, 0, n * 128:(n + 1) * 128], [[S, 4], [1, 128]]),
                    fap(psb[bO][:, :], [[128, 4], [1, 128]]), fap(td, [[128, 4], [1, 128]]), ALU.mult),
                    r=[kPS(bO), kd], w=[kYA(c, n // 4) for c in range(4)])

        for step in range(len(aunits) + 1):
            if step < len(aunits):
                stS(step)
            if step >= 1:
                stP(step - 1)
        if stop_after == "attn":
            return

        def ysrc1(mc, bt):
            if mc < 4:
                return yatT[:, mc, bt * 512:(bt + 1) * 512], kYA(mc, bt)
            return ysgT[:, mc - 6, bt * 512:(bt + 1) * 512], kYS(mc - 6, bt)
        wout_pass(l, [0, 1, 2, 3, 6, 7], ysrc1, lambda blk: d_woA[l, blk], 2)
        if stop_after == "wout1":
            return

        old = ([kYA(c, b_) for c in range(4) for b_ in range(4)] + [kYS(g_, b_) for g_ in range(2) for b_ in range(4)]
               + [kQ(c, b_) for c in range(4) for b_ in range(4)] + [kK(b_) for b_ in range(4)]
               + [kV(t_) for t_ in range(16)])
        kQM = lambda ch, bt: ("qTm", ch, bt)
        kKM = lambda ch, bt: ("kTm", ch, bt)
        kKT = lambda tt: ("km", tt)
        kVA = lambda tt: ("vaug", tt)
        kGO = lambda tt: ("GO", tt)
        newk = ([kQM(ch, b_) for ch in range(2) for b_ in range(4)] + [kKM(ch, b_) for ch in range(2) for b_ in range(4)]
                + [kKT(t_) for t_ in range(16)] + [kVA(t_) for t_ in range(16)] + [kGO(t_) for t_ in range(16)]
                + ["mlmisc", ("Sst", 0), ("Sst", 1), ("Sbf", 0), ("Sbf", 1)])
        SC.adopt(newk, old)

        for ch in range(2):
            def ep_q(bt, b, ch=ch):
                SC.op("act", lambda e: e.activation(qTm[:, ch, bt * 512:(bt + 1) * 512], psb[b][:, :], AF.Copy),
                      r=[kPS(b)], w=[kQM(ch, bt)])
            fproj(l, 5 + ch, ep_q)
        for ch in range(2):
            def ep_k(bt, b, ch=ch):
                SC.op("act", lambda e: e.activation(kTm[:, ch, bt * 512:(bt + 1) * 512], psb[b][:, :], AF.Copy,
                                                    scale=0.125), r=[kPS(b)], w=[kKM(ch, bt)])
            fproj(l, 7 + ch, ep_k)

        def ep_mk(tt, b):
            SC.op("act", lambda e: e.activation(km[:, tt, :], psb[b][:, 0:256], AF.Copy, scale=0.125),
                  r=[kPS(b)], w=[kKT(tt)])
        tproj(l, d_winT[l, 1], 256, ep_mk)

        def ep_mv(tt, b):
            SC.op("dve", lambda e: e.tensor_copy(fap(vaug[:, tt, 0, 0:64], [[65, 4], [1, 64]]),
                                                 fap(psb[b][:, 0:256], [[64, 4], [1, 64]])), r=[kPS(b)], w=[kVA(tt)])
            SC.op("dve", lambda e: e.memset(fap(vaug[:, tt, 0, 64:65], [[65, 4], [1, 1]]), 1.0), r=[], w=[kVA(tt)])
        tproj(l, d_winT[l, 2], 256, ep_mv)

        hg = pcol(l, PT_HG, 256)

        def ep_mo(tt, b):
            tm_, km_ = tmp_next()
            t = tm_[:, 0:256]
            SC.op("act", lambda e: e.activation(t, psb[b][:, 0:256], AF.Sigmoid), r=[kPS(b)], w=[km_])
            SC.op("dve", lambda e: e.tensor_tensor(GO[:, tt, :], t, hg, ALU.mult), r=[km_, "ptab"], w=[kGO(tt)])
        tproj(l, d_winT[l, 3], 256, ep_mo)
        kHM = lambda tt: ("hm", tt)
        kYM = lambda ch, bt: ("ymT", ch, bt)
        SC.adopt([kHM(t_) for t_ in range(16)] + [kYM(ch, b_) for ch in range(2) for b_ in range(4)] + ["rowt", ("qblk", 0), ("qblk", 1)], ALLH)

        KM = "mlmisc"
        def gview(types):
            t0, t1 = types
            return fap(Gs[:, 0, t0 * 4:t0 * 4 + 4], [[(t1 - t0) * 4, 2], [16, 16], [1, 4]])
        SPv = fap(SPt, [[64, 2], [4, 16], [1, 4]])
        Uv = fap(Ut, [[64, 2], [4, 16], [1, 4]])
        GK = [kG(t_) for t_ in range(16)]
        SC.op("act", lambda e: e.activation(SPv, gview((1, 3)), AF.Exp, scale=-1.0), r=GK, w=[KM])
        SC.op("act", lambda e: e.activation(SPt, SPt, AF.Ln, bias=1.0), r=[KM], w=[KM])
        bB = ps_next()
        SC.op("pe", lambda e: e.matmul(psb[bB][:, 0:64], triF_f, SPt[:, 0:64], start=True, stop=True),
              r=[KM, "ctab"], w=[kPS(bB)])
        SC.op("pe", lambda e: e.matmul(psb[bB][:, 64:128], triB_f, SPt[:, 64:128], start=True, stop=True),
              r=[KM, "ctab"], w=[kPS(bB)])
        SC.op("pe", lambda e: e.matmul(psb[bB][0:1, 128:256], ones_f[:, 0:1], SPt, start=True, stop=True),
              r=[KM, "ctab"], w=[kPS(bB)])
        SC.op("dve", lambda e: e.tensor_tensor(Uv, gview((0, 2)), fap(psb[bB][:, 0:128], [[64, 2], [4, 16], [1, 4]]),
                                               ALU.add), r=GK + [kPS(bB)], w=[KM])
        SC.op("act", lambda e: e.activation(FLt, psb[bB][:, 0:128], AF.Copy), r=[kPS(bB)], w=[KM])
        rows = small[0:1, 16:16 + 128]
        SC.op("act", lambda e: e.activation(rows, psb[bB][0:1, 128:256], AF.Copy), r=[kPS(bB)], w=[("small", "rows")])
        bT = ps_next()
        SC.op("pe", lambda e: e.transpose(psb[bT][:, 0:128], Ut, ident_f), r=[KM, "ctab"], w=[kPS(bT)])
        ucol = small[:, 12:13]
        SC.op("dve", lambda e: e.tensor_reduce(ucol, psb[bT][:, 0:128], AX.X, ALU.max), r=[kPS(bT)], w=[("small", "ucol")])
        SC.op("pe", lambda e: e.matmul(psb[bT][0:1, 128:256], ucol, ident_f, start=True, stop=True),
              r=[("small", "ucol"), "ctab"], w=[kPS(bT)])
        umax = rowt[0:1, 0:128]
        mprev = rowt[0:1, 128:256]
        mcr = rowt[0:1, 256:384]
        alr = rowt[0:1, 384:512]
        KR = "rowt"
        SC.op("act", lambda e: e.activation(umax, psb[bT][0:1, 128:256], AF.Copy), r=[kPS(bT)], w=[KR])
        SC.op("dve", lambda e: e.memset(mprev, 0.0), r=[], w=[KR])
        for dr in range(2):
            for h in range(4):
                if dr == 0:
                    sel = lambda t_: fap(t_[:, h:h + 1], [[4, 16]])
                else:
                    sel = lambda t_: fap(t_[:, 64 + 60 + h:64 + 60 + h + 1], [[-4, 16]])
                SC.op("dve", lambda e: e.tensor_tensor_scan(sel(mcr), sel(umax), sel(rows), 0.0, ALU.max, ALU.subtract),
                      r=[KR, ("small", "rows")], w=[KR])
        SC.op("dve", lambda e: e.tensor_copy(mprev[:, 4:64], mcr[:, 0:60]), r=[KR], w=[KR])
        SC.op("dve", lambda e: e.tensor_copy(mprev[:, 64:124], mcr[:, 68:128]), r=[KR], w=[KR])
        SC.op("dve", lambda e: e.tensor_tensor(mcr, mcr, rows, ALU.add), r=[KR, ("small", "rows")], w=[KR])
        SC.op("dve", lambda e: e.tensor_tensor(alr, mprev, mcr, ALU.subtract), r=[KR], w=[KR])
        SC.op("act", lambda e: e.activation(alr, alr, AF.Exp), r=[KR], w=[KR])
        bC = ps_next()
        SC.op("pe", lambda e: e.matmul(psb[bC][:, 0:256], ones_f[0:1, :], rowt[0:1, 256:512], start=True, stop=True),
              r=[KR, "ctab"], w=[kPS(bC)])
        SC.op("dve", lambda e: e.tensor_tensor(Et, Ut, psb[bC][:, 0:128], ALU.subtract), r=[KM, kPS(bC)], w=[KM])
        SC.op("act", lambda e: e.activation(Et, Et, AF.Exp), r=[KM], w=[KM])
        SC.op("dve", lambda e: e.tensor_tensor(FLt, FLt, psb[bC][:, 0:128], ALU.subtract), r=[KM, kPS(bC)], w=[KM])
        SC.op("act", lambda e: e.activation(FLt, FLt, AF.Exp), r=[KM], w=[KM])
        SC.op("dve", lambda e: e.memset(Atab, 0.0), r=[], w=[KM])
        for hp in range(2):
            prs = slice(hp * 64, (hp + 1) * 64)
            SC.op("act", lambda e, prs=prs, hp=hp: e.activation(
                fap(Atab[prs, hp:hp + 1], [[64, 2], [4, 16], [2, 2]]),
                fap(psb[bC][prs, 128 + hp:129 + hp], [[64, 2], [4, 16], [2, 2]]), AF.Copy), r=[kPS(bC)], w=[KM])
        SC.op("dve", lambda e: e.memset(Sst, 0.0), r=[], w=[("Sst", 0), ("Sst", 1)])

        prefetch(("wo", l, 2, 0), d_woB[l], 2048)
        for fb_ in range(2):
            prefetch(("wg", 1, l, fb_, 0), d_wg[1][l, fb_], 2048)
            prefetch(("wu", 1, l, fb_, 0), d_wu[1][l, fb_], 2048)
        SC.op("dve", lambda e: e.memset(Qblk, 0.0), r=[], w=[("qblk", 0), ("qblk", 1)])
        units = []
        for i in range(16):
            units += [(0, i), (1, 15 - i)]
        UC = [dict() for _ in units]

        def stA(ui):
            dr, c = units[ui]
            cx = UC[ui]
            qb = ui % 2
            kqb = ("qblk", qb)
            for hp in range(2):
                prs = slice(hp * 64, (hp + 1) * 64)
                SC.op("act", lambda e: e.activation(
                    fap(Qblk[prs, qb, hp * 128:hp * 128 + 1], [[256, 2], [1, 128]]),
                    fap(qTm[prs, 0, c * 128:c * 128 + 1], [[S, 2], [1, 128]]), AF.Copy),
                    r=[kQM(0, c // 4), kQM(1, c // 4)], w=[kqb])
            bq = ps_next()
            cx["bq"] = bq
            for ch in range(2):
                SC.op("pe", lambda e: e.matmul(
                    psb[bq][:, ch * 256:(ch + 1) * 256], kTm[:, ch, c * 128:(c + 1) * 128],
                    Qblk[:, qb, ch * 256:(ch + 1) * 256], start=True, stop=True),
                    r=[kKM(ch, c // 4), kqb], w=[kPS(bq)])

        def stB(ui):
            dr, c = units[ui]
            cx = UC[ui]
            col = dr * 64 + c * 4
            msk = triF_b if dr == 0 else triB_b
            tp, kp = tmp_next()
            pT = tmp_bf(tp)[:, 0:512]
            bq = cx["bq"]
            SC.op("dve", lambda e: e.tensor_tensor(
                fap(pT, [[128, 4], [1, 128]]), fap(psb[bq][:, :], [[128, 4], [1, 128]]),
                fap(msk, [[0, 4], [1, 128]]), ALU.mult), r=[kPS(bq), "cbf"], w=[kp])
            tv, kv_ = tmp_next()
            vwf = tmp_bf(tv)[:, 0:260]
            SC.op("pool", lambda e: e.tensor_tensor(fap(vwf, [[65, 4], [1, 65]]), vaug[:, c, :, :],
                                                    fap(Et[:, col:col + 4], [[1, 4], [0, 65]]),
                                                    ALU.mult), r=[kVA(c), KM], w=[kv_])
            cx.update(pT=pT, kp=kp, vwf=vwf, kv=kv_)

        def stC(ui):
            dr, c = units[ui]
            cx = UC[ui]
            col = dr * 64 + c * 4
            pT, kp, vwf, kv_ = cx["pT"], cx["kp"], cx["vwf"], cx["kv"]
            Sd = Sst[:, dr, :]
            Sb = Sbf[:, dr, :]
            SC.op("pool", lambda e: e.tensor_tensor(fap(Sd, [[65, 4], [1, 65]]), fap(Sd, [[65, 4], [1, 65]]),
                                                    fap(Atab[:, col:col + 4], [[1, 4], [0, 65]]),
                                                    ALU.mult), r=[("Sst", dr), KM], w=[("Sst", dr)])
            SC.op("act", lambda e: e.activation(Sb, Sd, AF.Copy), r=[("Sst", dr)], w=[("Sbf", dr)])
            bn = ps_next()
            for ch in range(2):
                SC.op("pe", lambda e: e.matmul(
                    psb[bn][:, ch * 130:(ch + 1) * 130], qTm[:, ch, c * 128:(c + 1) * 128], Sbf[:, dr, ch * 130:(ch + 1) * 130],
                    start=True, stop=False), r=[kQM(ch, c // 4), ("Sbf", dr)], w=[kPS(bn)])
                for hp in range(2):
                    h = 2 * ch + hp
                    SC.op("pe", lambda e: e.matmul(
                        psb[bn][:, h * 65:(h + 1) * 65], pT[:, h * 128:(h + 1) * 128], vwf[:, h * 65:(h + 1) * 65],
                        start=False, stop=(hp == 1)), r=[kp, kv_], w=[kPS(bn)])
            bs = ps_next()
            for ch in range(2):
                SC.op("pe", lambda e: e.matmul(
                    psb[bs][:, ch * 130:(ch + 1) * 130], km[:, c, ch * 128:(ch + 1) * 128], vwf[:, ch * 130:(ch + 1) * 130],
                    start=True, stop=True), r=[kKT(c), kv_], w=[kPS(bs)])
            cx.update(bn=bn, bs=bs)

        def stD(ui):
            dr, c = units[ui]
            cx = UC[ui]
            col = dr * 64 + c * 4
            bn, bs = cx["bn"], cx["bs"]
            Sd = Sst[:, dr, :]
            SC.op("dve", lambda e: e.tensor_tensor(Sd, Sd, psb[bs][:, 0:260], ALU.add),
                  r=[("Sst", dr), kPS(bs)], w=[("Sst", dr)])
            kd_ = ("small", "den", dr)
            den = small[:, 144 + dr * 4: 148 + dr * 4]
            SC.op("act", lambda e: e.activation(den, fap(psb[bn][:, 64:65], [[65, 4]]), AF.Abs), r=[kPS(bn)], w=[kd_])
            SC.op("dve", lambda e: e.tensor_tensor(den, den, FLt[:, col:col + 4], ALU.max), r=[kd_, KM], w=[kd_])
            SC.op("dve", lambda e: e.reciprocal(den, den), r=[kd_], w=[kd_])
            first = (dr == 0 and c < 8) or (dr == 1 and c >= 8)
            hmv = fap(hm[:, c, :], [[64, 4], [1, 64]])
            nv = fap(psb[bn][:, 0:64], [[65, 4], [1, 64]])
            rb = fap(den, [[1, 4], [0, 64]])
            if first:
                SC.op("dve", lambda e: e.tensor_tensor(hmv, nv, rb, ALU.mult), r=[kPS(bn), kd_], w=[kHM(c)])
            else:
                t3, k3 = tmp_next()
                t3v = fap(t3[:, 0:256], [[64, 4], [1, 64]])
                SC.op("dve", lambda e: e.tensor_tensor(t3v, nv, rb, ALU.mult), r=[kPS(bn), kd_], w=[k3])
                SC.op("pool", lambda e: e.tensor_tensor(hm[:, c, :], hm[:, c, :], t3[:, 0:256], ALU.add),
                      r=[k3, kHM(c)], w=[kHM(c)])
                finish(c)

        def finish(c):
            t1, k1 = tmp_next()
            sq = t1[:, 0:256]
            SC.op("act", lambda e: e.activation(sq, hm[:, c, :], AF.Square), r=[kHM(c)], w=[k1])
            kf_ = ("small", "fin")
            ms = small[:, 152:156]
            SC.op("dve", lambda e: e.tensor_reduce(ms, fap(sq, [[64, 4], [1, 64]]), AX.X, ALU.add), r=[k1], w=[kf_])
            SC.op("act", lambda e: e.activation(ms, ms, AF.Ln, bias=EPS, scale=1.0 / 64), r=[kf_], w=[kf_])
            SC.op("act", lambda e: e.activation(ms, ms, AF.Exp, scale=-0.5), r=[kf_], w=[kf_])
            yv = t1[:, 256:512]
            SC.op("pool", lambda e: e.tensor_tensor(fap(yv, [[64, 4], [1, 64]]), fap(hm[:, c, :], [[64, 4], [1, 64]]),
                                                    fap(ms, [[1, 4], [0, 64]]), ALU.mult), r=[kHM(c), kf_, k1], w=[k1])
            t2, k2 = tmp_next()
            yb = tmp_bf(t2)[:, 0:256]
            SC.op("pool", lambda e: e.tensor_tensor(yb, yv, GO[:, c, :], ALU.mult), r=[k1, kGO(c)], w=[k2])
            bt_ = ps_next()
            pbf = psb[bt_][:, :].bitcast(BF16)
            for ch in range(2):
                SC.op("pe", lambda e, ch=ch: e.transpose(pbf[:, ch * 128:(ch + 1) * 128], yb[:, ch * 128:(ch + 1) * 128],
                                                         ident_b), r=[k2, "cbf"], w=[kPS(bt_)])
            SC.op("act", lambda e: e.activation(fap(ymT[:, 0, c * 128:(c + 1) * 128], [[S, 2], [1, 128]]),
                                                fap(pbf[:, 0:256], [[128, 2], [1, 128]]), AF.Copy),
                  r=[kPS(bt_)], w=[kYM(0, c // 4), kYM(1, c // 4)])

        nU = len(units)
        for step in range(nU + 3):
            if step < nU:
                stA(step)
            if 0 <= step - 1 < nU:
                stB(step - 1)
            if 0 <= step - 2 < nU:
                stC(step - 2)
            if 0 <= step - 3 < nU:
                stD(step - 3)
        if stop_after == "mlstm":
            return
        wout_pass(l, [4, 5], lambda mc, bt: (ymT[:, mc - 4, bt * 512:(bt + 1) * 512], kYM(mc - 4, bt)),
                  lambda blk: d_woB[l], 8, bt_hook=bt_hook)
        SC.adopt(ALLH, [kHM(t_) for t_ in range(16)] + [kYM(ch, b_) for ch in range(2) for b_ in range(4)] + ["rowt", ("qblk", 0), ("qblk", 1)])
        SC.adopt(ACTKEYS, newk)

    Gs = nc.alloc_sbuf_tensor("Gs", [128, 16, 16], F32)

    build_rope()
    if stop_after is not None:
        for l in layers:
            rms_norm(l, 0, "h")
            ffn(l, 0)
            if stop_after == "ffn1":
                break
            rms_norm(l, 1, "h")
            mixer(l)
            if stop_after != "ffn2":
                break
            rms_norm(l, 2, "h")
            ffn(l, 1)
            break
    else:
        rms_norm(layers[0], 0, "h", (0, 1))
        carry = [lambda l0=layers[0]: rms_norm(l0, 0, "h", (2, 3))]
        for li, l in enumerate(layers):
            last = li == len(layers) - 1
            hookA = carry[0]
            ffn(l, 0, hooks=(hookA, lambda l=l: rms_norm(l, 1, "h", (0, 1))))
            rms_norm(l, 1, "h", (2, 3))
            mixer(l)
            rms_norm(l, 2, "h", (0, 1))

            def hookB(l=l, last=last, li=li):
                rms_norm(l, 3, "x", (0, 1))
                if not last:
                    rms_norm(layers[li + 1], 0, "h", (0, 1))
            ffn(l, 1, hooks=(lambda l=l: rms_norm(l, 2, "h", (2, 3)), hookB))
            if last:
                rms_norm(l, 3, "x", (2, 3))
            else:
                def nxt(l=l, li=li):
                    rms_norm(l, 3, "x", (2, 3))
                    rms_norm(layers[li + 1], 0, "h", (2, 3))
                carry[0] = nxt

    dump_aps = {}
    if dumps:
        avail = {"yatT": (yatT, [128, 4, S], BF16), "ysgT": (ysgT, [128, 2, S], BF16), "ymT": (ymT, [128, 2, S], BF16),
                 "hm": (hm, [128, 16, 256], F32), "qTa": (qTa, [128, 4, S], BF16), "kTa": (kTa, [128, S], BF16),
                 "hT": (hT, [128, 8, S], BF16), "Et": (Et, [128, 128], F32), "FLt": (FLt, [128, 128], F32),
                 "Ut": (Ut, [128, 128], F32), "Atab": (Atab, [128, 128], F32), "ropeC": (ropeC[:, :], [128, S], BF16),
                 "ropeS": (ropeS[:, :], [128, S], BF16), "va": (va, [128, 16, 128], BF16), "Gs": (Gs[:, :, :], [128, 16, 16], F32),
                 "uT": (uT, [128, 2, S], BF16), "vgn": (vgn, [128, 16, 256], BF16)}
        for nm in dumps:
            apv, shp, dt = avail[nm]
            dd = dram("dump_" + nm, shp, dt, out=True)
            allkeys = list(SC.lastw.keys())
            SC.dma("sp", dd, apv, r=allkeys, w=[], slot="dump_" + nm, final=True)

    for c in range(8):
        for hf in range(2):
            SC.dma("sp", d_out[c * 128:(c + 1) * 128, hf * 1024:(hf + 1) * 1024], xT[:, c, hf * 1024:(hf + 1) * 1024],
                   r=[kX(c, 2 * hf), kX(c, 2 * hf + 1)], w=[], slot="o%d_%d" % (c, hf), final=True)
    SC.finalize()
    return nc


def _consts():
    ct = np.zeros((128, NCT), np.float32)
    i = np.arange(128)
    ct[:, C_ID:C_ID + 128] = np.eye(128, dtype=np.float32)
    ct[:, C_TRIF:C_TRIF + 128] = (i[:, None] <= i[None, :]).astype(np.float32)
    ct[:, C_TRIB:C_TRIB + 128] = (i[:, None] >= i[None, :]).astype(np.float32)
    ct[:, C_ONES:C_ONES + 128] = 1.0
    ct[:, C_BD:C_BD + 128] = ((i[:, None] // 64) == (i[None, :] // 64)).astype(np.float32)
    partner = np.where((i % 64) < 32, i + 32, i - 32)
    perm = np.zeros((128, 128), np.float32)
    perm[partner, i] = 1.0
    ct[:, C_PERM:C_PERM + 128] = perm
    j = (i % 64) % 32
    ct[:, C_FREQ] = (10000.0 ** (-(2.0 * j.astype(np.float32)) / np.float32(64.0))).astype(np.float32)
    ct[:, C_SIGN] = np.where((i % 64) < 32, -1.0, 1.0)
    return ct


def _freqs_like_reference():
    d = 64
    ar = np.arange(0, d, 2, dtype=np.float32)
    return (np.float32(10000.0) ** (-ar / np.float32(d))).astype(np.float32)


def _prep_shared(inp):
    L = inp["w_in"].shape[0]
    f32 = np.float32
    ct = _consts()
    fr = _freqs_like_reference()
    i = np.arange(128)
    ct[:, C_FREQ] = fr[(i % 64) % 32]
    pt = np.zeros((128, L * NPT), f32)
    names = ["norm_ffn1_g", "norm_mix_g", "norm_ffn2_g", "norm_out_g"]
    for l in range(L):
        o = l * NPT
        for k, nm in enumerate(names):
            pt[:, o + PT_GN + k * 8: o + PT_GN + (k + 1) * 8] = inp[nm][l].reshape(8, 128).T
        pt[:, o + PT_GQ] = inp["q_norm_g"][l][i % 64]
        pt[:, o + PT_GK] = inp["k_norm_g"][l][i % 64]
        sk = inp["attn_sink"][l]
        for c in range(4):
            pt[:64, o + PT_SINK + c] = sk[c]
            pt[64:, o + PT_SINK + c] = sk[4 + c]
        pt[:, o + PT_GB:o + PT_GB + 16] = inp["mlstm_gate_b"][l].reshape(1, 16)
        pt[:, o + PT_HG:o + PT_HG + 256] = inp["mlstm_head_g"][l][None, :]
        pt[:, o + PT_SG:o + PT_SG + 256] = inp["sgu_norm_g"][l][None, :]
        sb = inp["sgu_b"][l]
        for gp in range(2):
            pt[:64, o + PT_SB + gp * 128: o + PT_SB + (gp + 1) * 128] = sb[2 * gp][None, :]
            pt[64:, o + PT_SB + gp * 128: o + PT_SB + (gp + 1) * 128] = sb[2 * gp + 1][None, :]
    wst = np.ascontiguousarray(np.transpose(inp["sgu_w_s"], (0, 3, 1, 2)).reshape(L, 128, 512))

    def kblocks(W, nb):
        Lk, K, N = W.shape
        return np.ascontiguousarray(
            W.reshape(Lk, K // 128, 128, N // nb, nb).transpose(0, 3, 2, 1, 4).reshape(Lk, N // nb, 128, (K // 128) * nb))

    out = {"ctab": ct, "ptab": pt, "wst": wst}
    for k, pre in enumerate(("ffn1", "ffn2")):
        out["wg%d" % k] = kblocks(inp[pre + "_w_gate"], 256)
        out["wu%d" % k] = kblocks(inp[pre + "_w_up"], 256)
        wd = inp[pre + "_w_down"]
        wdr = wd.reshape(L, 2, 11, 128, 8, 128).transpose(0, 4, 1, 3, 2, 5)
        out["wd%d" % k] = np.ascontiguousarray(wdr.reshape(L, 16, 128, 1408))
    w_in = inp["w_in"]
    colsF = []
    for c in range(4):
        colsF.append(np.concatenate([np.arange(c * 64, c * 64 + 64), np.arange((4 + c) * 64, (4 + c) * 64 + 64)]))
    colsF.append(np.arange(512, 640))
    for ch in range(2):
        colsF.append(np.arange(768 + ch * 128, 768 + ch * 128 + 128))
    for ch in range(2):
        colsF.append(np.arange(1024 + ch * 128, 1024 + ch * 128 + 128))
    for gp in range(2):
        colsF.append(np.arange(1808 + gp * 128, 1808 + gp * 128 + 128))
    winF = np.stack([w_in[:, :, cols] for cols in colsF], axis=1)
    out["winF"] = np.ascontiguousarray(
        winF.reshape(L, 11, 8, 128, 128).transpose(0, 1, 3, 2, 4).reshape(L, 11, 128, 1024))
    cols0 = np.concatenate([np.arange(640, 768), np.arange(1792, 1808)])
    w0 = w_in[:, :, cols0]
    out["winT0"] = np.ascontiguousarray(w0.reshape(L, 8, 128, 144).transpose(0, 2, 1, 3).reshape(L, 128, 8 * 144))
    tb = [np.arange(2064, 2320), np.arange(1024, 1280), np.arange(1280, 1536), np.arange(1536, 1792)]
    wT = np.stack([w_in[:, :, cols] for cols in tb], axis=1)
    out["winT"] = np.ascontiguousarray(wT.reshape(L, 4, 8, 128, 256).transpose(0, 1, 3, 2, 4).reshape(L, 4, 128, 2048))
    w_out = inp["w_out"]
    p = np.arange(128)
    rows = []
    for c in range(4):
        rows.append(np.where(p < 64, c * 64 + p, (4 + c) * 64 + (p - 64)))
    for ch in range(2):
        rows.append(512 + ch * 128 + p)
    for gp in range(2):
        rows.append(768 + gp * 128 + p)
    woP = np.stack([w_out[:, r, :] for r in rows], axis=1)
    woP = woP.reshape(L, 8, 128, 8, 128)
    a = woP[:, [0, 1, 2, 3, 6, 7]]
    a = a.transpose(0, 3, 2, 1, 4)
    out["woA"] = np.ascontiguousarray(a.reshape(L, 4, 2, 128, 6, 128).transpose(0, 1, 3, 2, 4, 5).reshape(L, 4, 128, 1536))
    b = woP[:, [4, 5]].transpose(0, 2, 3, 1, 4)
    out["woB"] = np.ascontiguousarray(b.reshape(L, 128, 2048))
    return {k: np.ascontiguousarray(v, dtype=np.float32) for k, v in out.items()}


_CACHE = {}


def _get_program(key, layers, L, **kw):
    if key not in _CACHE:
        _CACHE[key] = build_program(layers, L, **kw)
    return _CACHE[key]


FUSED = True


def kernel(**inputs):
    inp = {k: np.asarray(v) for k, v in inputs.items()}
    L = inp["w_in"].shape[0]
    shared = _prep_shared(inp)
    x = inp["x"].astype(np.float32, copy=False)
    pos = inp["positions"].astype(np.int32, copy=False)
    xT = [np.ascontiguousarray(x[b].T) for b in range(NCORES)]
    if FUSED:
        nc = _get_program("fused", list(range(L)), L)
        in_maps = []
        for b in range(NCORES):
            m = dict(shared)
            m["xT"] = xT[b]
            m["pos"] = np.ascontiguousarray(pos[b:b + 1])
            in_maps.append(m)
        res = run_bass_kernel_spmd(nc, in_maps, core_ids=list(range(NCORES)))
        outs = [res.results[b]["outT"] for b in range(NCORES)]
    else:
        cur = xT
        for l in range(L):
            nc = _get_program("layer%d" % l, [l], L)
            in_maps = []
            for b in range(NCORES):
                m = dict(shared)
                m["xT"] = cur[b]
                m["pos"] = np.ascontiguousarray(pos[b:b + 1])
                in_maps.append(m)
            res = run_bass_kernel_spmd(nc, in_maps, core_ids=list(range(NCORES)))
            cur = [np.ascontiguousarray(res.results[b]["outT"]) for b in range(NCORES)]
        outs = cur
    return np.stack([np.ascontiguousarray(o.T) for o in outs], axis=0).astype(np.float32)
```

```python
import numpy as np
import concourse.bass as bass
import concourse.mybir as mybir
from concourse.bass_utils import run_bass_kernel_spmd

F32 = mybir.dt.float32
BF16 = mybir.dt.bfloat16
I32 = mybir.dt.int32
AF = mybir.ActivationFunctionType
ALU = mybir.AluOpType
AX = mybir.AxisListType

D = 1024
S = 2048
DFF = 2816
NFC = DFF // 128
DIN = 2320
EPS = 1e-6
NCORES = 8
PI = 3.14159265358979
TWO_PI = 2.0 * PI

PT_GN = 0
PT_GQ = 32
PT_GK = 33
PT_SINK = 34
PT_GB = 38
PT_HG = 54
PT_SG = 310
PT_SB = 566
NPT = 822

C_ID = 0
C_TRIF = 128
C_TRIB = 256
C_ONES = 384
C_BD = 512
C_PERM = 640
C_FREQ = 768
C_SIGN = 769
NCT = 770


class Op:
    __slots__ = ("eng", "fn", "deps", "signal", "sigval", "dsem", "dval", "idx")


class _Rec:
    def __getattr__(self, name):
        def f(*a, **k):
            self.call = (name, a, k)
            return self
        return f


class Sched:
    def __init__(self, nc):
        self.nc = nc
        self.E = {"pe": nc.tensor, "act": nc.scalar, "dve": nc.vector, "pool": nc.gpsimd, "sp": nc.sync}
        self.ops = []
        self.lastw = {}
        self.readers = {}
        self.slots = {}
        self.final_waits = []

    def _new(self, eng, fn, eager=True):
        o = Op()
        o.eng = eng
        if eager:
            rec = _Rec()
            fn(rec)
            name, a, k = rec.call
            o.fn = lambda e, name=name, a=a, k=k: getattr(e, name)(*a, **k)
        else:
            o.fn = fn
        o.deps = []
        o.signal = False
        o.sigval = 0
        o.dsem = None
        o.dval = 0
        o.idx = len(self.ops)
        self.ops.append(o)
        return o

    def _track(self, o, r, w):
        cand = {}
        for k in r:
            lw = self.lastw.get(k)
            if lw is not None:
                cand[lw.idx] = (lw, True)
            if isinstance(k, tuple) and k[0] == "ps":
                for rk, rd in self.readers.get(k, {}).items():
                    if rd.eng != o.eng and rd.idx not in cand:
                        cand[rd.idx] = (rd, False)
        for k in w:
            lw = self.lastw.get(k)
            if lw is not None and lw.idx not in cand:
                cand[lw.idx] = (lw, False)
            for rd in self.readers.get(k, {}).values():
                if rd.idx not in cand:
                    cand[rd.idx] = (rd, False)
        for idx in sorted(cand):
            d, raw = cand[idx]
            if d is o:
                continue
            if d.dsem is None and o.dsem is None and d.eng == o.eng:
                if o.eng == "pe":
                    continue
            if d.dsem is None:
                d.signal = True
            o.deps.append(d)
        for k in r:
            rk = o.eng if o.dsem is None else ("dma", o.idx)
            self.readers.setdefault(k, {})[rk] = o
        for k in w:
            self.lastw[k] = o
            self.readers[k] = {}

    def op(self, eng, fn, r=(), w=()):
        o = self._new(eng, fn)
        self._track(o, r, w)
        return o

    def dma(self, eng, out, in_, r, w, slot, final=False):
        o = self._new(eng, lambda e: e.dma_start(out=out, in_=in_), eager=False)
        if slot not in self.slots:
            self.slots[slot] = [self.nc.alloc_semaphore("dq_" + str(len(self.slots))), 0]
        sl = self.slots[slot]
        sl[1] += 16
        o.dsem = sl[0]
        o.dval = sl[1]
        self._track(o, r, w)
        if final:
            self.final_waits.append(o)
        return o

    def adopt(self, new_keys, old_keys):
        ops = {}
        for k in old_keys:
            lw = self.lastw.get(k)
            if lw is not None:
                ops[("w", lw.idx)] = lw
            for rk, rd in self.readers.get(k, {}).items():
                ops[("r", rd.idx)] = rd
        for k in new_keys:
            d = self.readers.setdefault(k, {})
            for o in ops.values():
                key = o.eng if o.dsem is None else ("dma", o.idx)
                if key in d and d[key].idx >= o.idx:
                    continue
                d[key] = o

    def finalize(self):
        nc = self.nc
        sem = {e: nc.alloc_semaphore("eng_" + e) for e in ("pe", "act", "dve", "pool")}
        cnt = {e: 0 for e in sem}
        waited = {}
        for o in self.ops:
            e = self.E[o.eng]
            for d in o.deps:
                if d.dsem is not None:
                    s, v = d.dsem, d.dval
                else:
                    s, v = sem[d.eng], d.sigval
                    assert v > 0
                key = (o.eng, s.num)
                if waited.get(key, 0) >= v:
                    continue
                e.wait_ge(s, v)
                waited[key] = v
            inst = o.fn(e)
            if o.dsem is not None:
                inst.then_inc(o.dsem, 16)
            elif o.signal:
                cnt[o.eng] += 1
                o.sigval = cnt[o.eng]
                inst.then_inc(sem[o.eng], 1)
        for o in self.final_waits:
            key = ("sp", o.dsem.num)
            if waited.get(key, 0) >= o.dval:
                continue
            nc.sync.wait_ge(o.dsem, o.dval)
            waited[key] = o.dval


def fap(base, dims):
    return bass.AP(base.tensor, base.offset, [list(base.ap[0])] + [list(d) for d in dims])


def build_program(layers, n_layers_total, stop_after=None, dumps=None):
    nc = bass.Bass("TRN2", target_bir_lowering=False)
    L = n_layers_total
    SC = Sched(nc)

    def dram(name, shape, dt, out=False):
        return nc.dram_tensor(name, list(shape), dt, kind="ExternalOutput" if out else "ExternalInput").ap()

    d_xT = dram("xT", [D, S], F32)
    d_pos = dram("pos", [1, S], I32)
    d_ct = dram("ctab", [128, NCT], F32)
    d_pt = dram("ptab", [128, L * NPT], F32)
    d_wst = dram("wst", [L, 128, 512], F32)
    d_wg = [dram("wg%d" % k, [L, 11, 128, 2048], F32) for k in range(2)]
    d_wu = [dram("wu%d" % k, [L, 11, 128, 2048], F32) for k in range(2)]
    d_wd = [dram("wd%d" % k, [L, 16, 128, 1408], F32) for k in range(2)]
    d_winF = dram("winF", [L, 11, 128, 1024], F32)
    d_winT0 = dram("winT0", [L, 128, 8 * 144], F32)
    d_winT = dram("winT", [L, 4, 128, 2048], F32)
    d_woA = dram("woA", [L, 4, 128, 1536], F32)
    d_woB = dram("woB", [L, 128, 2048], F32)
    d_out = dram("outT", [D, S], F32, out=True)

    xT = nc.alloc_sbuf_tensor("xT_sb", [128, 8, S], F32)
    ring = nc.alloc_sbuf_tensor("ring", [128, 6, 2048], BF16)
    ropeC = nc.alloc_sbuf_tensor("ropeC", [128, S], BF16)
    ropeS = nc.alloc_sbuf_tensor("ropeS", [128, S], BF16)
    ctab = nc.alloc_sbuf_tensor("ctab_sb", [128, NCT], F32)
    cbf = nc.alloc_sbuf_tensor("cbf", [128, 768], BF16)
    ptab = nc.alloc_sbuf_tensor("ptab_sb", [128, L * NPT], F32)
    wst = nc.alloc_sbuf_tensor("wst_sb", [128, L, 512], BF16)
    NTMP = 8
    tmpt = nc.alloc_sbuf_tensor("tmp", [128, NTMP, 512], F32)
    small = nc.alloc_sbuf_tensor("small", [128, 256], F32)
    AR_BYTES = 81920
    arena = nc.alloc_sbuf_tensor("arena", [128, AR_BYTES // 2], BF16)
    psb = [nc.alloc_psum_tensor("ps%d" % i, [128, 512], F32) for i in range(8)]

    def ar(off_bytes, shape, dt):
        n = int(np.prod(shape[1:]))
        if dt == BF16:
            base = arena[:, off_bytes // 2: off_bytes // 2 + n]
        else:
            base = arena[:, off_bytes // 2: off_bytes // 2 + 2 * n].bitcast(F32)
        dims = []
        st = 1
        for s_ in reversed(shape[1:]):
            dims.insert(0, [st, s_])
            st *= s_
        return fap(base, dims)

    hT = ar(0, [128, 8, S], BF16)
    act = ar(32768, [128, NFC, 1024], BF16)
    uT = ar(32768, [128, 2, S], BF16)
    vgn = ar(40960, [128, 16, 256], BF16)
    ysgT = ar(49152, [128, 2, S], BF16)
    yatT = ar(32768, [128, 4, S], BF16)
    qTa = ar(57344, [128, 4, S], BF16)
    kTa = ar(73728, [128, S], BF16)
    va = ar(77824, [128, 16, 128], BF16)
    qTm = ar(32768, [128, 2, S], BF16)
    kTm = ar(40960, [128, 2, S], BF16)
    km = ar(49152, [128, 16, 256], BF16)
    vaug = ar(57344, [128, 16, 4, 65], BF16)
    GO = ar(65664, [128, 16, 256], BF16)
    SPt = ar(74880, [128, 128], F32)
    Ut = ar(75392, [128, 128], F32)
    Et = ar(75904, [128, 128], F32)
    FLt = ar(76416, [128, 128], F32)
    Atab = ar(76928, [128, 128], F32)
    Sst = ar(77440, [128, 2, 260], F32)
    Sbf = ar(79520, [128, 2, 260], BF16)
    hm = ar(0, [128, 16, 256], F32)
    ymT = ar(16384, [128, 2, S], BF16)
    rowt = ar(24576, [128, 512], F32)
    Qblk = ar(26624, [128, 2, 512], BF16)

    kX = lambda c, bt: ("x", c, bt)
    kH = lambda c, bt: ("h", c, bt)
    ALLH = [kH(c, bt) for c in range(8) for bt in range(4)]
    kPS = lambda b: ("ps", b)

    st = {"ps": 0, "tmp": 0, "ring": 0}

    def ps_next():
        b = st["ps"]
        st["ps"] = (b + 1) % 8
        return b

    def tmp_next():
        i = st["tmp"]
        st["tmp"] = (i + 1) % NTMP
        return tmpt[:, i, :], ("tmp", i)

    def tmp_bf(tile_):
        return tile_.bitcast(BF16)

    PRE = {}

    def _wissue(src2d, n):
        i = st["ring"]
        st["ring"] = (i + 1) % 6
        dst = ring[:, i, 0:n]
        SC.dma("pool", dst, src2d, r=[], w=[("ring", i)], slot=("ring", i))
        return ring[:, i, :], ("ring", i)

    def prefetch(tag, src2d, n):
        PRE[tag] = _wissue(src2d, n)

    def wload(src2d, n, tag=None):
        if tag is not None and tag in PRE:
            return PRE.pop(tag)
        return _wissue(src2d, n)

    SC.dma("sp", ctab[:, :], d_ct, r=[], w=["ctab"], slot="ct")
    SC.dma("sp", ptab[:, :], d_pt, r=[], w=["ptab"], slot="pt")
    SC.dma("pool", cbf[:, :], d_ct[:, 0:768], r=[], w=["cbf"], slot="cbf")
    for l in range(L):
        SC.dma("pool", wst[:, l, :], d_wst[l], r=[], w=["wst"], slot="wst%d" % l)
    for hf in range(2):
        for c in range(8):
            SC.dma("sp" if hf == 0 else "act", xT[:, c, hf * 1024:(hf + 1) * 1024],
                   d_xT[c * 128:(c + 1) * 128, hf * 1024:(hf + 1) * 1024],
                   r=[], w=[kX(c, 2 * hf), kX(c, 2 * hf + 1)], slot="x%d_%d" % (c, hf))

    ident_f = ctab[:, C_ID:C_ID + 128]
    triF_f = ctab[:, C_TRIF:C_TRIF + 128]
    triB_f = ctab[:, C_TRIB:C_TRIB + 128]
    ones_f = ctab[:, C_ONES:C_ONES + 128]
    ident_b = cbf[:, C_ID:C_ID + 128]
    triF_b = cbf[:, C_TRIF:C_TRIF + 128]
    triB_b = cbf[:, C_TRIB:C_TRIB + 128]
    ones_b = cbf[:, C_ONES:C_ONES + 128]
    bd_b = cbf[:, C_BD:C_BD + 128]
    perm_b = cbf[:, C_PERM:C_PERM + 128]

    def pcol(l, col, n=1):
        return ptab[:, l * NPT + col: l * NPT + col + n]

    def build_rope():
        for pc in range(4):
            sl = slice(pc * 512, (pc + 1) * 512)
            ti, ki = tmp_next()
            posi = ti.bitcast(I32)
            SC.dma("sp", posi, bass.AP(d_pos.tensor, pc * 512, [[0, 128], [1, 512]]),
                   r=[], w=[ki], slot="pos%d" % pc)
            ta, ka = tmp_next()
            SC.op("dve", lambda e, o=ta, i=posi: e.tensor_copy(o, i), r=[ki], w=[ka])
            SC.op("dve", lambda e, o=ta: e.tensor_scalar(o, o, ctab[:, C_FREQ:C_FREQ + 1], None, ALU.mult),
                  r=[ka, "ctab"], w=[ka])
            ty, ky = tmp_next()
            SC.op("dve", lambda e, o=ty, i=ta: e.tensor_scalar(o, i, 1.0 / TWO_PI, None, ALU.mult), r=[ka], w=[ky])
            tk, kk = tmp_next()
            tki = tk.bitcast(I32)
            SC.op("dve", lambda e, o=tki, i=ty: e.tensor_copy(o, i), r=[ky], w=[kk])
            SC.op("dve", lambda e, o=ty, i=tki: e.tensor_copy(o, i), r=[kk], w=[ky])
            C1 = 6.28125
            C2 = TWO_PI - C1
            SC.op("dve", lambda e, o=ta, k_=ty: e.scalar_tensor_tensor(o, k_, -C1, o, ALU.mult, ALU.add),
                  r=[ka, ky], w=[ka])
            SC.op("dve", lambda e, o=ta, k_=ty: e.scalar_tensor_tensor(o, k_, -C2, o, ALU.mult, ALU.add),
                  r=[ka, ky], w=[ka])

            def wrap(t_, kt, m_, km_):
                SC.op("dve", lambda e: e.tensor_single_scalar(m_, t_, PI, ALU.is_gt), r=[kt], w=[km_])
                SC.op("dve", lambda e: e.scalar_tensor_tensor(t_, m_, -TWO_PI, t_, ALU.mult, ALU.add),
                      r=[kt, km_], w=[kt])
                SC.op("dve", lambda e: e.tensor_single_scalar(m_, t_, -PI, ALU.is_lt), r=[kt], w=[km_])
                SC.op("dve", lambda e: e.scalar_tensor_tensor(t_, m_, TWO_PI, t_, ALU.mult, ALU.add),
                      r=[kt, km_], w=[kt])
                SC.op("dve", lambda e: e.tensor_scalar(t_, t_, -3.1415925, 3.1415925, ALU.max, ALU.min),
                      r=[kt], w=[kt])

            wrap(ta, ka, tk, kk)
            SC.op("act", lambda e, i=ta: e.activation(ropeS[:, sl], i, AF.Sin, scale=ctab[:, C_SIGN:C_SIGN + 1]),
                  r=[ka, "ctab"], w=[("ropeS", pc)])
            SC.op("dve", lambda e, o=ty, i=ta: e.tensor_scalar(o, i, PI / 2, None, ALU.add), r=[ka], w=[ky])
            wrap(ty, ky, tk, kk)
            SC.op("act", lambda e, i=ty: e.activation(ropeC[:, sl], i, AF.Sin), r=[ky], w=[("ropeC", pc)])

    def rms_norm(l, which, dst, bts=(0, 1, 2, 3)):
        for bt in bts:
            tsl = slice(bt * 512, (bt + 1) * 512)
            b = ps_next()
            for c in range(8):
                tq, kq = tmp_next()
                sq = tmp_bf(tq)[:, 0:512]
                SC.op("act", lambda e, o=sq, c=c: e.activation(o, xT[:, c, tsl], AF.Square),
                      r=[kX(c, bt)], w=[kq])
                SC.op("pe", lambda e, b=b, i=sq, c=c: e.matmul(psb[b][:, :], ones_b, i, start=(c == 0), stop=(c == 7)),
                      r=[kq, "cbf"], w=[kPS(b)])
            tl, kl = tmp_next()
            SC.op("act", lambda e, o=tl, b=b: e.activation(o, psb[b][:, :], AF.Ln, bias=EPS, scale=1.0 / D),
                  r=[kPS(b)], w=[kl])
            SC.op("act", lambda e, o=tl: e.activation(o, o, AF.Exp, scale=-0.5), r=[kl], w=[kl])
            for c in range(8):
                gcol = pcol(l, PT_GN + which * 8 + c)
                if dst == "h":
                    SC.op("dve", lambda e, c=c, g=gcol, rs=tl: e.scalar_tensor_tensor(
                        hT[:, c, tsl], xT[:, c, tsl], g, rs, ALU.mult, ALU.mult),
                        r=[kX(c, bt), kl, "ptab"], w=[kH(c, bt)])
                else:
                    SC.op("dve", lambda e, c=c, g=gcol, rs=tl: e.scalar_tensor_tensor(
                        xT[:, c, tsl], xT[:, c, tsl], g, rs, ALU.mult, ALU.mult),
                        r=[kX(c, bt), kl, "ptab"], w=[kX(c, bt)])

    def ffn(l, k, hooks=(None, None)):
        kA = lambda f, s_: ("act", f, s_)
        for hf in range(2):
            for fb in range(11):
                if fb == 2 and hooks[hf] is not None:
                    hooks[hf]()
                wg, kwg = wload(d_wg[k][l, fb], 2048, tag=("wg", k, l, fb, hf))
                wu, kwu = wload(d_wu[k][l, fb], 2048, tag=("wu", k, l, fb, hf))
                for fc in range(2):
                    f = fb * 2 + fc
                    bg = [ps_next(), ps_next()]
                    bu = [ps_next(), ps_next()]
                    for (wt, kw, bb) in ((wg, kwg, bg), (wu, kwu, bu)):
                        for c in range(8):
                            lhsT = wt[:, c * 256 + fc * 128: c * 256 + fc * 128 + 128]
                            for s_ in range(2):
                                bt = hf * 2 + s_
                                SC.op("pe", lambda e, o=psb[bb[s_]], a=lhsT, c=c, bt=bt: e.matmul(
                                    o[:, :], a, hT[:, c, bt * 512:(bt + 1) * 512], start=(c == 0), stop=(c == 7)),
                                    r=[kw, kH(c, bt)], w=[kPS(bb[s_])])
                    for s_ in range(2):
                        tg, kg = tmp_next()
                        SC.op("act", lambda e, o=tg, b=bg[s_]: e.activation(o, psb[b][:, :], AF.Silu),
                              r=[kPS(bg[s_])], w=[kg])
                        SC.op("dve", lambda e, i=tg, b=bu[s_], f=f, s_=s_: e.tensor_tensor(
                            act[:, f, s_ * 512:(s_ + 1) * 512], i, psb[b][:, :], ALU.mult),
                            r=[kg, kPS(bu[s_])], w=[kA(f, s_)])
            for dc in range(8):
                wd0, kw0 = wload(d_wd[k][l, dc * 2], 1408)
                wd1, kw1 = wload(d_wd[k][l, dc * 2 + 1], 1408)
                bo = [ps_next(), ps_next()]
                for f in range(NFC):
                    wt, kw = (wd0, kw0) if f < 11 else (wd1, kw1)
                    lhsT = wt[:, (f % 11) * 128:(f % 11) * 128 + 128]
                    for s_ in range(2):
                        SC.op("pe", lambda e, o=psb[bo[s_]], a=lhsT, f=f, s_=s_: e.matmul(
                            o[:, :], a, act[:, f, s_ * 512:(s_ + 1) * 512], start=(f == 0), stop=(f == NFC - 1)),
                            r=[kw, kA(f, s_)], w=[kPS(bo[s_])])
                for s_ in range(2):
                    bt = hf * 2 + s_
                    SC.op("dve", lambda e, b=bo[s_], dc=dc, bt=bt: e.scalar_tensor_tensor(
                        xT[:, dc, bt * 512:(bt + 1) * 512], psb[b][:, :], 0.5, xT[:, dc, bt * 512:(bt + 1) * 512],
                        ALU.mult, ALU.add),
                        r=[kPS(bo[s_]), kX(dc, bt)], w=[kX(dc, bt)])

    ACTKEYS = [("act", f, s_) for f in range(NFC) for s_ in range(2)]

    PEND = []

    def flush_pending():
        while PEND:
            PEND.pop(0)()

    def fproj(l, chunk, epilogue):
        wt, kw = wload(d_winF[l, chunk], 1024)
        for bt in range(4):
            b = ps_next()
            for c in range(8):
                SC.op("pe", lambda e, b=b, c=c, bt=bt: e.matmul(
                    psb[b][:, :], wt[:, c * 128:(c + 1) * 128], hT[:, c, bt * 512:(bt + 1) * 512],
                    start=(c == 0), stop=(c == 7)), r=[kw, kH(c, bt)], w=[kPS(b)])
            flush_pending()
            cont = epilogue(bt, b)
            if cont is not None:
                PEND.append(cont)

    def tproj(l, src, ncols, epilogue):
        wt, kw = wload(src, 8 * ncols)
        for tt in range(16):
            b = ps_next()
            for c in range(8):
                SC.op("pe", lambda e, b=b, c=c, tt=tt: e.matmul(
                    psb[b][:, 0:ncols], hT[:, c, tt * 128:(tt + 1) * 128], wt[:, c * ncols:(c + 1) * ncols],
                    start=(c == 0), stop=(c == 7)), r=[kw, kH(c, tt // 4)], w=[kPS(b)])
            flush_pending()
            epilogue(tt, b)

    def wout_pass(l, chunks, ysrc, src_blocks, ndc_per_blk, bt_hook=None):
        nmc = len(chunks)
        for blk in range(8 // ndc_per_blk):
            wt, kw = wload(src_blocks(blk), ndc_per_blk * nmc * 128, tag=("wo", l, nmc, blk))
            order = ([(dci, bt) for bt in range(4) for dci in range(ndc_per_blk)] if bt_hook is not None
                     else [(dci, bt) for dci in range(ndc_per_blk) for bt in range(4)])
            for oi, (dci, bt) in enumerate(order):
                dc = blk * ndc_per_blk + dci
                if True:
                    b = ps_next()
                    for i, mc in enumerate(chunks):
                        rhs, kr = ysrc(mc, bt)
                        o0 = (dci * nmc + i) * 128
                        SC.op("pe", lambda e, b=b, o0=o0, rhs=rhs, i=i: e.matmul(
                            psb[b][:, :], wt[:, o0:o0 + 128], rhs, start=(i == 0), stop=(i == nmc - 1)),
                            r=[kw, kr], w=[kPS(b)])
                    SC.op("dve", lambda e, b=b, dc=dc, bt=bt: e.tensor_tensor(
                        xT[:, dc, bt * 512:(bt + 1) * 512], psb[b][:, :], xT[:, dc, bt * 512:(bt + 1) * 512], ALU.add),
                        r=[kPS(b), kX(dc, bt)], w=[kX(dc, bt)])
                    if bt_hook is not None and dci == ndc_per_blk - 1:
                        bt_hook(bt)

    def mixer(l, bt_hook=None):
        kU = lambda gp, bt: ("uT", gp, bt)
        kVG = lambda tt: ("vgn", tt)
        kYS = lambda gp, bt: ("ysg", gp, bt)
        SC.adopt([kU(g_, b_) for g_ in range(2) for b_ in range(4)] + [kVG(t_) for t_ in range(16)]
                 + [kYS(g_, b_) for g_ in range(2) for b_ in range(4)], ACTKEYS)
        kQ = lambda c, bt: ("qTa", c, bt)
        kK = lambda bt: ("kTa", bt)
        kV = lambda tt: ("va", tt)
        SC.adopt([kQ(c, b_) for c in range(4) for b_ in range(4)] + [kK(b_) for b_ in range(4)]
                 + [kV(t_) for t_ in range(16)], ACTKEYS)

        for gp in range(2):
            def ep_u(bt, b, gp=gp):
                SC.op("act", lambda e: e.activation(uT[:, gp, bt * 512:(bt + 1) * 512], psb[b][:, :], AF.Gelu),
                      r=[kPS(b)], w=[kU(gp, bt)])
            fproj(l, 9 + gp, ep_u)

        sgug = pcol(l, PT_SG, 256)

        def ep_sv(tt, b):
            tv, kv_ = tmp_next()
            vg = tv[:, 0:256]
            vsq = tv[:, 256:512]
            SC.op("act", lambda e: e.activation(vg, psb[b][:, 0:256], AF.Gelu), r=[kPS(b)], w=[kv_])
            SC.op("dve", lambda e: e.tensor_tensor(vsq, vg, vg, ALU.mult), r=[kv_], w=[kv_])
            ks_ = ("small", "sgu")
            ms = small[:, 0:4]
            SC.op("dve", lambda e: e.tensor_reduce(ms, fap(vsq, [[64, 4], [1, 64]]), AX.X, ALU.add), r=[kv_], w=[ks_])
            SC.op("act", lambda e: e.activation(ms, ms, AF.Ln, bias=EPS, scale=1.0 / 64), r=[ks_], w=[ks_])
            SC.op("act", lambda e: e.activation(ms, ms, AF.Exp, scale=-0.5), r=[ks_], w=[ks_])
            SC.op("dve", lambda e: e.tensor_tensor(fap(vg, [[64, 4], [1, 64]]), fap(vg, [[64, 4], [1, 64]]),
                                                   fap(ms, [[1, 4], [0, 64]]), ALU.mult), r=[kv_, ks_], w=[kv_])
            SC.op("dve", lambda e: e.tensor_tensor(vgn[:, tt, :], vg, sgug, ALU.mult), r=[kv_, "ptab"], w=[kVG(tt)])
        tproj(l, d_winT[l, 0], 256, ep_sv)

        for gp in range(2):
            for bt in range(4):
                b = ps_next()
                for ci in range(4):
                    tt = bt * 4 + ci
                    for gi in range(2):
                        g = gp * 2 + gi
                        SC.op("pe", lambda e, b=b, ci=ci, tt=tt, gi=gi, g=g: e.matmul(
                            psb[b][gi * 64:(gi + 1) * 64, ci * 128:(ci + 1) * 128],
                            vgn[:, tt, g * 64:(g + 1) * 64], wst[:, l, g * 128:(g + 1) * 128],
                            start=True, stop=True), r=[kVG(tt), "wst"], w=[kPS(b)])
                tb, kb = tmp_next()
                biasT = pcol(l, PT_SB + gp * 128, 128)
                SC.op("dve", lambda e, b=b, tb=tb, biasT=biasT: e.tensor_tensor(
                    fap(tb, [[128, 4], [1, 128]]), fap(psb[b][:, :], [[128, 4], [1, 128]]),
                    fap(biasT, [[0, 4], [1, 128]]), ALU.add), r=[kPS(b), "ptab"], w=[kb])
                SC.op("dve", lambda e, tb=tb, gp=gp, bt=bt: e.tensor_tensor(
                    ysgT[:, gp, bt * 512:(bt + 1) * 512], tb, uT[:, gp, bt * 512:(bt + 1) * 512], ALU.mult),
                    r=[kb, kU(gp, bt)], w=[kYS(gp, bt)])
        if stop_after == "sgu":
            return

        kYA = lambda c, bt: ("yat", c, bt)
        SC.adopt([kYA(c, b_) for c in range(4) for b_ in range(4)],
                 [kU(g_, b_) for g_ in range(2) for b_ in range(4)] + [kVG(t_) for t_ in range(16)])

        def qk_epilogue(dst_fn, kdst_fn, gcol):
            def ep(bt, b):
                tsl = slice(bt * 512, (bt + 1) * 512)
                tq, kq = tmp_next()
                ts, ks = tmp_next()
                sqb = tmp_bf(ts)[:, 0:512]
                qgb = tmp_bf(ts)[:, 512:1024]
                SC.op("act", lambda e: e.activation(sqb, psb[b][:, :], AF.Square), r=[kPS(b)], w=[ks])
                SC.op("act", lambda e: e.activation(tq, psb[b][:, :], AF.Copy, scale=gcol), r=[kPS(b), "ptab"], w=[kq])
                SC.op("dve", lambda e: e.tensor_copy(qgb, tq), r=[kq], w=[ks])

                def part2():
                    bm = ps_next()
                    SC.op("pe", lambda e: e.matmul(psb[bm][:, :], bd_b, sqb, start=True, stop=True),
                          r=[ks, "cbf"], w=[kPS(bm)])
                    bp = ps_next()
                    SC.op("pe", lambda e: e.matmul(psb[bp][:, :], perm_b, qgb, start=True, stop=True),
                          r=[ks, "cbf"], w=[kPS(bp)])
                    tr, kr = tmp_next()
                    SC.op("act", lambda e: e.activation(tr, psb[bm][:, :], AF.Ln, bias=EPS, scale=1.0 / 64),
                          r=[kPS(bm)], w=[kr])
                    SC.op("act", lambda e: e.activation(tr, tr, AF.Exp, scale=-0.5), r=[kr], w=[kr])
                    t2, k2 = tmp_next()
                    SC.op("dve", lambda e: e.tensor_tensor(t2, psb[bp][:, :], ropeS[:, tsl], ALU.mult),
                          r=[kPS(bp), ("ropeS", bt)], w=[k2])
                    SC.op("dve", lambda e: e.tensor_tensor(tq, tq, ropeC[:, tsl], ALU.mult), r=[kq, ("ropeC", bt)], w=[kq])
                    SC.op("dve", lambda e: e.tensor_tensor(tq, tq, t2, ALU.add), r=[kq, k2], w=[kq])
                    SC.op("dve", lambda e: e.tensor_tensor(dst_fn(bt), tq, tr, ALU.mult), r=[kq, kr], w=[kdst_fn(bt)])
                return part2
            return ep

        for c in range(4):
            fproj(l, c, qk_epilogue(lambda bt, c=c: qTa[:, c, bt * 512:(bt + 1) * 512], lambda bt, c=c: kQ(c, bt),
                                    pcol(l, PT_GQ)))
        fproj(l, 4, qk_epilogue(lambda bt: kTa[:, bt * 512:(bt + 1) * 512], lambda bt: kK(bt), pcol(l, PT_GK)))

        kG = lambda tt: ("G", tt)
        gbias = pcol(l, PT_GB, 16)

        def ep_t0(tt, b):
            SC.op("act", lambda e: e.activation(va[:, tt, :], psb[b][:, 0:128], AF.Copy), r=[kPS(b)], w=[kV(tt)])
            SC.op("dve", lambda e: e.tensor_tensor(Gs[:, tt, :], psb[b][:, 128:144], gbias, ALU.add),
                  r=[kPS(b), "ptab"], w=[kG(tt)])
        tproj(l, d_winT0[l], 144, ep_t0)

        flush_pending()
        for blk in range(4):
            prefetch(("wo", l, 6, blk), d_woA[l, blk], 1536)
        esink = small[:, 8:12]
        SC.op("act", lambda e: e.activation(esink, pcol(l, PT_SINK, 4), AF.Exp), r=["ptab"], w=[("small", "esink")])

        aunits = [(n, hp) for n in range(16) for hp in range(2)]
        AC = [dict() for _ in aunits]
        OD = {}

        def stS(ui):
            n, hp = aunits[ui]
            prs = slice(hp * 64, (hp + 1) * 64)
            js = [j for j in (n - 1, n, n + 1) if 0 <= j < 16]
            ptiles = []
            for j in js:
                bs = ps_next()
                rhs = fap(qTa[prs, 0, n * 128:(n + 1) * 128], [[S, 4], [1, 128]])
                SC.op("pe", lambda e: e.matmul(
                    psb[bs][:, :], kTa[prs, j * 128:(j + 1) * 128], rhs, start=True, stop=True),
                    r=[kK(j // 4)] + [kQ(c, n // 4) for c in range(4)], w=[kPS(bs)])
                tp, kp = tmp_next()
                pt = tmp_bf(tp)[:, 0:512]
                SC.op("act", lambda e: e.activation(pt, psb[bs][:, :], AF.Exp, scale=0.125),
                      r=[kPS(bs)], w=[kp])
                if j != n:
                    msk = triB_b if j == n - 1 else triF_b
                    SC.op("pool", lambda e: e.tensor_tensor(
                        fap(pt, [[128, 4], [1, 128]]), fap(pt, [[128, 4], [1, 128]]),
                        fap(msk, [[0, 4], [1, 128]]), ALU.mult), r=[kp, "cbf"], w=[kp])
                ptiles.append((j, pt, kp))
            AC[ui]["pt"] = ptiles

        def stP(ui):
            n, hp = aunits[ui]
            prs = slice(hp * 64, (hp + 1) * 64)
            if hp == 0:
                OD[n] = (ps_next(), ps_next())
            bO, bD = OD[n]
            ptiles = AC[ui]["pt"]
            for i, (j, pt, kp) in enumerate(ptiles):
                SC.op("pe", lambda e: e.matmul(
                    psb[bO][prs, :], va[:, j, hp * 64:(hp + 1) * 64], pt,
                    start=(i == 0), stop=(i == len(ptiles) - 1)), r=[kV(j), kp], w=[kPS(bO)])
                SC.op("pe", lambda e: e.matmul(
                    psb[bD][prs, :], ones_b[:, 0:64], pt,
                    start=(i == 0), stop=(i == len(ptiles) - 1)), r=["cbf", kp], w=[kPS(bD)])
            if hp == 1:
                td, kd = tmp_next()
                SC.op("dve", lambda e: e.tensor_tensor(
                    fap(td, [[128, 4], [1, 128]]), fap(psb[bD][:, :], [[128, 4], [1, 128]]),
                    fap(esink, [[1, 4], [0, 128]]), ALU.add), r=[kPS(bD), ("small", "esink")], w=[kd])
                SC.op("dve", lambda e: e.reciprocal(td, td), r=[kd], w=[kd])
                SC.op("dve", lambda e: e.tensor_tensor(
                    fap(yatT[:, 0, n * 128:(n + 1) * 128], [[S, 4], [1, 128]]),
                    fap(psb[bO][:, :], [[128, 4], [1, 128]]), fap(td, [[128, 4], [1, 128]]), ALU.mult),
                    r=[kPS(bO), kd], w=[kYA(c, n // 4) for c in range(4)])

        def ysrc1(mc, bt):
            if mc < 4:
                return yatT[:, mc, bt * 512:(bt + 1) * 512], kYA(mc, bt)
            return ysgT[:, mc - 6, bt * 512:(bt + 1) * 512], kYS(mc - 6, bt)
        WO = {}
        WQ = []
        chunks1 = [0, 1, 2, 3, 6, 7]

        def wo_group(bt, dc):
            blk, dci = dc // 2, dc % 2
            if blk not in WO:
                WO[blk] = wload(d_woA[l, blk], 1536, tag=("wo", l, 6, blk))
            wt, kw = WO[blk]
            b = ps_next()
            for i, mc in enumerate(chunks1):
                rhs, kr = ysrc1(mc, bt)
                o0 = (dci * 6 + i) * 128
                SC.op("pe", lambda e: e.matmul(psb[b][:, :], wt[:, o0:o0 + 128], rhs, start=(i == 0), stop=(i == 5)),
                      r=[kw, kr], w=[kPS(b)])
            SC.op("dve", lambda e: e.tensor_tensor(
                xT[:, dc, bt * 512:(bt + 1) * 512], psb[b][:, :], xT[:, dc, bt * 512:(bt + 1) * 512], ALU.add),
                r=[kPS(b), kX(dc, bt)], w=[kX(dc, bt)])

        do_wo = False
        for step in range(len(aunits) + 1):
            if step < len(aunits):
                stS(step)
            if step >= 1:
                stP(step - 1)
                n_, hp_ = aunits[step - 1]
                if do_wo and hp_ == 1 and n_ % 4 == 3:
                    WQ.extend([(n_ // 4, dc) for dc in range(8)])
                for _ in range(2):
                    if WQ:
                        wo_group(*WQ.pop(0))
        while WQ:
            wo_group(*WQ.pop(0))
        if stop_after not in ("attn",):
            wout_pass(l, chunks1, ysrc1, lambda blk: d_woA[l, blk], 2)
        if stop_after == "attn":
            return

        if stop_after == "wout1":
            return

        old = ([kYA(c, b_) for c in range(4) for b_ in range(4)] + [kYS(g_, b_) for g_ in range(2) for b_ in range(4)]
               + [kQ(c, b_) for c in range(4) for b_ in range(4)] + [kK(b_) for b_ in range(4)]
               + [kV(t_) for t_ in range(16)])
        kQM = lambda ch, bt: ("qTm", ch, bt)
        kKM = lambda ch, bt: ("kTm", ch, bt)
        kKT = lambda tt: ("km", tt)
        kVA = lambda tt: ("vaug", tt)
        kGO = lambda tt: ("GO", tt)
        newk = ([kQM(ch, b_) for ch in range(2) for b_ in range(4)] + [kKM(ch, b_) for ch in range(2) for b_ in range(4)]
                + [kKT(t_) for t_ in range(16)] + [kVA(t_) for t_ in range(16)] + [kGO(t_) for t_ in range(16)]
                + ["mlmisc", ("Sst", 0), ("Sst", 1), ("Sbf", 0), ("Sbf", 1)])
        SC.adopt(newk, old)

        for ch in range(2):
            def ep_q(bt, b, ch=ch):
                SC.op("act", lambda e: e.activation(qTm[:, ch, bt * 512:(bt + 1) * 512], psb[b][:, :], AF.Copy),
                      r=[kPS(b)], w=[kQM(ch, bt)])
            fproj(l, 5 + ch, ep_q)
        for ch in range(2):
            def ep_k(bt, b, ch=ch):
                SC.op("act", lambda e: e.activation(kTm[:, ch, bt * 512:(bt + 1) * 512], psb[b][:, :], AF.Copy,
                                                    scale=0.125), r=[kPS(b)], w=[kKM(ch, bt)])
            fproj(l, 7 + ch, ep_k)

        def ep_mk(tt, b):
            SC.op("act", lambda e: e.activation(km[:, tt, :], psb[b][:, 0:256], AF.Copy, scale=0.125),
                  r=[kPS(b)], w=[kKT(tt)])
        tproj(l, d_winT[l, 1], 256, ep_mk)

        def ep_mv(tt, b):
            SC.op("dve", lambda e: e.tensor_copy(fap(vaug[:, tt, 0, 0:64], [[65, 4], [1, 64]]),
                                                 fap(psb[b][:, 0:256], [[64, 4], [1, 64]])), r=[kPS(b)], w=[kVA(tt)])
            SC.op("dve", lambda e: e.memset(fap(vaug[:, tt, 0, 64:65], [[65, 4], [1, 1]]), 1.0), r=[], w=[kVA(tt)])
        tproj(l, d_winT[l, 2], 256, ep_mv)

        hg = pcol(l, PT_HG, 256)

        def ep_mo(tt, b):
            tm_, km_ = tmp_next()
            t = tm_[:, 0:256]
            SC.op("act", lambda e: e.activation(t, psb[b][:, 0:256], AF.Sigmoid), r=[kPS(b)], w=[km_])
            SC.op("dve", lambda e: e.tensor_tensor(GO[:, tt, :], t, hg, ALU.mult), r=[km_, "ptab"], w=[kGO(tt)])
        tproj(l, d_winT[l, 3], 256, ep_mo)
        kHM = lambda tt: ("hm", tt)
        kYM = lambda ch, bt: ("ymT", ch, bt)
        SC.adopt([kHM(t_) for t_ in range(16)] + [kYM(ch, b_) for ch in range(2) for b_ in range(4)] + ["rowt", ("qblk", 0), ("qblk", 1)], ALLH)

        KM = "mlmisc"
        def gview(types):
            t0, t1 = types
            return fap(Gs[:, 0, t0 * 4:t0 * 4 + 4], [[(t1 - t0) * 4, 2], [16, 16], [1, 4]])
        SPv = fap(SPt, [[64, 2], [4, 16], [1, 4]])
        Uv = fap(Ut, [[64, 2], [4, 16], [1, 4]])
        GK = [kG(t_) for t_ in range(16)]
        SC.op("act", lambda e: e.activation(SPv, gview((1, 3)), AF.Exp, scale=-1.0), r=GK, w=[KM])
        SC.op("act", lambda e: e.activation(SPt, SPt, AF.Ln, bias=1.0), r=[KM], w=[KM])
        bB = ps_next()
        SC.op("pe", lambda e: e.matmul(psb[bB][:, 0:64], triF_f, SPt[:, 0:64], start=True, stop=True),
              r=[KM, "ctab"], w=[kPS(bB)])
        SC.op("pe", lambda e: e.matmul(psb[bB][:, 64:128], triB_f, SPt[:, 64:128], start=True, stop=True),
              r=[KM, "ctab"], w=[kPS(bB)])
        SC.op("pe", lambda e: e.matmul(psb[bB][0:1, 128:256], ones_f[:, 0:1], SPt, start=True, stop=True),
              r=[KM, "ctab"], w=[kPS(bB)])
        SC.op("dve", lambda e: e.tensor_tensor(Uv, gview((0, 2)), fap(psb[bB][:, 0:128], [[64, 2], [4, 16], [1, 4]]),
                                               ALU.add), r=GK + [kPS(bB)], w=[KM])
        SC.op("act", lambda e: e.activation(FLt, psb[bB][:, 0:128], AF.Copy), r=[kPS(bB)], w=[KM])
        rows = small[0:1, 16:16 + 128]
        SC.op("act", lambda e: e.activation(rows, psb[bB][0:1, 128:256], AF.Copy), r=[kPS(bB)], w=[("small", "rows")])
        bT = ps_next()
        SC.op("pe", lambda e: e.transpose(psb[bT][:, 0:128], Ut, ident_f), r=[KM, "ctab"], w=[kPS(bT)])
        ucol = small[:, 12:13]
        SC.op("dve", lambda e: e.tensor_reduce(ucol, psb[bT][:, 0:128], AX.X, ALU.max), r=[kPS(bT)], w=[("small", "ucol")])
        SC.op("pe", lambda e: e.matmul(psb[bT][0:1, 128:256], ucol, ident_f, start=True, stop=True),
              r=[("small", "ucol"), "ctab"], w=[kPS(bT)])
        umax = rowt[0:1, 0:128]
        mprev = rowt[0:1, 128:256]
        mcr = rowt[0:1, 256:384]
        alr = rowt[0:1, 384:512]
        KR = "rowt"
        SC.op("act", lambda e: e.activation(umax, psb[bT][0:1, 128:256], AF.Copy), r=[kPS(bT)], w=[KR])
        SC.op("dve", lambda e: e.memset(mprev, 0.0), r=[], w=[KR])
        for dr in range(2):
            for h in range(4):
                if dr == 0:
                    sel = lambda t_: fap(t_[:, h:h + 1], [[4, 16]])
                else:
                    sel = lambda t_: fap(t_[:, 64 + 60 + h:64 + 60 + h + 1], [[-4, 16]])
                SC.op("dve", lambda e: e.tensor_tensor_scan(sel(mcr), sel(umax), sel(rows), 0.0, ALU.max, ALU.subtract),
                      r=[KR, ("small", "rows")], w=[KR])
        SC.op("dve", lambda e: e.tensor_copy(mprev[:, 4:64], mcr[:, 0:60]), r=[KR], w=[KR])
        SC.op("dve", lambda e: e.tensor_copy(mprev[:, 64:124], mcr[:, 68:128]), r=[KR], w=[KR])
        SC.op("dve", lambda e: e.tensor_tensor(mcr, mcr, rows, ALU.add), r=[KR, ("small", "rows")], w=[KR])
        SC.op("dve", lambda e: e.tensor_tensor(alr, mprev, mcr, ALU.subtract), r=[KR], w=[KR])
        SC.op("act", lambda e: e.activation(alr, alr, AF.Exp), r=[KR], w=[KR])
        bC = ps_next()
        SC.op("pe", lambda e: e.matmul(psb[bC][:, 0:256], ones_f[0:1, :], rowt[0:1, 256:512], start=True, stop=True),
              r=[KR, "ctab"], w=[kPS(bC)])
        SC.op("dve", lambda e: e.tensor_tensor(Et, Ut, psb[bC][:, 0:128], ALU.subtract), r=[KM, kPS(bC)], w=[KM])
        SC.op("act", lambda e: e.activation(Et, Et, AF.Exp), r=[KM], w=[KM])
        SC.op("dve", lambda e: e.tensor_tensor(FLt, FLt, psb[bC][:, 0:128], ALU.subtract), r=[KM, kPS(bC)], w=[KM])
        SC.op("act", lambda e: e.activation(FLt, FLt, AF.Exp), r=[KM], w=[KM])
        SC.op("dve", lambda e: e.memset(Atab, 0.0), r=[], w=[KM])
        for hp in range(2):
            prs = slice(hp * 64, (hp + 1) * 64)
            SC.op("act", lambda e, prs=prs, hp=hp: e.activation(
                fap(Atab[prs, hp:hp + 1], [[64, 2], [4, 16], [2, 2]]),
                fap(psb[bC][prs, 128 + hp:129 + hp], [[64, 2], [4, 16], [2, 2]]), AF.Copy), r=[kPS(bC)], w=[KM])
        SC.op("dve", lambda e: e.memset(Sst, 0.0), r=[], w=[("Sst", 0), ("Sst", 1)])

        prefetch(("wo", l, 2, 0), d_woB[l], 2048)
        for fb_ in range(2):
            prefetch(("wg", 1, l, fb_, 0), d_wg[1][l, fb_], 2048)
            prefetch(("wu", 1, l, fb_, 0), d_wu[1][l, fb_], 2048)
        SC.op("dve", lambda e: e.memset(Qblk, 0.0), r=[], w=[("qblk", 0), ("qblk", 1)])
        units = []
        for i in range(16):
            units += [(0, i), (1, 15 - i)]
        UC = [dict() for _ in units]

        def stA(ui):
            dr, c = units[ui]
            cx = UC[ui]
            qb = ui % 2
            kqb = ("qblk", qb)
            for hp in range(2):
                prs = slice(hp * 64, (hp + 1) * 64)
                SC.op("act", lambda e: e.activation(
                    fap(Qblk[prs, qb, hp * 128:hp * 128 + 1], [[256, 2], [1, 128]]),
                    fap(qTm[prs, 0, c * 128:c * 128 + 1], [[S, 2], [1, 128]]), AF.Copy),
                    r=[kQM(0, c // 4), kQM(1, c // 4)], w=[kqb])
            bq = ps_next()
            cx["bq"] = bq
            for ch in range(2):
                SC.op("pe", lambda e: e.matmul(
                    psb[bq][:, ch * 256:(ch + 1) * 256], kTm[:, ch, c * 128:(c + 1) * 128],
                    Qblk[:, qb, ch * 256:(ch + 1) * 256], start=True, stop=True),
                    r=[kKM(ch, c // 4), kqb], w=[kPS(bq)])

        def stB(ui):
            dr, c = units[ui]
            cx = UC[ui]
            col = dr * 64 + c * 4
            msk = triF_b if dr == 0 else triB_b
            tp, kp = tmp_next()
            pT = tmp_bf(tp)[:, 0:512]
            bq = cx["bq"]
            SC.op("dve", lambda e: e.tensor_tensor(
                fap(pT, [[128, 4], [1, 128]]), fap(psb[bq][:, :], [[128, 4], [1, 128]]),
                fap(msk, [[0, 4], [1, 128]]), ALU.mult), r=[kPS(bq), "cbf"], w=[kp])
            tv, kv_ = tmp_next()
            vwf = tmp_bf(tv)[:, 0:260]
            SC.op("pool", lambda e: e.tensor_tensor(fap(vwf, [[65, 4], [1, 65]]), vaug[:, c, :, :],
                                                    fap(Et[:, col:col + 4], [[1, 4], [0, 65]]),
                                                    ALU.mult), r=[kVA(c), KM], w=[kv_])
            cx.update(pT=pT, kp=kp, vwf=vwf, kv=kv_)

        def stC(ui):
            dr, c = units[ui]
            cx = UC[ui]
            col = dr * 64 + c * 4
            pT, kp, vwf, kv_ = cx["pT"], cx["kp"], cx["vwf"], cx["kv"]
            Sd = Sst[:, dr, :]
            Sb = Sbf[:, dr, :]
            SC.op("pool", lambda e: e.tensor_tensor(fap(Sd, [[65, 4], [1, 65]]), fap(Sd, [[65, 4], [1, 65]]),
                                                    fap(Atab[:, col:col + 4], [[1, 4], [0, 65]]),
                                                    ALU.mult), r=[("Sst", dr), KM], w=[("Sst", dr)])
            SC.op("act", lambda e: e.activation(Sb, Sd, AF.Copy), r=[("Sst", dr)], w=[("Sbf", dr)])
            bn = ps_next()
            for ch in range(2):
                SC.op("pe", lambda e: e.matmul(
                    psb[bn][:, ch * 130:(ch + 1) * 130], qTm[:, ch, c * 128:(c + 1) * 128], Sbf[:, dr, ch * 130:(ch + 1) * 130],
                    start=True, stop=False), r=[kQM(ch, c // 4), ("Sbf", dr)], w=[kPS(bn)])
                for hp in range(2):
                    h = 2 * ch + hp
                    SC.op("pe", lambda e: e.matmul(
                        psb[bn][:, h * 65:(h + 1) * 65], pT[:, h * 128:(h + 1) * 128], vwf[:, h * 65:(h + 1) * 65],
                        start=False, stop=(hp == 1)), r=[kp, kv_], w=[kPS(bn)])
            bs = ps_next()
            for ch in range(2):
                SC.op("pe", lambda e: e.matmul(
                    psb[bs][:, ch * 130:(ch + 1) * 130], km[:, c, ch * 128:(ch + 1) * 128], vwf[:, ch * 130:(ch + 1) * 130],
                    start=True, stop=True), r=[kKT(c), kv_], w=[kPS(bs)])
            cx.update(bn=bn, bs=bs)

        def stD(ui):
            dr, c = units[ui]
            cx = UC[ui]
            col = dr * 64 + c * 4
            bn, bs = cx["bn"], cx["bs"]
            Sd = Sst[:, dr, :]
            SC.op("dve", lambda e: e.tensor_tensor(Sd, Sd, psb[bs][:, 0:260], ALU.add),
                  r=[("Sst", dr), kPS(bs)], w=[("Sst", dr)])
            kd_ = ("small", "den", dr)
            den = small[:, 144 + dr * 4: 148 + dr * 4]
            SC.op("act", lambda e: e.activation(den, fap(psb[bn][:, 64:65], [[65, 4]]), AF.Abs), r=[kPS(bn)], w=[kd_])
            SC.op("dve", lambda e: e.tensor_tensor(den, den, FLt[:, col:col + 4], ALU.max), r=[kd_, KM], w=[kd_])
            SC.op("dve", lambda e: e.reciprocal(den, den), r=[kd_], w=[kd_])
            first = (dr == 0 and c < 8) or (dr == 1 and c >= 8)
            hmv = fap(hm[:, c, :], [[64, 4], [1, 64]])
            nv = fap(psb[bn][:, 0:64], [[65, 4], [1, 64]])
            rb = fap(den, [[1, 4], [0, 64]])
            if first:
                SC.op("dve", lambda e: e.tensor_tensor(hmv, nv, rb, ALU.mult), r=[kPS(bn), kd_], w=[kHM(c)])
            else:
                t3, k3 = tmp_next()
                t3v = fap(t3[:, 0:256], [[64, 4], [1, 64]])
                SC.op("dve", lambda e: e.tensor_tensor(t3v, nv, rb, ALU.mult), r=[kPS(bn), kd_], w=[k3])
                SC.op("pool", lambda e: e.tensor_tensor(hm[:, c, :], hm[:, c, :], t3[:, 0:256], ALU.add),
                      r=[k3, kHM(c)], w=[kHM(c)])
                finish(c)

        def finish(c):
            t1, k1 = tmp_next()
            sq = t1[:, 0:256]
            SC.op("act", lambda e: e.activation(sq, hm[:, c, :], AF.Square), r=[kHM(c)], w=[k1])
            kf_ = ("small", "fin")
            ms = small[:, 152:156]
            SC.op("dve", lambda e: e.tensor_reduce(ms, fap(sq, [[64, 4], [1, 64]]), AX.X, ALU.add), r=[k1], w=[kf_])
            SC.op("act", lambda e: e.activation(ms, ms, AF.Ln, bias=EPS, scale=1.0 / 64), r=[kf_], w=[kf_])
            SC.op("act", lambda e: e.activation(ms, ms, AF.Exp, scale=-0.5), r=[kf_], w=[kf_])
            yv = t1[:, 256:512]
            SC.op("pool", lambda e: e.tensor_tensor(fap(yv, [[64, 4], [1, 64]]), fap(hm[:, c, :], [[64, 4], [1, 64]]),
                                                    fap(ms, [[1, 4], [0, 64]]), ALU.mult), r=[kHM(c), kf_, k1], w=[k1])
            t2, k2 = tmp_next()
            yb = tmp_bf(t2)[:, 0:256]
            SC.op("pool", lambda e: e.tensor_tensor(yb, yv, GO[:, c, :], ALU.mult), r=[k1, kGO(c)], w=[k2])
            bt_ = ps_next()
            pbf = psb[bt_][:, :].bitcast(BF16)
            for ch in range(2):
                SC.op("pe", lambda e, ch=ch: e.transpose(pbf[:, ch * 128:(ch + 1) * 128], yb[:, ch * 128:(ch + 1) * 128],
                                                         ident_b), r=[k2, "cbf"], w=[kPS(bt_)])
            SC.op("act", lambda e: e.activation(fap(ymT[:, 0, c * 128:(c + 1) * 128], [[S, 2], [1, 128]]),
                                                fap(pbf[:, 0:256], [[128, 2], [1, 128]]), AF.Copy),
                  r=[kPS(bt_)], w=[kYM(0, c // 4), kYM(1, c // 4)])

        nU = len(units)
        for step in range(nU + 3):
            if step < nU:
                stA(step)
            if 0 <= step - 1 < nU:
                stB(step - 1)
            if 0 <= step - 2 < nU:
                stC(step - 2)
            if 0 <= step - 3 < nU:
                stD(step - 3)
        if stop_after == "mlstm":
            return
        wout_pass(l, [4, 5], lambda mc, bt: (ymT[:, mc - 4, bt * 512:(bt + 1) * 512], kYM(mc - 4, bt)),
                  lambda blk: d_woB[l], 8, bt_hook=bt_hook)
        SC.adopt(ALLH, [kHM(t_) for t_ in range(16)] + [kYM(ch, b_) for ch in range(2) for b_ in range(4)] + ["rowt", ("qblk", 0), ("qblk", 1)])
        SC.adopt(ACTKEYS, newk)

    Gs = nc.alloc_sbuf_tensor("Gs", [128, 16, 16], F32)

    stored = set()

    def store_out(hf):
        stored.add(hf)
        for c in range(8):
            SC.dma("sp", d_out[c * 128:(c + 1) * 128, hf * 1024:(hf + 1) * 1024], xT[:, c, hf * 1024:(hf + 1) * 1024],
                   r=[kX(c, 2 * hf), kX(c, 2 * hf + 1)], w=[], slot="o%d_%d" % (c, hf), final=True)

    build_rope()
    if stop_after is not None:
        for l in layers:
            rms_norm(l, 0, "h")
            ffn(l, 0)
            if stop_after == "ffn1":
                break
            rms_norm(l, 1, "h")
            mixer(l)
            if stop_after != "ffn2":
                break
            rms_norm(l, 2, "h")
            ffn(l, 1)
            break
    else:
        rms_norm(layers[0], 0, "h", (0, 1))
        carry = [lambda l0=layers[0]: rms_norm(l0, 0, "h", (2, 3))]
        for li, l in enumerate(layers):
            last = li == len(layers) - 1
            hookA = carry[0]
            ffn(l, 0, hooks=(hookA, lambda l=l: rms_norm(l, 1, "h", (0, 1))))
            rms_norm(l, 1, "h", (2, 3))
            mixer(l)
            rms_norm(l, 2, "h", (0, 1))

            def hookB(l=l, last=last, li=li):
                rms_norm(l, 3, "x", (0, 1))
                if not last:
                    rms_norm(layers[li + 1], 0, "h", (0, 1))
                elif not dumps:
                    store_out(0)
            ffn(l, 1, hooks=(lambda l=l: rms_norm(l, 2, "h", (2, 3)), hookB))
            if last:
                rms_norm(l, 3, "x", (2, 3))
            else:
                def nxt(l=l, li=li):
                    rms_norm(l, 3, "x", (2, 3))
                    rms_norm(layers[li + 1], 0, "h", (2, 3))
                carry[0] = nxt

    dump_aps = {}
    if dumps:
        avail = {"yatT": (yatT, [128, 4, S], BF16), "ysgT": (ysgT, [128, 2, S], BF16), "ymT": (ymT, [128, 2, S], BF16),
                 "hm": (hm, [128, 16, 256], F32), "qTa": (qTa, [128, 4, S], BF16), "kTa": (kTa, [128, S], BF16),
                 "hT": (hT, [128, 8, S], BF16), "Et": (Et, [128, 128], F32), "FLt": (FLt, [128, 128], F32),
                 "Ut": (Ut, [128, 128], F32), "Atab": (Atab, [128, 128], F32), "ropeC": (ropeC[:, :], [128, S], BF16),
                 "ropeS": (ropeS[:, :], [128, S], BF16), "va": (va, [128, 16, 128], BF16), "Gs": (Gs[:, :, :], [128, 16, 16], F32),
                 "uT": (uT, [128, 2, S], BF16), "vgn": (vgn, [128, 16, 256], BF16)}
        for nm in dumps:
            apv, shp, dt = avail[nm]
            dd = dram("dump_" + nm, shp, dt, out=True)
            allkeys = list(SC.lastw.keys())
            SC.dma("sp", dd, apv, r=allkeys, w=[], slot="dump_" + nm, final=True)

    for hf in range(2):
        if hf in stored:
            continue
        store_out(hf)
    SC.finalize()
    return nc


def _consts():
    ct = np.zeros((128, NCT), np.float32)
    i = np.arange(128)
    ct[:, C_ID:C_ID + 128] = np.eye(128, dtype=np.float32)
    ct[:, C_TRIF:C_TRIF + 128] = (i[:, None] <= i[None, :]).astype(np.float32)
    ct[:, C_TRIB:C_TRIB + 128] = (i[:, None] >= i[None, :]).astype(np.float32)
    ct[:, C_ONES:C_ONES + 128] = 1.0
    ct[:, C_BD:C_BD + 128] = ((i[:, None] // 64) == (i[None, :] // 64)).astype(np.float32)
    partner = np.where((i % 64) < 32, i + 32, i - 32)
    perm = np.zeros((128, 128), np.float32)
    perm[partner, i] = 1.0
    ct[:, C_PERM:C_PERM + 128] = perm
    j = (i % 64) % 32
    ct[:, C_FREQ] = (10000.0 ** (-(2.0 * j.astype(np.float32)) / np.float32(64.0))).astype(np.float32)
    ct[:, C_SIGN] = np.where((i % 64) < 32, -1.0, 1.0)
    return ct


def _freqs_like_reference():
    d = 64
    ar = np.arange(0, d, 2, dtype=np.float32)
    return (np.float32(10000.0) ** (-ar / np.float32(d))).astype(np.float32)


def _prep_shared(inp):
    L = inp["w_in"].shape[0]
    f32 = np.float32
    ct = _consts()
    fr = _freqs_like_reference()
    i = np.arange(128)
    ct[:, C_FREQ] = fr[(i % 64) % 32]
    pt = np.zeros((128, L * NPT), f32)
    names = ["norm_ffn1_g", "norm_mix_g", "norm_ffn2_g", "norm_out_g"]
    for l in range(L):
        o = l * NPT
        for k, nm in enumerate(names):
            pt[:, o + PT_GN + k * 8: o + PT_GN + (k + 1) * 8] = inp[nm][l].reshape(8, 128).T
        pt[:, o + PT_GQ] = inp["q_norm_g"][l][i % 64]
        pt[:, o + PT_GK] = inp["k_norm_g"][l][i % 64]
        sk = inp["attn_sink"][l]
        for c in range(4):
            pt[:64, o + PT_SINK + c] = sk[c]
            pt[64:, o + PT_SINK + c] = sk[4 + c]
        pt[:, o + PT_GB:o + PT_GB + 16] = inp["mlstm_gate_b"][l].reshape(1, 16)
        pt[:, o + PT_HG:o + PT_HG + 256] = inp["mlstm_head_g"][l][None, :]
        pt[:, o + PT_SG:o + PT_SG + 256] = inp["sgu_norm_g"][l][None, :]
        sb = inp["sgu_b"][l]
        for gp in range(2):
            pt[:64, o + PT_SB + gp * 128: o + PT_SB + (gp + 1) * 128] = sb[2 * gp][None, :]
            pt[64:, o + PT_SB + gp * 128: o + PT_SB + (gp + 1) * 128] = sb[2 * gp + 1][None, :]
    wst = np.ascontiguousarray(np.transpose(inp["sgu_w_s"], (0, 3, 1, 2)).reshape(L, 128, 512))

    def kblocks(W, nb):
        Lk, K, N = W.shape
        return np.ascontiguousarray(
            W.reshape(Lk, K // 128, 128, N // nb, nb).transpose(0, 3, 2, 1, 4).reshape(Lk, N // nb, 128, (K // 128) * nb))

    out = {"ctab": ct, "ptab": pt, "wst": wst}
    for k, pre in enumerate(("ffn1", "ffn2")):
        out["wg%d" % k] = kblocks(inp[pre + "_w_gate"], 256)
        out["wu%d" % k] = kblocks(inp[pre + "_w_up"], 256)
        wd = inp[pre + "_w_down"]
        wdr = wd.reshape(L, 2, 11, 128, 8, 128).transpose(0, 4, 1, 3, 2, 5)
        out["wd%d" % k] = np.ascontiguousarray(wdr.reshape(L, 16, 128, 1408))
    w_in = inp["w_in"]
    colsF = []
    for c in range(4):
        colsF.append(np.concatenate([np.arange(c * 64, c * 64 + 64), np.arange((4 + c) * 64, (4 + c) * 64 + 64)]))
    colsF.append(np.arange(512, 640))
    for ch in range(2):
        colsF.append(np.arange(768 + ch * 128, 768 + ch * 128 + 128))
    for ch in range(2):
        colsF.append(np.arange(1024 + ch * 128, 1024 + ch * 128 + 128))
    for gp in range(2):
        colsF.append(np.arange(1808 + gp * 128, 1808 + gp * 128 + 128))
    winF = np.stack([w_in[:, :, cols] for cols in colsF], axis=1)
    out["winF"] = np.ascontiguousarray(
        winF.reshape(L, 11, 8, 128, 128).transpose(0, 1, 3, 2, 4).reshape(L, 11, 128, 1024))
    cols0 = np.concatenate([np.arange(640, 768), np.arange(1792, 1808)])
    w0 = w_in[:, :, cols0]
    out["winT0"] = np.ascontiguousarray(w0.reshape(L, 8, 128, 144).transpose(0, 2, 1, 3).reshape(L, 128, 8 * 144))
    tb = [np.arange(2064, 2320), np.arange(1024, 1280), np.arange(1280, 1536), np.arange(1536, 1792)]
    wT = np.stack([w_in[:, :, cols] for cols in tb], axis=1)
    out["winT"] = np.ascontiguousarray(wT.reshape(L, 4, 8, 128, 256).transpose(0, 1, 3, 2, 4).reshape(L, 4, 128, 2048))
    w_out = inp["w_out"]
    p = np.arange(128)
    rows = []
    for c in range(4):
        rows.append(np.where(p < 64, c * 64 + p, (4 + c) * 64 + (p - 64)))
    for ch in range(2):
        rows.append(512 + ch * 128 + p)
    for gp in range(2):
        rows.append(768 + gp * 128 + p)
    woP = np.stack([w_out[:, r, :] for r in rows], axis=1)
    woP = woP.reshape(L, 8, 128, 8, 128)
    a = woP[:, [0, 1, 2, 3, 6, 7]]
    a = a.transpose(0, 3, 2, 1, 4)
    out["woA"] = np.ascontiguousarray(a.reshape(L, 4, 2, 128, 6, 128).transpose(0, 1, 3, 2, 4, 5).reshape(L, 4, 128, 1536))
    b = woP[:, [4, 5]].transpose(0, 2, 3, 1, 4)
    out["woB"] = np.ascontiguousarray(b.reshape(L, 128, 2048))
    return {k: np.ascontiguousarray(v, dtype=np.float32) for k, v in out.items()}


_CACHE = {}


def _get_program(key, layers, L, **kw):
    if key not in _CACHE:
        _CACHE[key] = build_program(layers, L, **kw)
    return _CACHE[key]


FUSED = True


def kernel(**inputs):
    inp = {k: np.asarray(v) for k, v in inputs.items()}
    L = inp["w_in"].shape[0]
    shared = _prep_shared(inp)
    x = inp["x"].astype(np.float32, copy=False)
    pos = inp["positions"].astype(np.int32, copy=False)
    xT = [np.ascontiguousarray(x[b].T) for b in range(NCORES)]
    if FUSED:
        nc = _get_program("fused", list(range(L)), L)
        in_maps = []
        for b in range(NCORES):
            m = dict(shared)
            m["xT"] = xT[b]
            m["pos"] = np.ascontiguousarray(pos[b:b + 1])
            in_maps.append(m)
        res = run_bass_kernel_spmd(nc, in_maps, core_ids=list(range(NCORES)))
        outs = [res.results[b]["outT"] for b in range(NCORES)]
    else:
        cur = xT
        for l in range(L):
            nc = _get_program("layer%d" % l, [l], L)
            in_maps = []
            for b in range(NCORES):
                m = dict(shared)
                m["xT"] = cur[b]
                m["pos"] = np.ascontiguousarray(pos[b:b + 1])
                in_maps.append(m)
            res = run_bass_kernel_spmd(nc, in_maps, core_ids=list(range(NCORES)))
            cur = [np.ascontiguousarray(res.results[b]["outT"]) for b in range(NCORES)]
        outs = cur
    return np.stack([np.ascontiguousarray(o.T) for o in outs], axis=0).astype(np.float32)
```
